# Optimizing a Trainium2 kernel written in Bass

```python
import math
import jax, jax.numpy as jnp
from jax import lax
import numpy as np

D_MODEL = 1024
BATCH = 8
SEQ = 8192
DEPTH = 1
DEC_BATCH = 8
DEC_SEQ = 64
PAST_LEN = 4096

CHUNK = 64
BAND_CHUNKS = 8
BAND_ROWS = BAND_CHUNKS * CHUNK
D_RNN = D_MODEL
N_LRU_BLOCKS = 8
LRU_BLOCK = D_RNN // N_LRU_BLOCKS
CONV_W = 4
LRU_C = 8.0
N_HEADS = 8
HEAD_DIM = D_MODEL // N_HEADS
D_ATT = N_HEADS * HEAD_DIM
MAX_REL = 128
ATT_SCALE = HEAD_DIM ** -0.5
NEG_INF = -1e30
D_FF = 4 * D_MODEL
ALPHA = (2 * DEPTH) ** 0.25
BETA = (8 * DEPTH) ** -0.25
LN_EPS = 1e-5
D_IN = 2 * D_RNN + 3 * D_ATT + 2 * D_MODEL
IN_SPLITS = (D_RNN, 2 * D_RNN, 2 * D_RNN + D_ATT, 2 * D_RNN + 2 * D_ATT, 2 * D_RNN + 3 * D_ATT, 2 * D_RNN + 3 * D_ATT + D_MODEL)

kernel_name = 'hawk_chunkband_deepnorm_adaln_step'


def _ln(x, g, b):
    xf = x.astype(jnp.float32)
    mu = jnp.mean(xf, axis=-1, keepdims=True)
    xc = xf - mu
    var = jnp.mean(jnp.square(xc), axis=-1, keepdims=True)
    y = xc * lax.rsqrt(var + LN_EPS) * g.astype(jnp.float32) + b.astype(jnp.float32)
    return y.astype(x.dtype)


def _causal_conv(xr, hist, w, b):
    t = xr.shape[1]
    xp = jnp.concatenate([hist.astype(xr.dtype), xr], axis=1)
    y = b + xp[:, 0:t] * w[0]
    for j in range(1, CONV_W):
        y = y + xp[:, j:j + t] * w[j]
    return y, xp[:, -(CONV_W - 1):]


def _lin_comb(e1, e2):
    a1, b1 = e1
    a2, b2 = e2
    return a1 * a2, a2 * b1 + b2


def _rg_lru(xc, h0, w_rg, b_rg, w_ig, b_ig, lam, reset_first):
    bsz, t, _ = xc.shape
    f32 = jnp.float32
    xf = xc.astype(f32)
    xb = xf.reshape(bsz, t, N_LRU_BLOCKS, LRU_BLOCK)
    r = jax.nn.sigmoid(jnp.einsum('btnc,ncd->btnd', xb, w_rg.astype(f32)) + b_rg.astype(f32)).reshape(bsz, t, D_RNN)
    i = jax.nn.sigmoid(jnp.einsum('btnc,ncd->btnd', xb, w_ig.astype(f32)) + b_ig.astype(f32)).reshape(bsz, t, D_RNN)
    log_a = -LRU_C * r * jax.nn.softplus(-lam.astype(f32))
    a = jnp.exp(log_a)
    mult = jnp.sqrt(-jnp.expm1(2.0 * log_a))
    if reset_first:
        mult = mult.at[:, 0].set(1.0)
        a = a.at[:, 0].set(0.0)
    bterm = mult * (i * xf)
    bterm = bterm.at[:, 0].add(a[:, 0] * h0.astype(f32))
    _, h = lax.associative_scan(_lin_comb, (a, bterm), axis=1)
    return h, h[:, -1]


def _rel_bias(table, rel):
    return table[:, jnp.clip(rel, -MAX_REL, MAX_REL) + MAX_REL].astype(jnp.float32)


def _attend(q, k, v, bias, valid):
    s = jnp.einsum('bqhd,bkhd->bhqk', q, k).astype(jnp.float32) * ATT_SCALE + bias
    if valid is not None:
        s = jnp.where(valid, s, NEG_INF)
    p = jax.nn.softmax(s, axis=-1)
    return jnp.einsum('bhqk,bkhd->bqhd', p.astype(v.dtype), v)


def _prompt_band_attn(q, k, v, table):
    bsz, t = q.shape[0], q.shape[1]
    n_chunks = t // CHUNK
    pad = ((0, 0), (BAND_ROWS, 0), (0, 0), (0, 0))
    kp = jnp.pad(k, pad)
    vp = jnp.pad(v, pad)
    qi = jnp.arange(CHUNK)
    ki = jnp.arange(BAND_ROWS + CHUNK)
    bias = _rel_bias(table, qi[:, None] + BAND_ROWS - ki[None, :])

    def one_chunk(j):
        start = j * CHUNK
        qj = lax.dynamic_slice_in_dim(q, start, CHUNK, axis=1)
        kj = lax.dynamic_slice_in_dim(kp, start, BAND_ROWS + CHUNK, axis=1)
        vj = lax.dynamic_slice_in_dim(vp, start, BAND_ROWS + CHUNK, axis=1)
        valid = (ki >= BAND_ROWS - start)[None, :]
        return _attend(qj, kj, vj, bias, valid)

    o = lax.map(one_chunk, jnp.arange(n_chunks))
    o = jnp.moveaxis(o, 0, 1).reshape(bsz, t, D_ATT)
    rows = min(BAND_ROWS, t)
    return o, k[:, t - rows:], v[:, t - rows:]


def _sample_band_attn(q, k, v, k_past, v_past, table):
    bsz, s = q.shape[0], q.shape[1]
    n_past = k_past.shape[1]
    kk = jnp.concatenate([k_past.astype(k.dtype), k], axis=1)
    vv = jnp.concatenate([v_past.astype(v.dtype), v], axis=1)
    kpos = jnp.arange(n_past + s) - n_past
    bias = _rel_bias(table, jnp.arange(s)[:, None] - kpos[None, :])
    o = _attend(q, kk, vv, bias, None)
    return o.reshape(bsz, s, D_ATT), k, v


def _layer(x, c, conv_hist, h0, k_past, v_past, reset_first, w):
    bsz, t, _ = x.shape
    mod = jax.nn.silu(c) @ w['w_ada'] + w['b_ada']
    sh1, sc1, g1, sh2, sc2, g2 = jnp.split(mod[:, None, :], 6, axis=-1)
    u = x * (1.0 + sc1) + sh1
    xr, gl, q, k, v, ga, gb = jnp.split(u @ w['w_in'], IN_SPLITS, axis=-1)
    xc, conv_new = _causal_conv(xr, conv_hist, w['conv_w'], w['conv_b'])
    h, h_last = _rg_lru(xc, h0, w['w_rg'], w['b_rg'], w['w_ig'], w['b_ig'], w['lru_lambda'], reset_first)
    y_a = h.astype(x.dtype) * jax.nn.gelu(gl)
    heads = (bsz, t, N_HEADS, HEAD_DIM)
    q, k, v = q.reshape(heads), k.reshape(heads), v.reshape(heads)
    if k_past is None:
        y_b, k_new, v_new = _prompt_band_attn(q, k, v, w['rel_bias'])
    else:
        y_b, k_new, v_new = _sample_band_attn(q, k, v, k_past, v_past, w['rel_bias'])
    merged = jax.nn.sigmoid(ga) * y_a + jax.nn.sigmoid(gb) * y_b
    x1 = _ln(ALPHA * x + (1.0 + g1) * (merged @ w['w_out']), w['ln1_g'], w['ln1_b'])
    u2 = x1 * (1.0 + sc2) + sh2
    f = jnp.square(jax.nn.relu(u2 @ w['w_up'] + w['b_up'])) @ w['w_down'] + w['b_down']
    y = _ln(ALPHA * x1 + (1.0 + g2) * f, w['ln2_g'], w['ln2_b'])
    return y, k_new, v_new, conv_new, h_last.astype(x.dtype)


def setup_inputs(seed: int = 0) -> dict:
    key = jax.random.key(seed)
    ks = jax.random.split(key, 32)

    def nrm(k, shape, s):
        return s * jax.random.normal(k, shape, jnp.float32)

    L = DEPTH
    cache_rows = min(BAND_ROWS, PAST_LEN)
    a0 = jax.random.uniform(ks[31], (L, D_RNN), jnp.float32, minval=0.9, maxval=0.999)
    return {
        'x_prompt': nrm(ks[0], (BATCH, SEQ, D_MODEL), 1.0),
        'x_sample': nrm(ks[1], (DEC_BATCH, DEC_SEQ, D_MODEL), 1.0),
        'c_prompt': nrm(ks[2], (BATCH, D_MODEL), 1.0),
        'c_sample': nrm(ks[3], (DEC_BATCH, D_MODEL), 1.0),
        'cache_k': nrm(ks[4], (L, DEC_BATCH, cache_rows, N_HEADS, HEAD_DIM), 1.0),
        'cache_v': nrm(ks[5], (L, DEC_BATCH, cache_rows, N_HEADS, HEAD_DIM), 1.0),
        'state_conv': nrm(ks[6], (L, DEC_BATCH, CONV_W - 1, D_RNN), 1.0),
        'state_lru': nrm(ks[7], (L, DEC_BATCH, D_RNN), 0.3),
        'w_ada': nrm(ks[8], (L, D_MODEL, 6 * D_MODEL), 0.1 * D_MODEL ** -0.5),
        'b_ada': nrm(ks[9], (L, 6 * D_MODEL), 0.01),
        'w_in': nrm(ks[10], (L, D_MODEL, D_IN), D_MODEL ** -0.5),
        'conv_w': nrm(ks[11], (L, CONV_W, D_RNN), CONV_W ** -0.5),
        'conv_b': nrm(ks[12], (L, D_RNN), 0.01),
        'w_rg': nrm(ks[13], (L, N_LRU_BLOCKS, LRU_BLOCK, LRU_BLOCK), LRU_BLOCK ** -0.5),
        'b_rg': nrm(ks[14], (L, N_LRU_BLOCKS, LRU_BLOCK), 0.01),
        'w_ig': nrm(ks[15], (L, N_LRU_BLOCKS, LRU_BLOCK, LRU_BLOCK), LRU_BLOCK ** -0.5),
        'b_ig': nrm(ks[16], (L, N_LRU_BLOCKS, LRU_BLOCK), 0.01),
        'lru_lambda': jnp.log(a0) - jnp.log1p(-a0),
        'rel_bias': nrm(ks[17], (L, N_HEADS, 2 * MAX_REL + 1), 0.2),
        'w_out': nrm(ks[18], (L, D_MODEL, D_MODEL), BETA * D_MODEL ** -0.5),
        'ln1_g': 1.0 + nrm(ks[19], (L, D_MODEL), 0.01),
        'ln1_b': nrm(ks[20], (L, D_MODEL), 0.01),
        'w_up': nrm(ks[21], (L, D_MODEL, D_FF), D_MODEL ** -0.5),
        'b_up': nrm(ks[22], (L, D_FF), 0.01),
        'w_down': nrm(ks[23], (L, D_FF, D_MODEL), BETA * D_FF ** -0.5),
        'b_down': nrm(ks[24], (L, D_MODEL), 0.01),
        'ln2_g': 1.0 + nrm(ks[25], (L, D_MODEL), 0.01),
        'ln2_b': nrm(ks[26], (L, D_MODEL), 0.01),
    }


def reference(x_prompt, x_sample, c_prompt, c_sample, cache_k, cache_v, state_conv, state_lru,
              w_ada, b_ada, w_in, conv_w, conv_b, w_rg, b_rg, w_ig, b_ig, lru_lambda, rel_bias,
              w_out, ln1_g, ln1_b, w_up, b_up, w_down, b_down, ln2_g, ln2_b):
    yp, ys = x_prompt, x_sample
    bp = x_prompt.shape[0]
    kp_l, vp_l, cp_l, hp_l = [], [], [], []
    ks_l, vs_l, cs_l, hs_l = [], [], [], []
    for l in range(DEPTH):
        w = {
            'w_ada': w_ada[l], 'b_ada': b_ada[l], 'w_in': w_in[l],
            'conv_w': conv_w[l], 'conv_b': conv_b[l],
            'w_rg': w_rg[l], 'b_rg': b_rg[l], 'w_ig': w_ig[l], 'b_ig': b_ig[l],
            'lru_lambda': lru_lambda[l], 'rel_bias': rel_bias[l], 'w_out': w_out[l],
            'ln1_g': ln1_g[l], 'ln1_b': ln1_b[l], 'w_up': w_up[l], 'b_up': b_up[l],
            'w_down': w_down[l], 'b_down': b_down[l], 'ln2_g': ln2_g[l], 'ln2_b': ln2_b[l],
        }
        conv0 = jnp.zeros((bp, CONV_W - 1, D_RNN), yp.dtype)
        h0 = jnp.zeros((bp, D_RNN), jnp.float32)
        yp, kp, vp, cp, hp = _layer(yp, c_prompt, conv0, h0, None, None, True, w)
        ys, ksn, vsn, csn, hsn = _layer(ys, c_sample, state_conv[l], state_lru[l], cache_k[l], cache_v[l], False, w)
        kp_l.append(kp); vp_l.append(vp); cp_l.append(cp); hp_l.append(hp)
        ks_l.append(ksn); vs_l.append(vsn); cs_l.append(csn); hs_l.append(hsn)
    return (yp, ys,
            jnp.stack(kp_l), jnp.stack(vp_l), jnp.stack(cp_l), jnp.stack(hp_l),
            jnp.stack(ks_l), jnp.stack(vs_l), jnp.stack(cs_l), jnp.stack(hs_l))
```

```python
import numpy as np
from contextlib import ExitStack
import concourse.bass as bass
import concourse.mybir as mybir
from concourse.bass_utils import run_bass_kernel_spmd

F32 = mybir.dt.float32
BF16 = mybir.dt.bfloat16
AF = mybir.ActivationFunctionType
ALU = mybir.AluOpType

ALPHA = 2.0 ** 0.25
ATT_SCALE = 128.0 ** -0.5
LN_EPS = 1e-5
NEG = -30000.0


class Res:
    __slots__ = ("name", "w", "r", "ro")

    def __init__(self, name, ro=False):
        self.name = name
        self.w = None
        self.r = []
        self.ro = ro


class Slot:
    __slots__ = ("sem", "count")

    def __init__(self, sem):
        self.sem = sem
        self.count = 0


class Prog:
    ENG = ("pe", "act", "dve", "pool", "sp")
    LOOK = 24
    WIN = 150.0
    LAT_TO_PE = 3000.0
    LAT_FROM_PE = 300.0
    LAT_X = 600.0

    def __init__(self, nc, es):
        self.nc = nc
        self.es = es
        self.ops = []
        self.sem = {e: es.enter_context(nc.semaphore("s_" + e)) for e in self.ENG}
        self.nslot = 0
        self.final_slots = []
        self.defn = 512

    def slot(self):
        self.nslot += 1
        return Slot(self.es.enter_context(self.nc.semaphore("d%d" % self.nslot)))

    def cost(self, eng, slot, n):
        if slot is not None:
            return float(n if n is not None else 1 << 20)
        n = self.defn if n is None else n
        if eng == "pe":
            return n / 2.4 + 10.0
        if eng == "act":
            return 230.0 + n / 1.15
        if eng == "dve":
            return 120.0 + n / 0.9
        return 300.0 + n / 0.45

    SERVED = {0: ("exp", "tanh"), 2: ("tanh", "sigmoid"), 3: ("sqrt",), 11: ("gelu", "tanh"), 5: ("ln",), 18: ("silu", "tanh")}
    LOWEST = {"exp": 0, "tanh": 0, "sigmoid": 2, "sqrt": 3, "gelu": 11, "ln": 5, "silu": 18}

    def op(self, eng, fn, reads=(), writes=(), slot=None, n=None, tbl=None, delay=0.0):
        idx = len(self.ops)
        deps = set()
        for r in reads:
            if r.w is not None:
                deps.add(r.w)
        for w in writes:
            if w.w is not None:
                deps.add(w.w)
            deps.update(w.r)
        self.ops.append((eng, fn, slot, deps, self.cost(eng, slot, n), tbl, delay))
        for r in reads:
            if not r.ro:
                r.r.append(idx)
        for w in writes:
            w.w = idx
            w.r = []
        return idx

    def fence(self, frm, to):
        toks = []
        for f in frm:
            if f.w is not None:
                toks.append(f.w)
            toks.extend(f.r)
        for t in to:
            t.r.extend(toks)

    def schedule(self):
        import bisect
        ops = self.ops
        N = len(ops)
        succ = [[] for _ in range(N)]
        indeg = [0] * N
        for i, o in enumerate(ops):
            for d in o[3]:
                succ[d].append(i)
            indeg[i] = len(o[3])
        ready = {e: [] for e in self.ENG}
        rtime = [0.0] * N
        fin = [0.0] * N
        efree = {e: 0.0 for e in self.ENG}
        order = {e: [] for e in self.ENG}
        for i in range(N):
            if indeg[i] == 0:
                ready[ops[i][0]].append(i)
        bw_free = 0.0
        left = N
        LOOK, WIN = self.LOOK, self.WIN
        cur_set = -1
        SERVED, LOWEST = self.SERVED, self.LOWEST
        while left:
            best = None
            for e in self.ENG:
                L = ready[e]
                if not L:
                    continue
                te = efree[e]
                c = None
                cr = None
                cfall = None
                for i in L[:LOOK]:
                    rt = rtime[i]
                    if rt <= te + WIN:
                        if e == "act":
                            tb = ops[i][5]
                            if tb is not None and (cur_set < 0 or tb not in SERVED[cur_set]):
                                if cfall is None:
                                    cfall = i
                                continue
                        c = i
                        break
                    if cr is None or rt < cr:
                        cr = rt
                        c2 = i
                if c is None:
                    c = cfall if cfall is not None else c2
                st = te if rtime[c] < te else rtime[c]
                if best is None or (st, c) < (best[0], best[2]):
                    best = (st, e, c)
            st, e, c = best
            L = ready[e]
            L.pop(bisect.bisect_left(L, c))
            o = ops[c]
            if o[2] is not None:
                issue = 60.0 if e == "sp" else 600.0
                efree[e] = st + issue
                b0 = bw_free if bw_free > st else st
                bw_free = b0 + o[4] / 160.0
                f = max(st + 2000.0, bw_free)
            else:
                dur = o[4]
                if e == "act" and o[5] is not None and (cur_set < 0 or o[5] not in SERVED[cur_set]):
                    cur_set = LOWEST[o[5]]
                    dur += 1283.0
                    self.n_tbl = getattr(self, "n_tbl", 0) + 1
                efree[e] = st + dur
                f = st + dur + (60.0 if e == "pe" else 0.0)
            fin[c] = f
            order[e].append(c)
            left -= 1
            for s_ in succ[c]:
                ce = ops[s_][0]
                if o[2] is not None:
                    lat = f + 200.0
                elif ce == e:
                    lat = f
                elif ce == "pe":
                    lat = f + self.LAT_TO_PE
                elif e == "pe":
                    lat = f + self.LAT_FROM_PE
                else:
                    lat = f + self.LAT_X
                if lat > rtime[s_]:
                    rtime[s_] = lat
                indeg[s_] -= 1
                if indeg[s_] == 0:
                    rtime[s_] += ops[s_][6]
                    bisect.insort(ready[ops[s_][0]], s_)
        self.est_ns = max(fin) if fin else 0.0
        return order

    def emit(self):
        nc = self.nc
        ops = self.ops
        order = self.schedule()
        tok = [None] * len(ops)
        sig = [False] * len(ops)
        for i, o in enumerate(ops):
            for d in o[3]:
                if ops[d][2] is None and not (o[0] == "pe" and ops[d][0] == "pe"):
                    sig[d] = True
        cnt = {e: 0 for e in self.ENG}
        for e in self.ENG:
            for i in order[e]:
                sl = ops[i][2]
                if sl is None:
                    if sig[i]:
                        cnt[e] += 1
                        tok[i] = (self.sem[e], cnt[e], e)
                else:
                    sl.count += 16
                    tok[i] = (sl.sem, sl.count, None)
        self.n_sig = sum(sig)

        def run(eng, name):
            waited = {}
            own = self.sem[name]
            for i in order[name]:
                o = ops[i]
                need = {}
                for d in o[3]:
                    if name == "pe" and ops[d][0] == "pe" and ops[d][2] is None:
                        continue
                    s, v, de = tok[d]
                    if need.get(s, 0) < v:
                        need[s] = v
                for s, v in need.items():
                    if waited.get(s, 0) < v:
                        waited[s] = v
                        eng.wait_ge(s, v)
                ins = o[1](eng)
                t = tok[i]
                if t is not None:
                    ins.then_inc(t[0], 16 if o[2] is not None else 1)
            if name == "sp":
                for sl in self.final_slots:
                    eng.wait_ge(sl.sem, sl.count)

        with nc.Block() as block:
            @block.tensor
            def _(e):
                run(e, "pe")

            @block.scalar
            def _(e):
                run(e, "act")

            @block.vector
            def _(e):
                run(e, "dve")

            @block.gpsimd
            def _(e):
                run(e, "pool")

            @block.sync
            def _(e):
                run(e, "sp")


V_CW, V_CB, V_BRG, V_BIG, V_LAM, V_L1G, V_L1B, V_L2G, V_L2B, V_BD, V_BUP, V_BADA, NV = \
    0, 32, 40, 48, 56, 64, 72, 80, 88, 96, 104, 136, 184
C_C, C_LRU, C_CONV, NCV = 0, 16, 24, 48


def build(NT):
    nc = bass.Bass("TRN2", target_bir_lowering=False)
    SEQ = NT * 512

    def din(name, shape, dt=F32):
        return nc.dram_tensor(name, shape, dt, kind="ExternalInput").ap()

    def dout(name, shape):
        return nc.dram_tensor(name, shape, F32, kind="ExternalOutput").ap()

    xp = din("xp", [SEQ, 1024]); xs = din("xs", [64, 1024])
    ck = din("ck", [512, 1024]); cv = din("cv", [512, 1024])
    vecs_d = din("vecs", [128, NV]); cvec_d = din("cvec", [128, NCV])
    biasM_d = din("biasM", [128, 8 * 3 * 128]); cbias_d = din("cbias", [128, 8])
    w_ada = din("w_ada", [1024, 6144]); w_in = din("w_in", [1024, 7168])
    w_rg = din("w_rg", [8, 128, 128]); w_ig = din("w_ig", [8, 128, 128])
    w_out = din("w_out", [1024, 1024]); w_up = din("w_up", [1024, 4096]); w_down = din("w_down", [4096, 1024])
    yp = dout("yp", [SEQ, 1024]); ys = dout("ys", [64, 1024])
    nkp = dout("nkp", [512, 1024]); nvp = dout("nvp", [512, 1024])
    ncp = dout("ncp", [128, 24]); nhp = dout("nhp", [128, 8])
    nks = dout("nks", [64, 1024]); nvs = dout("nvs", [64, 1024])
    ncs = dout("ncs", [128, 24]); nhs = dout("nhs", [128, 8])
    win_b = nc.dram_tensor("win_b", [14, 128, 8, 512], BF16, kind="Internal").ap()
    wout_b = nc.dram_tensor("wout_b", [2, 128, 8, 512], BF16, kind="Internal").ap()
    wup_b = nc.dram_tensor("wup_b", [8, 128, 8, 512], BF16, kind="Internal").ap()
    wdn_b = nc.dram_tensor("wdn_b", [8, 128, 32, 128], BF16, kind="Internal").ap()

    with ExitStack() as es:
        P = Prog(nc, es)

        def sb(name, shape, dt=F32):
            return nc.alloc_sbuf_tensor(name, shape, dt).ap()

        xst = sb("xst", [128, 2, 1024]); Rxst = [Res("xst0"), Res("xst1")]; Sxst = [P.slot(), P.slot()]
        ost = sb("ost", [128, 2, 1024]); Rost = [Res("ost0"), Res("ost1")]; Sost = [P.slot(), P.slot()]
        resid = sb("resid", [128, 8, 512]); Rres = [Res("res%d" % i) for i in range(8)]
        uT = sb("uT", [128, 8, 512], BF16); RuT = [Res("uT%d" % i) for i in range(8)]
        u2T = sb("u2T", [128, 8, 512], BF16); Ru2 = [Res("u2T%d" % i) for i in range(8)]
        big = sb("big", [128, 16384], BF16)
        hT = big.rearrange("p (m t) -> p m t", t=512); RhT = [Res("hT%d" % i) for i in range(32)]
        kTr = sb("kTr", [128, 8, 1024], BF16); RkT = [Res("kT%d" % i) for i in range(8)]
        Vr = sb("Vr", [128, 8, 8, 129], BF16); RV = [Res("V%d" % i) for i in range(8)]
        biasM = sb("biasM_s", [128, 8, 3, 128]); cbias = sb("cbias_s", [128, 8])
        pT = sb("pT", [128, 2, 5, 128], BF16); RpT = [Res("pT0"), Res("pT1")]
        sbm = sb("sbm", [128, 2, 3, 128]); Rsbm = [Res("sbm0"), Res("sbm1")]
        osb = sb("osb", [128, 2, 128], BF16); Rosb = [Res("osb0"), Res("osb1")]
        rc = sb("rc", [128, 2]); Rrc = [Res("rc0"), Res("rc1")]
        mtmp = sb("mtmp", [128, 2, 128]); Rmt = [Res("mt0"), Res("mt1")]
        xb = sb("xb", [128, 2, 512], BF16); Rxb = [Res("xb0"), Res("xb1")]
        sq = sb("sq", [128, 2, 512], BF16); Rsq = [Res("sq0"), Res("sq1")]
        lnm = sb("lnm", [128, 512]); lnv = sb("lnv", [128, 512]); lnn = sb("lnn", [128, 512])
        lnt = sb("lnt", [128, 2, 512]); Rlnt = [Res("lnt0"), Res("lnt1")]
        Rlnm, Rlnv, Rlnn = Res("lnm"), Res("lnv"), Res("lnn")
        Rln2m = Res("ln2m")
        tg = sb("tg", [128, 2, 4, 512]); Rtg = [[Res("tg%d_%d" % (p_, i)) for i in range(4)] for p_ in range(2)]
        tgs = sb("tgs", [128, 2, 512]); Rtgs = [Res("tgs0"), Res("tgs1")]
        xcb = sb("xcb", [128, 2, 512], BF16); Rxcb = [Res("xcb0"), Res("xcb1")]
        xrb = sb("xrb", [128, 2, 515]); Rxrb = [Res("xrb0"), Res("xrb1")]
        hist = sb("hist", [128, 8, 3]); Rhist = [Res("hist%d" % i) for i in range(8)]
        hst = sb("hst", [128, 8]); Rhst = [Res("hst%d" % i) for i in range(8)]
        NWB = 3
        wbuf = sb("wbuf", [128, NWB, 8, 512], BF16); Rwb = [Res("wb%d" % i) for i in range(NWB)]; Swb = [P.slot() for _ in range(NWB)]; Swbp = [P.slot() for _ in range(NWB)]
        wrg = sb("wrg", [128, 8, 128], BF16); wig = sb("wig", [128, 8, 128], BF16); Rwg = Res("wg"); Swg = P.slot()
        vecs = sb("vecs_s", [128, NV]); cvec = sb("cvec_s", [128, NCV]); Rvec = Res("vec"); Svec = P.slot()
        mod = sb("mod", [128, 48, 2]); Rmod = Res("mod")
        cst = sb("cst", [128, 136]); Rcst = Res("cst")
        csil = sb("csil", [128, 8, 2]); Rcsil = Res("csil")
        identF = sb("identF", [128, 128]); identB = sb("identB", [128, 128], BF16); onesS = sb("onesS", [128, 128], BF16)
        Rid = Res("ident")

        def pb(name, dt=F32, n=512):
            return nc.alloc_psum_tensor(name, [128, n], dt).ap()
        pt = [pb("pt0"), pb("pt1")]; Rpt = [Res("pt0"), Res("pt1")]
        pz = [pb("pz0"), pb("pz1")]; Rpz = [Res("pz0"), Res("pz1")]
        pa = [pb("pa0"), pb("pa1")]; Rpa = [Res("pa0"), Res("pa1")]
        po = pb("po"); Rpo = Res("po")
        pob = pb("pob", BF16, 1024); Rpob = Res("pob")

        Rs_in = [Res("s_in%d" % g) for g in range(14)]; Ss_in = [P.slot() for _ in range(14)]
        Rs_out = [Res("s_out%d" % g) for g in range(2)]; Ss_out = [P.slot() for _ in range(2)]
        Rs_up = [Res("s_up%d" % g) for g in range(8)]; Ss_up = [P.slot() for _ in range(8)]
        Rs_dn = [Res("s_dn%d" % g) for g in range(8)]; Ss_dn = [P.slot() for _ in range(8)]
        Scv = [P.slot() for _ in range(4)]
        Sout = [P.slot() for _ in range(4)]

        P.op("sp", lambda e: e.dma_start(out=vecs, in_=vecs_d), writes=[Rvec], slot=Svec)
        P.op("sp", lambda e: e.dma_start(out=cvec, in_=cvec_d), writes=[Rvec], slot=Svec)
        P.op("sp", lambda e: e.dma_start(out=biasM.rearrange("p a b c -> p (a b c)"), in_=biasM_d), writes=[Rvec], slot=Svec)
        P.op("sp", lambda e: e.dma_start(out=cbias, in_=cbias_d), writes=[Rvec], slot=Svec)
        P.op("act", lambda e: e.activation(out=biasM.rearrange("p a b c -> p (a b c)"), in_=biasM.rearrange("p a b c -> p (a b c)"), func=AF.Exp),
             writes=[Rvec], tbl="exp", n=3072)
        P.op("pool", lambda e: e.dma_start(out=wrg, in_=w_rg.rearrange("n c d -> c n d")), writes=[Rwg], slot=Swg)
        P.op("pool", lambda e: e.dma_start(out=wig, in_=w_ig.rearrange("n c d -> c n d")), writes=[Rwg], slot=Swg)
        P.op("pool", lambda e: e.memset(identF, 0.0), writes=[Rid])
        P.op("pool", lambda e: e.affine_select(out=identF, in_=identF, pattern=[[-1, 128]], compare_op=ALU.not_equal,
                                                fill=1.0, base=0, channel_multiplier=1), writes=[Rid])
        P.op("pool", lambda e: e.tensor_copy(identB, identF), writes=[Rid])
        P.op("pool", lambda e: e.memset(onesS, 1.0 / 1024.0), writes=[Rid])
        P.op("pool", lambda e: e.memset(kTr.rearrange("p a b -> p (a b)"), 0.0), writes=RkT)
        P.op("pool", lambda e: e.memset(Vr.rearrange("p a b c -> p (a b c)"), 0.0), writes=RV)
        P.op("pool", lambda e: e.memset(Vr[:, :, :, 128:129].rearrange("p a b c -> p (a b c)"), 1.0), writes=RV)
        P.op("pool", lambda e: e.memset(pT.rearrange("p a b c -> p (a b c)"), 0.0), writes=RpT)

        P.op("act", lambda e: e.activation(out=csil.rearrange("p k j -> p j k"),
                                           in_=cvec[:, C_C:C_C + 16].rearrange("p (j k) -> p j k", j=2), func=AF.Silu),
             reads=[Rvec], writes=[Rcsil], tbl="silu")
        stg = [(xst[:, 0, :], Rxst[0], Sxst[0]), (xst[:, 1, :], Rxst[1], Sxst[1]), (ost[:, 0, :], Rost[0], Sost[0]), (ost[:, 1, :], Rost[1], Sost[1])]
        for n in range(48):
            buf, Rb, Sb = stg[n % 4]
            wv = buf.rearrange("p (k c) -> p k c", k=8)
            P.op("sp", lambda e, n=n, wv=wv: e.dma_start(out=wv, in_=w_ada[:, n * 128:(n + 1) * 128].rearrange("(k p) c -> p k c", p=128)),
                 writes=[Rb], slot=Sb, n=1 << 19)
            for k in range(8):
                P.op("pe", lambda e, k=k, n=n, wv=wv: e.matmul(pa[0][:, n * 2:n * 2 + 2], wv[:, k, :], csil[:, k, :], start=(k == 0), stop=(k == 7)),
                     reads=[Rb, Rcsil], writes=[Rpa[0]], n=600)
        P.op("dve", lambda e: e.tensor_tensor(mod[:, :, 0], pa[0][:, 0:96].rearrange("p (n j) -> p n j", j=2)[:, :, 0], vecs[:, V_BADA:V_BADA + 48], ALU.add),
             reads=[Rvec], writes=[Rpa[0], Rmod])
        P.op("dve", lambda e: e.tensor_tensor(mod[:, :, 1], pa[0][:, 0:96].rearrange("p (n j) -> p n j", j=2)[:, :, 1], vecs[:, V_BADA:V_BADA + 48], ALU.add),
             reads=[Rvec], writes=[Rpa[0], Rmod])
        CL, GA = 0, 8
        def CJ(j, i):
            return 16 + j * 48 + i * 8 - 0
        P.op("act", lambda e: e.activation(out=cst[:, CL:CL + 8], in_=vecs[:, V_LAM:V_LAM + 8], func=AF.Exp, scale=-1.0), reads=[Rvec], writes=[Rcst], tbl="exp")
        P.op("act", lambda e: e.activation(out=cst[:, CL:CL + 8], in_=cst[:, CL:CL + 8], func=AF.Ln, bias=1.0, scale=1.0), writes=[Rcst], tbl="ln")
        P.op("dve", lambda e: e.tensor_scalar(cst[:, CL:CL + 8], cst[:, CL:CL + 8], -8.0, None, ALU.mult), writes=[Rcst])
        P.op("dve", lambda e: e.tensor_scalar(cst[:, GA:GA + 8], vecs[:, V_L1G:V_L1G + 8], ALPHA, None, ALU.mult), reads=[Rvec], writes=[Rcst])
        CL2, HBR, HBI = 112, 120, 128
        P.op("dve", lambda e: e.tensor_scalar(cst[:, CL2:CL2 + 8], cst[:, CL:CL + 8], 0.5, None, ALU.mult), writes=[Rcst])
        P.op("dve", lambda e: e.tensor_scalar(cst[:, HBR:HBR + 8], vecs[:, V_BRG:V_BRG + 8], 0.5, None, ALU.mult), reads=[Rvec], writes=[Rcst])
        P.op("dve", lambda e: e.tensor_scalar(cst[:, HBI:HBI + 8], vecs[:, V_BIG:V_BIG + 8], 0.5, None, ALU.mult), reads=[Rvec], writes=[Rcst])
        for j in range(2):
            def M(blk, j=j):
                return mod[:, blk * 8:(blk + 1) * 8, j]
            P.op("dve", lambda e, j=j, M=M: e.tensor_scalar(cst[:, CJ(j, 0):CJ(j, 0) + 8], M(1), 1.0, 1.0 / ALPHA, ALU.add, ALU.mult), reads=[Rmod], writes=[Rcst])
            P.op("dve", lambda e, j=j, M=M: e.tensor_scalar(cst[:, CJ(j, 1):CJ(j, 1) + 8], M(2), 1.0, 0.5, ALU.add, ALU.mult), reads=[Rmod], writes=[Rcst])
            P.op("dve", lambda e, j=j, M=M: e.tensor_scalar(cst[:, CJ(j, 4):CJ(j, 4) + 8], M(5), 1.0, None, ALU.add), reads=[Rmod], writes=[Rcst])
            P.op("dve", lambda e, j=j, M=M: e.tensor_scalar(cst[:, CJ(j, 3):CJ(j, 3) + 8], M(4), 1.0, None, ALU.add), reads=[Rmod], writes=[Rcst])
            P.op("dve", lambda e, j=j: e.tensor_tensor(cst[:, CJ(j, 2):CJ(j, 2) + 8], vecs[:, V_L1G:V_L1G + 8], cst[:, CJ(j, 3):CJ(j, 3) + 8], ALU.mult), reads=[Rvec], writes=[Rcst])
            P.op("dve", lambda e, j=j: e.tensor_tensor(cst[:, CJ(j, 3):CJ(j, 3) + 8], vecs[:, V_L1B:V_L1B + 8], cst[:, CJ(j, 3):CJ(j, 3) + 8], ALU.mult), reads=[Rvec], writes=[Rcst])
            P.op("dve", lambda e, j=j, M=M: e.tensor_tensor(cst[:, CJ(j, 3):CJ(j, 3) + 8], cst[:, CJ(j, 3):CJ(j, 3) + 8], M(3), ALU.add), reads=[Rmod], writes=[Rcst])
            P.op("dve", lambda e, j=j: e.tensor_tensor(cst[:, CJ(j, 5):CJ(j, 5) + 8], vecs[:, V_BD:V_BD + 8], cst[:, CJ(j, 4):CJ(j, 4) + 8], ALU.mult), reads=[Rvec], writes=[Rcst])
            P.op("dve", lambda e, j=j: e.scalar_tensor_tensor(cst[:, CJ(j, 5):CJ(j, 5) + 8], vecs[:, V_L1B:V_L1B + 8], ALPHA, cst[:, CJ(j, 5):CJ(j, 5) + 8], ALU.mult, ALU.add), reads=[Rvec], writes=[Rcst])

        for r_ in [Rvec, Rcst, Rmod, Rid, Rwg, Rcsil]:
            r_.ro = True
        wq = [0]

        converted = set()
        f32src = {}
        for g in range(14):
            f32src[id(Rs_in[g])] = (w_in[:, g * 512:(g + 1) * 512].rearrange("(k p) c -> p k c", p=128), Ss_in[g])
        for g in range(2):
            f32src[id(Rs_out[g])] = (w_out[:, g * 512:(g + 1) * 512].rearrange("(k p) c -> p k c", p=128), Ss_out[g])
        for g in range(8):
            f32src[id(Rs_up[g])] = (w_up[:, g * 512:(g + 1) * 512].rearrange("(k p) c -> p k c", p=128), Ss_up[g])

        def stream(src, Rsrc):
            i = wq[0] % NWB
            wq[0] += 1
            if id(Rsrc) not in converted:
                converted.add(id(Rsrc))
                fsrc, Ssrc = f32src[id(Rsrc)]
                P.op("pool", lambda e: e.dma_start(out=wbuf[:, i], in_=fsrc), writes=[Rwb[i]], slot=Swbp[i], n=2 << 20)
                P.op("sp", lambda e: e.dma_start(out=src, in_=wbuf[:, i]), reads=[Rwb[i]], writes=[Rsrc], slot=Ssrc)
            else:
                P.op("sp", lambda e: e.dma_start(out=wbuf[:, i], in_=src), reads=[Rsrc], writes=[Rwb[i]], slot=Swb[i])
            return i

        zq = [0]

        pzz = [pz[0], pz[1], pt[0], pt[1]]
        Rpzz = [Rpz[0], Rpz[1], Rpt[0], Rpt[1]]

        def nextz():
            i = zq[0] % 4
            zq[0] += 1
            return i

        Gv = big[:, 0:8192].bitcast(F32).rearrange("p (c t) -> p c t", t=512)
        sgb = big[:, 8192:12288].rearrange("p (c t) -> p c t", t=512)
        qT = big[:, 12288:16384].rearrange("p (c t) -> p c t", t=512)
        RG = [Res("G%d" % i) for i in range(8)]; Rsgb = [Res("sgb%d" % i) for i in range(8)]; RqT = [Res("qT%d" % i) for i in range(8)]

        def col(base, n):
            return cst[:, base + n:base + n + 1]

        def vcol(base, n):
            return vecs[:, base + n:base + n + 1]

        def tile(t, T, j):
            P.defn = T
            sample = (j == 1)
            xsrc = xs if sample else xp[t * 512:(t + 1) * 512, :]
            ydst = ys if sample else yp[t * 512:(t + 1) * 512, :]
            NS = max(1, T // 128)
            RW = min(T, 128)
            last = sample or (t == NT - 1)
            P.fence(RhT, RG + Rsgb + RqT)

            if sample:
                for s in range(4):
                    b_ = s % 2
                    P.op("sp", lambda e, s=s, b_=b_: e.dma_start(out=xst[:, b_, :], in_=ck[s * 128:(s + 1) * 128, :]), writes=[Rxst[b_]], slot=Sxst[b_])
                    for half in range(2):
                        pi_ = half
                        for f4 in range(4):
                            fc = half * 4 + f4
                            P.op("pe", lambda e, b_=b_, fc=fc, f4=f4, pi_=pi_: e.transpose(pt[pi_][:, f4 * 128:(f4 + 1) * 128], xst[:, b_, fc * 128:(fc + 1) * 128], identF),
                                 reads=[Rxst[b_], Rid], writes=[Rpt[pi_]])
                        P.op("act", lambda e, half=half, s=s, pi_=pi_: e.activation(out=kTr[:, half * 4:(half + 1) * 4, s * 128:(s + 1) * 128],
                                                                                   in_=pt[pi_].rearrange("p (f t) -> p f t", t=128), func=AF.Copy),
                             writes=[Rpt[pi_]] + RkT[half * 4:(half + 1) * 4])
                    P.op("pool", lambda e, s=s: e.dma_start(out=Vr[:, s, :, 0:128], in_=cv[s * 128:(s + 1) * 128, :].rearrange("p (h d) -> p h d", d=128)),
                         writes=[RV[s]], slot=Scv[s], n=1 << 19)
                P.op("pool", lambda e: e.tensor_copy(hist.rearrange("p c j -> p j c"), cvec[:, C_CONV:C_CONV + 24].rearrange("p (j c) -> p j c", j=3)),
                     reads=[Rvec], writes=Rhist)
                P.op("pool", lambda e: e.tensor_copy(hst, cvec[:, C_LRU:C_LRU + 8]), reads=[Rvec], writes=Rhst)
            elif t == 0:
                P.op("pool", lambda e: e.memset(hist.rearrange("p c j -> p (c j)"), 0.0), writes=Rhist)
                P.op("pool", lambda e: e.memset(hst, 0.0), writes=Rhst)

            for s in range(NS):
                b_ = s % 2
                P.op("sp", lambda e, s=s, b_=b_: e.dma_start(out=xst[0:RW, b_, :], in_=xsrc[s * 128:s * 128 + RW, :]), writes=[Rxst[b_]], slot=Sxst[b_])
                for half in range(2):
                    pi_ = half
                    for f4 in range(4):
                        fc = half * 4 + f4
                        P.op("pe", lambda e, b_=b_, fc=fc, f4=f4, pi_=pi_: e.transpose(pt[pi_][:, f4 * 128:f4 * 128 + RW], xst[0:RW, b_, fc * 128:(fc + 1) * 128], identF[0:RW, 0:RW]),
                             reads=[Rxst[b_], Rid], writes=[Rpt[pi_]], n=256)
                    P.op("act", lambda e, half=half, s=s, pi_=pi_: e.activation(out=resid[:, half * 4:(half + 1) * 4, s * 128:s * 128 + RW],
                                                                               in_=pt[pi_].rearrange("p (f t) -> p f t", t=128)[:, :, 0:RW], func=AF.Copy, scale=ALPHA),
                         writes=[Rpt[pi_]] + Rres[half * 4:(half + 1) * 4])
            for fc in range(8):
                P.op("dve", lambda e, fc=fc: e.tensor_scalar(uT[:, fc, 0:T], resid[:, fc, 0:T], col(CJ(j, 0), fc), mod[:, fc, j:j + 1], ALU.mult, ALU.add),
                     reads=[Rres[fc], Rcst, Rmod], writes=[RuT[fc]])

            def proj_block(wi, jb, rhs_of_k, Rrhs):
                z = nextz()
                for k in range(8):
                    P.op("pe", lambda e, k=k, z=z: e.matmul(pzz[z][:, 0:T], wbuf[:, wi, k, jb * 128:(jb + 1) * 128], rhs_of_k(k), start=(k == 0), stop=(k == 7)),
                         reads=[Rwb[wi], Rrhs[k]], writes=[Rpzz[z]])
                return z

            uk = lambda k: uT[:, k, 0:T]
            XC, RA, IB, AM = 0, 1, 2, 3

            def chainA(fc, wi, jb):
                xb_ = fc % 2
                tp_ = fc % 2
                Rt = Rtg[tp_]
                tgp = tg[:, tp_]
                z = proj_block(wi, jb, uk, RuT)
                P.op("pool", lambda e: e.tensor_copy(xrb[:, xb_, 0:3], hist[:, fc, :]), reads=[Rhist[fc]], writes=[Rxrb[xb_]], n=8)
                P.op("act", lambda e: e.activation(out=xrb[:, xb_, 3:3 + T], in_=pzz[z][:, 0:T], func=AF.Copy), writes=[Rpzz[z], Rxrb[xb_]])
                P.op("pool", lambda e: e.tensor_copy(hist[:, fc, :], xrb[:, xb_, T:T + 3]), reads=[Rxrb[xb_]], writes=[Rhist[fc]], n=8)
                P.op("dve", lambda e: e.tensor_scalar(tgp[:, XC, 0:T], xrb[:, xb_, 0:T], vcol(V_CW, fc), vcol(V_CB, fc), ALU.mult, ALU.add),
                     reads=[Rxrb[xb_], Rvec], writes=[Rt[XC]])
                for jj in (1, 2, 3):
                    P.op("dve", lambda e, jj=jj: e.scalar_tensor_tensor(tgp[:, XC, 0:T], xrb[:, xb_, jj:jj + T], vcol(V_CW + 8 * jj, fc), tgp[:, XC, 0:T], ALU.mult, ALU.add),
                         reads=[Rxrb[xb_], Rvec], writes=[Rt[XC]])
                P.op("act", lambda e: e.activation(out=xcb[:, tp_, 0:T], in_=tgp[:, XC, 0:T], func=AF.Copy), reads=[Rt[XC]], writes=[Rxcb[tp_]])
                P.op("pe", lambda e: e.matmul(pa[0][:, 0:T], wrg[:, fc, :], xcb[:, tp_, 0:T], start=True, stop=True), reads=[Rwg, Rxcb[tp_]], writes=[Rpa[0]], delay=5000.0)
                P.op("pe", lambda e: e.matmul(pa[1][:, 0:T], wig[:, fc, :], xcb[:, tp_, 0:T], start=True, stop=True), reads=[Rwg, Rxcb[tp_]], writes=[Rpa[1]])
                P.op("act", lambda e: e.activation(out=tgp[:, RA, 0:T], in_=pa[0][:, 0:T], func=AF.Tanh, bias=col(HBR, fc), scale=0.5), reads=[Rcst], writes=[Rpa[0], Rt[RA]], tbl="tanh")
                P.op("act", lambda e: e.activation(out=tgp[:, IB, 0:T], in_=pa[1][:, 0:T], func=AF.Tanh, bias=col(HBI, fc), scale=0.5), reads=[Rcst], writes=[Rpa[1], Rt[IB]], tbl="tanh")
                P.op("act", lambda e: e.activation(out=tgp[:, RA, 0:T], in_=tgp[:, RA, 0:T], func=AF.Exp, scale=col(CL2, fc), bias=col(CL2, fc)), reads=[Rcst], writes=[Rt[RA]], tbl="exp")
                P.op("pool", lambda e: e.tensor_tensor(tgp[:, AM, 0:T], tgp[:, RA, 0:T], tgp[:, RA, 0:T], ALU.mult), reads=[Rt[RA]], writes=[Rt[AM]])
                P.op("dve", lambda e: e.scalar_tensor_tensor(tgp[:, IB, 0:T], tgp[:, IB, 0:T], 1.0, tgp[:, XC, 0:T], ALU.add, ALU.mult), reads=[Rt[XC]], writes=[Rt[IB]])

            def chainB(fc):
                tp_ = fc % 2
                Rt = Rtg[tp_]
                tgp = tg[:, tp_]
                if (not sample) and t == 0:
                    P.op("pool", lambda e: e.memset(tgp[:, AM, 0:1], 0.5), writes=[Rt[AM]], n=1)
                    P.op("pool", lambda e: e.memset(tgp[:, RA, 0:1], 0.0), writes=[Rt[RA]], n=1)
                P.op("pool", lambda e: e.tensor_tensor(tgp[:, IB, 0:T], tgp[:, IB, 0:T], tgp[:, AM, 0:T], ALU.mult), reads=[Rt[AM]], writes=[Rt[IB]])
                P.op("dve", lambda e: e.tensor_tensor_scan(Gv[:, fc, 0:T], tgp[:, RA, 0:T], tgp[:, IB, 0:T], hst[:, fc:fc + 1], ALU.mult, ALU.add),
                     reads=[Rt[RA], Rt[IB], Rhst[fc]], writes=[RG[fc]], n=2 * T)
                P.op("pool", lambda e: e.tensor_copy(hst[:, fc:fc + 1], Gv[:, fc, T - 1:T]), reads=[RG[fc]], writes=[Rhst[fc]], n=1)

            def emit_xr(g, jp):
                wi = stream(win_b[g], Rs_in[g])
                if True:
                    fcs = (g * 4 + 2 * jp, g * 4 + 2 * jp + 1)
                    for fc in fcs:
                        chainA(fc, wi, fc % 4)
                    P.op("act", lambda e: e.activation(out=tg[:, :, AM, 0:T], in_=tg[:, :, AM, 0:T], func=AF.Sqrt, bias=0.25, scale=-0.25),
                         writes=[Rtg[0][AM], Rtg[1][AM]], tbl="sqrt", n=2 * T)
                    for fc in fcs:
                        chainB(fc)
            def emit_gl(g):
                wi = stream(win_b[g], Rs_in[g])
                for jb in range(4):
                    fc = (g - 2) * 4 + jb
                    z = proj_block(wi, jb, uk, RuT)
                    sp_ = fc % 2
                    P.op("act", lambda e, z=z, sp_=sp_: e.activation(out=tgs[:, sp_, 0:T], in_=pzz[z][:, 0:T], func=AF.Gelu_apprx_tanh), writes=[Rpzz[z], Rtgs[sp_]], tbl="gelu")
                    P.op("pool", lambda e, fc=fc, sp_=sp_: e.tensor_tensor(Gv[:, fc, 0:T], Gv[:, fc, 0:T], tgs[:, sp_, 0:T], ALU.mult), reads=[Rtgs[sp_]], writes=[RG[fc]])
            def emit_ga(g):
                wi = stream(win_b[g], Rs_in[g])
                for jb in range(4):
                    fc = (g - 10) * 4 + jb
                    z = proj_block(wi, jb, uk, RuT)
                    sp_ = fc % 2
                    P.op("act", lambda e, z=z, sp_=sp_: e.activation(out=tgs[:, sp_, 0:T], in_=pzz[z][:, 0:T], func=AF.Tanh, scale=0.5), writes=[Rpzz[z], Rtgs[sp_]], tbl="tanh")
                    P.op("dve", lambda e, fc=fc, sp_=sp_: e.scalar_tensor_tensor(Gv[:, fc, 0:T], tgs[:, sp_, 0:T], 1.0, Gv[:, fc, 0:T], ALU.add, ALU.mult), reads=[Rtgs[sp_]], writes=[RG[fc]])
            def emit_gb(g):
                wi = stream(win_b[g], Rs_in[g])
                for jb in range(4):
                    fc = (g - 12) * 4 + jb
                    z = proj_block(wi, jb, uk, RuT)
                    P.op("act", lambda e, z=z, fc=fc: e.activation(out=sgb[:, fc, 0:T], in_=pzz[z][:, 0:T], func=AF.Tanh, scale=0.5), writes=[Rpzz[z], Rsgb[fc]], tbl="tanh")
            ro = 512 if sample else (t % 2) * 512
            def emit_q(g):
                wi = stream(win_b[g], Rs_in[g])
                for jb in range(4):
                    h = (g - 4) * 4 + jb
                    z = proj_block(wi, jb, uk, RuT)
                    P.op("dve", lambda e, z=z, h=h: e.tensor_scalar(qT[:, h, 0:T], pzz[z][:, 0:T], ATT_SCALE, None, ALU.mult), writes=[Rpzz[z], RqT[h]])
            def emit_k(g):
                wi = stream(win_b[g], Rs_in[g])
                for jb in range(4):
                    h = (g - 6) * 4 + jb
                    z = proj_block(wi, jb, uk, RuT)
                    P.op("dve", lambda e, z=z, h=h: e.tensor_copy(kTr[:, h, ro:ro + T], pzz[z][:, 0:T]), writes=[Rpzz[z], RkT[h]])
                if last:
                    for s in range(NS):
                        z = nextz()
                        for k in range(8):
                            P.op("pe", lambda e, k=k, z=z, s=s, wi=wi: e.matmul(pzz[z][0:RW, :], uT[:, k, s * 128:s * 128 + RW], wbuf[:, wi, k, :], start=(k == 0), stop=(k == 7)),
                                 reads=[Rwb[wi]] + RuT, writes=[Rpzz[z]])
                        ob = (s + g) % 2
                        P.op("dve", lambda e, z=z, ob=ob: e.tensor_copy(ost[0:RW, ob, 0:512], pzz[z][0:RW, :]), writes=[Rpzz[z], Rost[ob]])
                        dst = (nks if sample else nkp)[s * 128:s * 128 + RW, (g - 6) * 512:(g - 5) * 512]
                        P.op("sp", lambda e, ob=ob, dst=dst: e.dma_start(out=dst, in_=ost[0:RW, ob, 0:512]), reads=[Rost[ob]], slot=Sost[ob], n=1 << 18)
            def emit_v(g):
                wi = stream(win_b[g], Rs_in[g])
                for s in range(NS):
                    z = nextz()
                    for k in range(8):
                        P.op("pe", lambda e, k=k, z=z, s=s, wi=wi: e.matmul(pzz[z][0:RW, :], uT[:, k, s * 128:s * 128 + RW], wbuf[:, wi, k, :], start=(k == 0), stop=(k == 7)),
                             reads=[Rwb[wi]] + RuT, writes=[Rpzz[z]])
                    rb = 4 if sample else (t % 2) * 4 + s
                    P.op("dve", lambda e, z=z, rb=rb, g=g: e.tensor_copy(Vr[0:RW, rb, (g - 8) * 4:(g - 7) * 4, 0:128],
                                                                       pzz[z][0:RW, :].rearrange("p (h d) -> p h d", d=128)),
                         writes=[Rpzz[z], RV[rb]])
                    if last:
                        ob = (s + g) % 2
                        P.op("dve", lambda e, z=z, ob=ob: e.tensor_copy(ost[0:RW, ob, 0:512], pzz[z][0:RW, :]), writes=[Rpzz[z], Rost[ob]])
                        dst = (nvs if sample else nvp)[s * 128:s * 128 + RW, (g - 8) * 512:(g - 7) * 512]
                        P.op("sp", lambda e, ob=ob, dst=dst: e.dma_start(out=dst, in_=ost[0:RW, ob, 0:512]), reads=[Rost[ob]], slot=Sost[ob], n=1 << 18)
            emit_xr(0, 0); emit_q(4); emit_q(5)
            emit_xr(0, 1); emit_k(6); emit_k(7)
            emit_xr(1, 0); emit_v(8); emit_v(9)
            emit_xr(1, 1); emit_gb(12); emit_gb(13)
            emit_gl(2); emit_gl(3); emit_ga(10); emit_ga(11)
            if last:
                P.op("sp", lambda e: e.dma_start(out=(ncs if sample else ncp), in_=hist.rearrange("p c j -> p (c j)")), reads=Rhist, slot=Sout[0 if sample else 2], n=4096)
                P.op("sp", lambda e: e.dma_start(out=(nhs if sample else nhp), in_=hst), reads=Rhst, slot=Sout[1 if sample else 3], n=4096)

            NP = 1 if sample else 4
            QW = 64 if sample else 128
            it = 0
            Sset = [(pa[0], pa[1], Rpa[0], Rpa[1]), (pt[0], pt[1], Rpt[0], Rpt[1])]
            Oset = [(po, Rpo), (pz[1], Rpz[1])]
            for pi in range(NP):
                for h in range(8):
                    bb = it % 2
                    it += 1
                    sM, sC, RsM, RsC = Sset[bb]
                    oB, RoB = Oset[bb]
                    blocks = []
                    for b in range(5):
                        if sample:
                            blocks.append((b, b))
                        else:
                            cs = 8 * t + 2 * pi - 8 + 2 * b
                            if cs >= 0:
                                blocks.append((b, (cs // 2) % 8))
                    mpos = {0: 0, 3: 1, 4: 2}
                    cpos = {1: 0, 2: 1}
                    q_ap = qT[:, h, pi * 128:pi * 128 + QW]
                    for b, rb in blocks:
                        if b in mpos:
                            o_ap = sM[:, mpos[b] * 128:mpos[b] * 128 + QW]; R_ = RsM
                        else:
                            o_ap = sC[:, cpos[b] * 128:cpos[b] * 128 + QW]; R_ = RsC
                        P.op("pe", lambda e, o_ap=o_ap, h=h, rb=rb, q_ap=q_ap: e.matmul(o_ap, kTr[:, h, rb * 128:(rb + 1) * 128], q_ap, start=True, stop=True),
                             reads=[RkT[h], RqT[h]], writes=[R_], n=QW + 64)
                    mb = [b for b, _ in blocks if b in mpos]
                    cb = [b for b, _ in blocks if b in cpos]
                    m0 = mpos[mb[0]]
                    nm = len(mb)
                    P.op("act", lambda e, bb=bb, m0=m0, nm=nm, sM=sM: e.activation(
                        out=sbm[:, bb, m0:m0 + nm, 0:QW], in_=sM.rearrange("p (b q) -> p b q", q=128)[:, m0:m0 + nm, 0:QW], func=AF.Exp),
                        writes=[RsM, Rsbm[bb]], n=nm * QW, tbl="exp")
                    P.op("pool", lambda e, bb=bb, h=h, m0=m0, nm=nm: e.tensor_tensor(
                        pT[:, bb, m0:m0 + nm, 0:QW], sbm[:, bb, m0:m0 + nm, 0:QW], biasM[:, h, m0:m0 + nm, 0:QW], ALU.mult),
                        reads=[Rvec, Rsbm[bb]], writes=[RpT[bb]], n=nm * QW)
                    if cb:
                        c0_ = cpos[cb[0]]
                        ncb = len(cb)
                        P.op("act", lambda e, bb=bb, h=h, c0_=c0_, ncb=ncb, sC=sC: e.activation(
                            out=pT[:, bb, 3 + c0_:3 + c0_ + ncb, 0:QW], in_=sC.rearrange("p (b q) -> p b q", q=128)[:, c0_:c0_ + ncb, 0:QW],
                            func=AF.Exp, bias=cbias[:, h:h + 1], scale=1.0),
                            reads=[Rvec], writes=[RsC, RpT[bb]], n=ncb * QW, tbl="exp")
                    for i_, (b, rb) in enumerate(blocks):
                        pidx = mpos[b] if b in mpos else 3 + cpos[b]
                        P.op("pe", lambda e, bb=bb, pidx=pidx, rb=rb, h=h, i_=i_, nb=len(blocks), oB=oB: e.matmul(
                            oB[0:QW, 0:129], pT[:, bb, pidx, 0:QW], Vr[:, rb, h, :], start=(i_ == 0), stop=(i_ == nb - 1)),
                            reads=[RpT[bb], RV[rb]], writes=[RoB], n=129 + 64)
                    P.op("dve", lambda e, bb=bb, oB=oB: e.reciprocal(rc[0:QW, bb:bb + 1], oB[0:QW, 128:129]), writes=[RoB, Rrc[bb]], n=1)
                    P.op("dve", lambda e, bb=bb, oB=oB: e.tensor_scalar(osb[0:QW, bb, :], oB[0:QW, 0:128], rc[0:QW, bb:bb + 1], None, ALU.mult),
                         reads=[Rrc[bb]], writes=[RoB, Rosb[bb]], n=128)
                    P.op("pe", lambda e, bb=bb: e.transpose(pob[:, bb * 128:bb * 128 + QW], osb[0:QW, bb, :], identB[0:QW, 0:QW]), reads=[Rosb[bb], Rid], writes=[Rpob], n=128)
                    P.op("dve", lambda e, bb=bb, h=h, pi=pi: e.scalar_tensor_tensor(mtmp[:, bb, 0:QW], sgb[:, h, pi * 128:pi * 128 + QW], 1.0, pob[:, bb * 128:bb * 128 + QW], ALU.add, ALU.mult),
                         reads=[Rsgb[h]], writes=[Rpob, Rmt[bb]], n=QW)
                    P.op("dve", lambda e, bb=bb, h=h, pi=pi: e.tensor_tensor(uT[:, h, pi * 128:pi * 128 + QW], Gv[:, h, pi * 128:pi * 128 + QW], mtmp[:, bb, 0:QW], ALU.add),
                         reads=[Rmt[bb], RG[h]], writes=[RuT[h]], n=QW)

            def layernorm(emit_out, inplace=False):
                for n in range(8):
                    b2 = n % 2
                    P.op("dve", lambda e, n=n, b2=b2: e.tensor_copy(xb[:, b2, 0:T], resid[:, n, 0:T]), reads=[Rres[n]], writes=[Rxb[b2]])
                    P.op("act", lambda e, n=n, b2=b2: e.activation(out=sq[:, b2, 0:T], in_=resid[:, n, 0:T], func=AF.Square), reads=[Rres[n]], writes=[Rsq[b2]])
                    P.op("pe", lambda e, n=n, b2=b2: e.matmul(pa[0][:, 0:T], onesS, xb[:, b2, 0:T], start=(n == 0), stop=(n == 7)), reads=[Rid, Rxb[b2]], writes=[Rpa[0]])
                    P.op("pe", lambda e, n=n, b2=b2: e.matmul(pa[1][:, 0:T], onesS, sq[:, b2, 0:T], start=(n == 0), stop=(n == 7)), reads=[Rid, Rsq[b2]], writes=[Rpa[1]])
                P.op("act", lambda e: e.activation(out=lnv[:, 0:T], in_=pa[0][:, 0:T], func=AF.Square), writes=[Rpa[0], Rlnv])
                P.op("dve", lambda e: e.tensor_tensor(lnv[:, 0:T], pa[1][:, 0:T], lnv[:, 0:T], ALU.subtract), writes=[Rpa[1], Rlnv])
                P.op("act", lambda e: e.activation(out=lnv[:, 0:T], in_=lnv[:, 0:T], func=AF.Sqrt, bias=LN_EPS, scale=1.0), writes=[Rlnv], tbl="sqrt")
                P.op("dve", lambda e: e.reciprocal(lnv[:, 0:T], lnv[:, 0:T]), writes=[Rlnv])
                P.op("dve", lambda e: e.scalar_tensor_tensor(lnn[:, 0:T], pa[0][:, 0:T], -1.0, lnv[:, 0:T], ALU.mult, ALU.mult), reads=[Rlnv], writes=[Rpa[0], Rlnn])
                if inplace:
                    for n in range(8):
                        P.op("dve", lambda e, n=n: e.tensor_tensor(Gv[:, n, 0:T], resid[:, n, 0:T], lnv[:, 0:T], ALU.mult), reads=[Rres[n], Rlnv], writes=[RG[n], Rln2m])
                    for n in range(8):
                        P.op("pool", lambda e, n=n: e.tensor_tensor(Gv[:, n, 0:T], Gv[:, n, 0:T], lnn[:, 0:T], ALU.add), reads=[Rlnn, Rln2m], writes=[RG[n]])
                        emit_out(n, None)
                    return
                for n in range(8):
                    b2 = n % 2
                    P.op("dve", lambda e, n=n, b2=b2: e.tensor_tensor(lnt[:, b2, 0:T], resid[:, n, 0:T], lnv[:, 0:T], ALU.mult), reads=[Rres[n], Rlnv], writes=[Rlnt[b2]])
                    P.op("dve", lambda e, b2=b2: e.tensor_tensor(lnt[:, b2, 0:T], lnt[:, b2, 0:T], lnn[:, 0:T], ALU.add), reads=[Rlnn], writes=[Rlnt[b2]])
                    emit_out(n, b2)

            mk = lambda k: uT[:, k, 0:T]
            for g in range(2):
                wi = stream(wout_b[g], Rs_out[g])
                for jb in range(4):
                    n = g * 4 + jb
                    z = proj_block(wi, jb, mk, RuT)
                    P.op("dve", lambda e, z=z, n=n: e.scalar_tensor_tensor(resid[:, n, 0:T], pzz[z][:, 0:T], col(CJ(j, 1), n), resid[:, n, 0:T], ALU.mult, ALU.add),
                         reads=[Rcst], writes=[Rpzz[z], Rres[n]])

            def ln1_out(n, b2):
                P.op("act", lambda e, n=n, b2=b2: e.activation(out=u2T[:, n, 0:T], in_=lnt[:, b2, 0:T], func=AF.Identity, scale=col(CJ(j, 2), n), bias=col(CJ(j, 3), n)),
                     reads=[Rlnt[b2], Rcst], writes=[Ru2[n]])
                P.op("act", lambda e, n=n, b2=b2: e.activation(out=resid[:, n, 0:T], in_=lnt[:, b2, 0:T], func=AF.Identity, scale=col(GA, n), bias=col(CJ(j, 5), n)),
                     reads=[Rlnt[b2], Rcst], writes=[Rres[n]])
            layernorm(ln1_out)

            P.fence(RG + Rsgb + RqT, RhT)
            u2k = lambda k: u2T[:, k, 0:T]
            for g in range(8):
                wi = stream(wup_b[g], Rs_up[g])
                if g == 0:
                    zs0 = [nextz() for _ in range(4)]
                    for k in range(8):
                        for jb in range(4):
                            P.op("pe", lambda e, k=k, jb=jb, zz=zs0[jb], wi=wi: e.matmul(pzz[zz][:, 0:T], wbuf[:, wi, k, jb * 128:(jb + 1) * 128], u2T[:, k, 0:T], start=(k == 0), stop=(k == 7)),
                                 reads=[Rwb[wi], Ru2[k]], writes=[Rpzz[zs0[jb]]])
                for jb in range(4):
                    m = g * 4 + jb
                    z = zs0[jb] if g == 0 else proj_block(wi, jb, u2k, Ru2)
                    tb = m % 2
                    P.op("act", lambda e, z=z, m=m, tb=tb: e.activation(out=tgs[:, tb, 0:T], in_=pzz[z][:, 0:T], func=AF.Relu, bias=vcol(V_BUP, m), scale=1.0),
                         reads=[Rvec], writes=[Rpzz[z], Rtgs[tb]])
                    eng = "pool" if m % 2 == 0 else "dve"
                    P.op(eng, lambda e, m=m, tb=tb: e.tensor_tensor(hT[:, m, 0:T], tgs[:, tb, 0:T], tgs[:, tb, 0:T], ALU.mult), reads=[Rtgs[tb]], writes=[RhT[m]])
            for n in range(8):
                i = wq[0] % NWB
                wq[0] += 1
                wd = wbuf[:, i].rearrange("p k c -> p (k c)").rearrange("p (k c) -> p k c", c=128)
                if id(Rs_dn[n]) not in converted:
                    converted.add(id(Rs_dn[n]))
                    for kh in range(2):
                        P.op("pool", lambda e, n=n, wd=wd, kh=kh: e.dma_start(
                            out=wd[:, kh * 16:(kh + 1) * 16, :],
                            in_=w_down[kh * 2048:(kh + 1) * 2048, n * 128:(n + 1) * 128].rearrange("(k p) c -> p k c", p=128)),
                            writes=[Rwb[i]], slot=Swbp[i], n=1 << 20)
                    P.op("sp", lambda e, n=n, wd=wd: e.dma_start(out=wdn_b[n], in_=wd), reads=[Rwb[i]], writes=[Rs_dn[n]], slot=Ss_dn[n])
                else:
                    P.op("sp", lambda e, i=i, n=n: e.dma_start(out=wbuf[:, i].rearrange("p k c -> p (k c)"), in_=wdn_b[n].rearrange("p k c -> p (k c)")),
                         reads=[Rs_dn[n]], writes=[Rwb[i]], slot=Swb[i])
                z = nextz()
                for k in range(32):
                    P.op("pe", lambda e, k=k, z=z, wd=wd: e.matmul(pzz[z][:, 0:T], wd[:, k, :], hT[:, k, 0:T], start=(k == 0), stop=(k == 31)),
                         reads=[Rwb[i], RhT[k]], writes=[Rpzz[z]])
                P.op("dve", lambda e, z=z, n=n: e.scalar_tensor_tensor(resid[:, n, 0:T], pzz[z][:, 0:T], col(CJ(j, 4), n), resid[:, n, 0:T], ALU.mult, ALU.add),
                     reads=[Rcst], writes=[Rpzz[z], Rres[n]])

            def ln2_out(n, b2):
                P.op("act", lambda e, n=n: e.activation(out=Gv[:, n, 0:T], in_=Gv[:, n, 0:T], func=AF.Identity, scale=vcol(V_L2G, n), bias=vcol(V_L2B, n)),
                     reads=[Rvec], writes=[RG[n]])
            P.fence(RhT, RG)
            layernorm(ln2_out, inplace=True)

            for s in range(NS):
                ob = s % 2
                for half in range(2):
                    pi_ = half
                    for f4 in range(4):
                        fc = half * 4 + f4
                        P.op("pe", lambda e, s=s, fc=fc, f4=f4, pi_=pi_: e.transpose(pa[pi_][0:RW, f4 * 128:(f4 + 1) * 128], Gv[:, fc, s * 128:s * 128 + RW], identF),
                             reads=[RG[fc], Rid], writes=[Rpa[pi_]], n=256)
                    eng = "act" if half == 0 else "dve"
                    if eng == "act":
                        P.op("act", lambda e, ob=ob, half=half, pi_=pi_: e.activation(out=ost[0:RW, ob, half * 512:(half + 1) * 512], in_=pa[pi_][0:RW, :], func=AF.Copy),
                             writes=[Rpa[pi_], Rost[ob]])
                    else:
                        P.op("dve", lambda e, ob=ob, half=half, pi_=pi_: e.tensor_copy(ost[0:RW, ob, half * 512:(half + 1) * 512], pa[pi_][0:RW, :]),
                             writes=[Rpa[pi_], Rost[ob]])
                P.op("sp", lambda e, ob=ob, s=s: e.dma_start(out=ydst[s * 128:s * 128 + RW, :], in_=ost[0:RW, ob, :]), reads=[Rost[ob]], slot=Sost[ob])

        tile(None, 64, 1)
        for r_ in Rs_in + Rs_out + Rs_up + Rs_dn:
            r_.ro = True
        for t in range(NT):
            tile(t, 512, 0)

        P.final_slots = Sost + Sout
        P.emit()
        build.last = (P.est_ns, getattr(P, "n_tbl", 0), len(P.ops))
    return nc


_NC_CACHE = {}


def _pcol(v):
    v = np.asarray(v, np.float32).reshape(-1, 128)
    return np.ascontiguousarray(v.T)


def _host_layout(inputs, NT):
    f = lambda k: np.asarray(inputs[k], np.float32)
    conv_w = f("conv_w")[0]
    vecs = np.concatenate([
        _pcol(conv_w[0]), _pcol(conv_w[1]), _pcol(conv_w[2]), _pcol(conv_w[3]),
        _pcol(f("conv_b")[0]), _pcol(f("b_rg")[0].reshape(-1)), _pcol(f("b_ig")[0].reshape(-1)), _pcol(f("lru_lambda")[0]),
        _pcol(f("ln1_g")[0]), _pcol(f("ln1_b")[0]), _pcol(f("ln2_g")[0]), _pcol(f("ln2_b")[0]),
        _pcol(f("b_down")[0]), _pcol(f("b_up")[0]), _pcol(f("b_ada")[0])], axis=1)
    assert vecs.shape == (128, NV)
    table = f("rel_bias")[0]
    kr = np.arange(128)[:, None]; qc = np.arange(128)[None, :]
    qch = qc // 64
    biasM = np.empty((128, 8, 3, 128), np.float32)
    for i, b in enumerate((0, 3, 4)):
        kch = 2 * b - 8 + kr // 64
        kpos = (2 * b - 8) * 64 + kr
        rel = np.clip(qc - kpos, -128, 128) + 128
        vis = (kch <= qch) & (kch >= qch - 8)
        for h in range(8):
            biasM[:, h, i, :] = np.where(vis, table[h][rel], np.float32(NEG))
    cbias = np.ascontiguousarray(np.broadcast_to(table[:, 256][None, :], (128, 8))).astype(np.float32)
    shared = {
        "vecs": vecs, "biasM": np.ascontiguousarray(biasM.reshape(128, -1)), "cbias": cbias,
        "w_ada": np.ascontiguousarray(f("w_ada")[0]), "w_in": np.ascontiguousarray(f("w_in")[0]),
        "w_rg": np.ascontiguousarray(f("w_rg")[0]), "w_ig": np.ascontiguousarray(f("w_ig")[0]),
        "w_out": np.ascontiguousarray(f("w_out")[0]), "w_up": np.ascontiguousarray(f("w_up")[0]),
        "w_down": np.ascontiguousarray(f("w_down")[0]),
    }
    in_maps = []
    nb = f("x_prompt").shape[0]
    for b in range(nb):
        sc = f("state_conv")[0, b]
        cvec = np.concatenate([_pcol(f("c_prompt")[b]), _pcol(f("c_sample")[b]), _pcol(f("state_lru")[0, b]),
                               _pcol(sc[0]), _pcol(sc[1]), _pcol(sc[2])], axis=1)
        m = dict(shared)
        m.update({
            "xp": np.ascontiguousarray(f("x_prompt")[b, :NT * 512]), "xs": np.ascontiguousarray(f("x_sample")[b]),
            "ck": np.ascontiguousarray(f("cache_k")[0, b].reshape(512, 1024)),
            "cv": np.ascontiguousarray(f("cache_v")[0, b].reshape(512, 1024)),
            "cvec": np.ascontiguousarray(cvec),
        })
        in_maps.append(m)
    return in_maps


def _unp(a, n):
    return np.ascontiguousarray(a.T).reshape(-1)


def run(inputs, NT, cores=None):
    if NT not in _NC_CACHE:
        _NC_CACHE[NT] = build(NT)
    nc = _NC_CACHE[NT]
    in_maps = _host_layout(inputs, NT)
    if cores is not None:
        in_maps = [in_maps[c] for c in cores]
    res = run_bass_kernel_spmd(nc, in_maps, core_ids=list(range(len(in_maps))))
    R = res.results
    B = len(R)
    st = lambda k: np.stack([np.asarray(r[k], np.float32) for r in R])
    yp = st("yp"); ys = st("ys")
    nkp = st("nkp").reshape(1, B, 512, 8, 128); nvp = st("nvp").reshape(1, B, 512, 8, 128)
    nks = st("nks").reshape(1, B, 64, 8, 128); nvs = st("nvs").reshape(1, B, 64, 8, 128)

    def conv(k):
        a = st(k).reshape(B, 128, 8, 3)
        return np.ascontiguousarray(a.transpose(0, 3, 2, 1)).reshape(1, B, 3, 1024)

    def lru(k):
        a = st(k)
        return np.ascontiguousarray(a.transpose(0, 2, 1)).reshape(1, B, 1024)
    return (yp, ys, nkp, nvp, conv("ncp"), lru("nhp"), nks, nvs, conv("ncs"), lru("nhs"))


def kernel(**inputs):
    return run(inputs, 16)
```

```python
import numpy as np
from contextlib import ExitStack
import concourse.bass as bass
import concourse.mybir as mybir
from concourse.bass_utils import run_bass_kernel_spmd

F32 = mybir.dt.float32
BF16 = mybir.dt.bfloat16
AF = mybir.ActivationFunctionType
ALU = mybir.AluOpType

ALPHA = 2.0 ** 0.25
ATT_SCALE = 128.0 ** -0.5
LN_EPS = 1e-5
NEG = -30000.0


class Res:
    __slots__ = ("name", "w", "r", "ro")

    def __init__(self, name, ro=False):
        self.name = name
        self.w = None
        self.r = []
        self.ro = ro


class Slot:
    __slots__ = ("sem", "count")

    def __init__(self, sem):
        self.sem = sem
        self.count = 0


class Prog:
    ENG = ("pe", "act", "dve", "pool", "sp")
    LOOK = 24
    WIN = 150.0
    LAT_TO_PE = 3000.0
    LAT_FROM_PE = 300.0
    LAT_X = 600.0

    def __init__(self, nc, es):
        self.nc = nc
        self.es = es
        self.ops = []
        self.sem = {e: es.enter_context(nc.semaphore("s_" + e)) for e in self.ENG}
        self.nslot = 0
        self.final_slots = []
        self.defn = 512

    def slot(self):
        self.nslot += 1
        return Slot(self.es.enter_context(self.nc.semaphore("d%d" % self.nslot)))

    def cost(self, eng, slot, n):
        if slot is not None:
            return float(n if n is not None else 1 << 20)
        n = self.defn if n is None else n
        if eng == "pe":
            return n / 2.4 + 10.0
        if eng == "act":
            return 230.0 + n / 1.15
        if eng == "dve":
            return 120.0 + n / 0.9
        return 300.0 + n / 0.45

    SERVED = {0: ("exp", "tanh"), 2: ("tanh", "sigmoid"), 3: ("sqrt",), 11: ("gelu", "tanh"), 5: ("ln",), 18: ("silu", "tanh")}
    LOWEST = {"exp": 0, "tanh": 0, "sigmoid": 2, "sqrt": 3, "gelu": 11, "ln": 5, "silu": 18}

    def op(self, eng, fn, reads=(), writes=(), slot=None, n=None, tbl=None, delay=0.0):
        idx = len(self.ops)
        deps = set()
        for r in reads:
            if r.w is not None:
                deps.add(r.w)
        for w in writes:
            if w.w is not None:
                deps.add(w.w)
            deps.update(w.r)
        self.ops.append((eng, fn, slot, deps, self.cost(eng, slot, n), tbl, delay))
        for r in reads:
            if not r.ro:
                r.r.append(idx)
        for w in writes:
            w.w = idx
            w.r = []
        return idx

    def fence(self, frm, to):
        toks = []
        for f in frm:
            if f.w is not None:
                toks.append(f.w)
            toks.extend(f.r)
        for t in to:
            t.r.extend(toks)

    def schedule(self):
        import bisect
        ops = self.ops
        N = len(ops)
        succ = [[] for _ in range(N)]
        indeg = [0] * N
        for i, o in enumerate(ops):
            for d in o[3]:
                succ[d].append(i)
            indeg[i] = len(o[3])
        ready = {e: [] for e in self.ENG}
        rtime = [0.0] * N
        fin = [0.0] * N
        efree = {e: 0.0 for e in self.ENG}
        order = {e: [] for e in self.ENG}
        for i in range(N):
            if indeg[i] == 0:
                ready[ops[i][0]].append(i)
        bw_free = 0.0
        left = N
        LOOK, WIN = self.LOOK, self.WIN
        cur_set = -1
        SERVED, LOWEST = self.SERVED, self.LOWEST
        while left:
            best = None
            for e in self.ENG:
                L = ready[e]
                if not L:
                    continue
                te = efree[e]
                c = None
                cr = None
                cfall = None
                for i in L[:LOOK]:
                    rt = rtime[i]
                    if rt <= te + WIN:
                        if e == "act":
                            tb = ops[i][5]
                            if tb is not None and (cur_set < 0 or tb not in SERVED[cur_set]):
                                if cfall is None:
                                    cfall = i
                                continue
                        c = i
                        break
                    if cr is None or rt < cr:
                        cr = rt
                        c2 = i
                if c is None:
                    c = cfall if cfall is not None else c2
                st = te if rtime[c] < te else rtime[c]
                if best is None or (st, c) < (best[0], best[2]):
                    best = (st, e, c)
            st, e, c = best
            L = ready[e]
            L.pop(bisect.bisect_left(L, c))
            o = ops[c]
            if o[2] is not None:
                issue = 60.0 if e == "sp" else 600.0
                efree[e] = st + issue
                b0 = bw_free if bw_free > st else st
                bw_free = b0 + o[4] / 160.0
                f = max(st + 2000.0, bw_free)
            else:
                dur = o[4]
                if e == "act" and o[5] is not None and (cur_set < 0 or o[5] not in SERVED[cur_set]):
                    cur_set = LOWEST[o[5]]
                    dur += 1283.0
                    self.n_tbl = getattr(self, "n_tbl", 0) + 1
                efree[e] = st + dur
                f = st + dur + (60.0 if e == "pe" else 0.0)
            fin[c] = f
            order[e].append(c)
            left -= 1
            for s_ in succ[c]:
                ce = ops[s_][0]
                if o[2] is not None:
                    lat = f + 200.0
                elif ce == e:
                    lat = f
                elif ce == "pe":
                    lat = f + self.LAT_TO_PE
                elif e == "pe":
                    lat = f + self.LAT_FROM_PE
                else:
                    lat = f + self.LAT_X
                if lat > rtime[s_]:
                    rtime[s_] = lat
                indeg[s_] -= 1
                if indeg[s_] == 0:
                    rtime[s_] += ops[s_][6]
                    bisect.insort(ready[ops[s_][0]], s_)
        self.est_ns = max(fin) if fin else 0.0
        return order

    def emit(self):
        nc = self.nc
        ops = self.ops
        order = self.schedule()
        tok = [None] * len(ops)
        sig = [False] * len(ops)
        for i, o in enumerate(ops):
            for d in o[3]:
                if ops[d][2] is None and not (o[0] == "pe" and ops[d][0] == "pe"):
                    sig[d] = True
        cnt = {e: 0 for e in self.ENG}
        for e in self.ENG:
            for i in order[e]:
                sl = ops[i][2]
                if sl is None:
                    if sig[i]:
                        cnt[e] += 1
                        tok[i] = (self.sem[e], cnt[e], e)
                else:
                    sl.count += 16
                    tok[i] = (sl.sem, sl.count, None)
        self.n_sig = sum(sig)

        def run(eng, name):
            waited = {}
            own = self.sem[name]
            for i in order[name]:
                o = ops[i]
                need = {}
                for d in o[3]:
                    if name == "pe" and ops[d][0] == "pe" and ops[d][2] is None:
                        continue
                    s, v, de = tok[d]
                    if need.get(s, 0) < v:
                        need[s] = v
                for s, v in need.items():
                    if waited.get(s, 0) < v:
                        waited[s] = v
                        eng.wait_ge(s, v)
                ins = o[1](eng)
                t = tok[i]
                if t is not None:
                    ins.then_inc(t[0], 16 if o[2] is not None else 1)
            if name == "sp":
                for sl in self.final_slots:
                    eng.wait_ge(sl.sem, sl.count)

        with nc.Block() as block:
            @block.tensor
            def _(e):
                run(e, "pe")

            @block.scalar
            def _(e):
                run(e, "act")

            @block.vector
            def _(e):
                run(e, "dve")

            @block.gpsimd
            def _(e):
                run(e, "pool")

            @block.sync
            def _(e):
                run(e, "sp")


V_CW, V_CB, V_BRG, V_BIG, V_LAM, V_L1G, V_L1B, V_L2G, V_L2B, V_BD, V_BUP, V_BADA, NV = \
    0, 32, 40, 48, 56, 64, 72, 80, 88, 96, 104, 136, 184
C_C, C_LRU, C_CONV, NCV = 0, 16, 24, 48


def build(NT):
    nc = bass.Bass("TRN2", target_bir_lowering=False)
    SEQ = NT * 512

    def din(name, shape, dt=F32):
        return nc.dram_tensor(name, shape, dt, kind="ExternalInput").ap()

    def dout(name, shape):
        return nc.dram_tensor(name, shape, F32, kind="ExternalOutput").ap()

    xp = din("xp", [SEQ, 1024]); xs = din("xs", [64, 1024])
    ck = din("ck", [512, 1024]); cv = din("cv", [512, 1024])
    vecs_d = din("vecs", [128, NV]); cvec_d = din("cvec", [128, NCV])
    biasM_d = din("biasM", [128, 8 * 3 * 128]); cbias_d = din("cbias", [128, 8])
    w_ada = din("w_ada", [1024, 6144]); w_in = din("w_in", [1024, 7168])
    w_rg = din("w_rg", [8, 128, 128]); w_ig = din("w_ig", [8, 128, 128])
    w_out = din("w_out", [1024, 1024]); w_up = din("w_up", [1024, 4096]); w_down = din("w_down", [4096, 1024])
    yp = dout("yp", [SEQ, 1024]); ys = dout("ys", [64, 1024])
    nkp = dout("nkp", [512, 1024]); nvp = dout("nvp", [512, 1024])
    ncp = dout("ncp", [128, 24]); nhp = dout("nhp", [128, 8])
    nks = dout("nks", [64, 1024]); nvs = dout("nvs", [64, 1024])
    ncs = dout("ncs", [128, 24]); nhs = dout("nhs", [128, 8])
    win_b = nc.dram_tensor("win_b", [14, 128, 8, 512], BF16, kind="Internal").ap()
    wout_b = nc.dram_tensor("wout_b", [2, 128, 8, 512], BF16, kind="Internal").ap()
    wup_b = nc.dram_tensor("wup_b", [8, 128, 8, 512], BF16, kind="Internal").ap()
    wdn_b = nc.dram_tensor("wdn_b", [8, 128, 32, 128], BF16, kind="Internal").ap()

    with ExitStack() as es:
        P = Prog(nc, es)

        def sb(name, shape, dt=F32):
            return nc.alloc_sbuf_tensor(name, shape, dt).ap()

        xst = sb("xst", [128, 2, 1024]); Rxst = [Res("xst0"), Res("xst1")]; Sxst = [P.slot(), P.slot()]
        ost = sb("ost", [128, 2, 1024]); Rost = [Res("ost0"), Res("ost1")]; Sost = [P.slot(), P.slot()]
        resid = sb("resid", [128, 8, 512]); Rres = [Res("res%d" % i) for i in range(8)]
        uT = sb("uT", [128, 8, 512], BF16); RuT = [Res("uT%d" % i) for i in range(8)]
        u2T = sb("u2T", [128, 8, 512], BF16); Ru2 = [Res("u2T%d" % i) for i in range(8)]
        big = sb("big", [128, 16384], BF16)
        hT = big.rearrange("p (m t) -> p m t", t=512); RhT = [Res("hT%d" % i) for i in range(32)]
        kTr = sb("kTr", [128, 8, 1024], BF16); RkT = [Res("kT%d" % i) for i in range(8)]
        Vr = sb("Vr", [128, 8, 8, 129], BF16); RV = [Res("V%d" % i) for i in range(8)]
        biasM = sb("biasM_s", [128, 8, 3, 128]); cbias = sb("cbias_s", [128, 8])
        pT = sb("pT", [128, 2, 5, 128], BF16); RpT = [Res("pT0"), Res("pT1")]
        sbm = sb("sbm", [128, 2, 3, 128]); Rsbm = [Res("sbm0"), Res("sbm1")]
        osb = sb("osb", [128, 2, 128], BF16); Rosb = [Res("osb0"), Res("osb1")]
        rc = sb("rc", [128, 2]); Rrc = [Res("rc0"), Res("rc1")]
        mtmp = sb("mtmp", [128, 2, 128]); Rmt = [Res("mt0"), Res("mt1")]
        xb = sb("xb", [128, 2, 512], BF16); Rxb = [Res("xb0"), Res("xb1")]
        sq = sb("sq", [128, 2, 512], BF16); Rsq = [Res("sq0"), Res("sq1")]
        lnm = sb("lnm", [128, 512]); lnv = sb("lnv", [128, 512]); lnn = sb("lnn", [128, 512])
        lnt = sb("lnt", [128, 2, 512]); Rlnt = [Res("lnt0"), Res("lnt1")]
        Rlnm, Rlnv, Rlnn = Res("lnm"), Res("lnv"), Res("lnn")
        Rln2m = Res("ln2m")
        tg = sb("tg", [128, 2, 4, 512]); Rtg = [[Res("tg%d_%d" % (p_, i)) for i in range(4)] for p_ in range(2)]
        tgs = sb("tgs", [128, 2, 512]); Rtgs = [Res("tgs0"), Res("tgs1")]
        xcb = sb("xcb", [128, 2, 512], BF16); Rxcb = [Res("xcb0"), Res("xcb1")]
        xrb = sb("xrb", [128, 2, 515]); Rxrb = [Res("xrb0"), Res("xrb1")]
        hist = sb("hist", [128, 8, 3]); Rhist = [Res("hist%d" % i) for i in range(8)]
        hst = sb("hst", [128, 8]); Rhst = [Res("hst%d" % i) for i in range(8)]
        NWB = 3
        wbuf = sb("wbuf", [128, NWB, 8, 512], BF16); Rwb = [Res("wb%d" % i) for i in range(NWB)]; Swb = [P.slot() for _ in range(NWB)]; Swbp = [P.slot() for _ in range(NWB)]
        wrg = sb("wrg", [128, 8, 128], BF16); wig = sb("wig", [128, 8, 128], BF16); Rwg = Res("wg"); Swg = P.slot()
        vecs = sb("vecs_s", [128, NV]); cvec = sb("cvec_s", [128, NCV]); Rvec = Res("vec"); Svec = P.slot()
        mod = sb("mod", [128, 48, 2]); Rmod = Res("mod")
        cst = sb("cst", [128, 136]); Rcst = Res("cst")
        csil = sb("csil", [128, 8, 2]); Rcsil = Res("csil")
        identF = sb("identF", [128, 128]); identB = sb("identB", [128, 128], BF16); onesS = sb("onesS", [128, 128], BF16)
        Rid = Res("ident")

        def pb(name, dt=F32, n=512):
            return nc.alloc_psum_tensor(name, [128, n], dt).ap()
        pt = [pb("pt0"), pb("pt1")]; Rpt = [Res("pt0"), Res("pt1")]
        pz = [pb("pz0"), pb("pz1")]; Rpz = [Res("pz0"), Res("pz1")]
        pa = [pb("pa0"), pb("pa1")]; Rpa = [Res("pa0"), Res("pa1")]
        po = pb("po"); Rpo = Res("po")
        pob = pb("pob", BF16, 1024); Rpob = Res("pob")

        Rs_in = [Res("s_in%d" % g) for g in range(14)]; Ss_in = [P.slot() for _ in range(14)]
        Rs_out = [Res("s_out%d" % g) for g in range(2)]; Ss_out = [P.slot() for _ in range(2)]
        Rs_up = [Res("s_up%d" % g) for g in range(8)]; Ss_up = [P.slot() for _ in range(8)]
        Rs_dn = [Res("s_dn%d" % g) for g in range(8)]; Ss_dn = [P.slot() for _ in range(8)]
        Scv = [P.slot() for _ in range(4)]
        Sout = [P.slot() for _ in range(4)]

        P.op("sp", lambda e: e.dma_start(out=vecs, in_=vecs_d), writes=[Rvec], slot=Svec)
        P.op("sp", lambda e: e.dma_start(out=cvec, in_=cvec_d), writes=[Rvec], slot=Svec)
        P.op("sp", lambda e: e.dma_start(out=biasM.rearrange("p a b c -> p (a b c)"), in_=biasM_d), writes=[Rvec], slot=Svec)
        P.op("sp", lambda e: e.dma_start(out=cbias, in_=cbias_d), writes=[Rvec], slot=Svec)
        P.op("act", lambda e: e.activation(out=biasM.rearrange("p a b c -> p (a b c)"), in_=biasM.rearrange("p a b c -> p (a b c)"), func=AF.Exp),
             writes=[Rvec], tbl="exp", n=3072)
        P.op("pool", lambda e: e.dma_start(out=wrg, in_=w_rg.rearrange("n c d -> c n d")), writes=[Rwg], slot=Swg)
        P.op("pool", lambda e: e.dma_start(out=wig, in_=w_ig.rearrange("n c d -> c n d")), writes=[Rwg], slot=Swg)
        P.op("pool", lambda e: e.memset(identF, 0.0), writes=[Rid])
        P.op("pool", lambda e: e.affine_select(out=identF, in_=identF, pattern=[[-1, 128]], compare_op=ALU.not_equal,
                                                fill=1.0, base=0, channel_multiplier=1), writes=[Rid])
        P.op("pool", lambda e: e.tensor_copy(identB, identF), writes=[Rid])
        P.op("pool", lambda e: e.memset(onesS, 1.0 / 1024.0), writes=[Rid])
        P.op("pool", lambda e: e.memset(kTr.rearrange("p a b -> p (a b)"), 0.0), writes=RkT)
        P.op("pool", lambda e: e.memset(Vr.rearrange("p a b c -> p (a b c)"), 0.0), writes=RV)
        P.op("pool", lambda e: e.memset(Vr[:, :, :, 128:129].rearrange("p a b c -> p (a b c)"), 1.0), writes=RV)
        P.op("pool", lambda e: e.memset(pT.rearrange("p a b c -> p (a b c)"), 0.0), writes=RpT)

        P.op("act", lambda e: e.activation(out=csil.rearrange("p k j -> p j k"),
                                           in_=cvec[:, C_C:C_C + 16].rearrange("p (j k) -> p j k", j=2), func=AF.Silu),
             reads=[Rvec], writes=[Rcsil], tbl="silu")
        csil_b = sb("csil_b", [128, 8, 2], BF16)
        P.op("dve", lambda e: e.tensor_copy(csil_b.rearrange("p k j -> p (k j)"), csil.rearrange("p k j -> p (k j)")), reads=[Rcsil], writes=[Rcsil], n=16)
        Smod = [P.slot() for _ in range(4)]
        stg = [(xst[:, 0, :], Rxst[0]), (xst[:, 1, :], Rxst[1]), (ost[:, 0, :], Rost[0]), (ost[:, 1, :], Rost[1])]
        for c in range(24):
            buf, Rb = stg[c % 4]
            wv = buf.bitcast(BF16).rearrange("p (k c) -> p k c", k=8)
            P.op("pool", lambda e, c=c, wv=wv: e.dma_start(out=wv, in_=w_ada[:, c * 256:(c + 1) * 256].rearrange("(k p) c -> p k c", p=128)),
                 writes=[Rb], slot=Smod[c % 4], n=1 << 20)
            for jb in range(2):
                n = c * 2 + jb
                for k in range(8):
                    P.op("pe", lambda e, k=k, n=n, jb=jb, wv=wv: e.matmul(pa[0][:, n * 2:n * 2 + 2], wv[:, k, jb * 128:(jb + 1) * 128], csil_b[:, k, :], start=(k == 0), stop=(k == 7)),
                         reads=[Rb, Rcsil], writes=[Rpa[0]], n=200)
        P.op("dve", lambda e: e.tensor_tensor(mod[:, :, 0], pa[0][:, 0:96].rearrange("p (n j) -> p n j", j=2)[:, :, 0], vecs[:, V_BADA:V_BADA + 48], ALU.add),
             reads=[Rvec], writes=[Rpa[0], Rmod])
        P.op("dve", lambda e: e.tensor_tensor(mod[:, :, 1], pa[0][:, 0:96].rearrange("p (n j) -> p n j", j=2)[:, :, 1], vecs[:, V_BADA:V_BADA + 48], ALU.add),
             reads=[Rvec], writes=[Rpa[0], Rmod])
        CL, GA = 0, 8
        def CJ(j, i):
            return 16 + j * 48 + i * 8 - 0
        P.op("act", lambda e: e.activation(out=cst[:, CL:CL + 8], in_=vecs[:, V_LAM:V_LAM + 8], func=AF.Exp, scale=-1.0), reads=[Rvec], writes=[Rcst], tbl="exp")
        P.op("act", lambda e: e.activation(out=cst[:, CL:CL + 8], in_=cst[:, CL:CL + 8], func=AF.Ln, bias=1.0, scale=1.0), writes=[Rcst], tbl="ln")
        P.op("dve", lambda e: e.tensor_scalar(cst[:, CL:CL + 8], cst[:, CL:CL + 8], -8.0, None, ALU.mult), writes=[Rcst])
        P.op("dve", lambda e: e.tensor_scalar(cst[:, GA:GA + 8], vecs[:, V_L1G:V_L1G + 8], ALPHA, None, ALU.mult), reads=[Rvec], writes=[Rcst])
        CL2, HBR, HBI = 112, 120, 128
        P.op("dve", lambda e: e.tensor_scalar(cst[:, CL2:CL2 + 8], cst[:, CL:CL + 8], 0.5, None, ALU.mult), writes=[Rcst])
        P.op("dve", lambda e: e.tensor_scalar(cst[:, HBR:HBR + 8], vecs[:, V_BRG:V_BRG + 8], 0.5, None, ALU.mult), reads=[Rvec], writes=[Rcst])
        P.op("dve", lambda e: e.tensor_scalar(cst[:, HBI:HBI + 8], vecs[:, V_BIG:V_BIG + 8], 0.5, None, ALU.mult), reads=[Rvec], writes=[Rcst])
        for j in range(2):
            def M(blk, j=j):
                return mod[:, blk * 8:(blk + 1) * 8, j]
            P.op("dve", lambda e, j=j, M=M: e.tensor_scalar(cst[:, CJ(j, 0):CJ(j, 0) + 8], M(1), 1.0, 1.0 / ALPHA, ALU.add, ALU.mult), reads=[Rmod], writes=[Rcst])
            P.op("dve", lambda e, j=j, M=M: e.tensor_scalar(cst[:, CJ(j, 1):CJ(j, 1) + 8], M(2), 1.0, 0.5, ALU.add, ALU.mult), reads=[Rmod], writes=[Rcst])
            P.op("dve", lambda e, j=j, M=M: e.tensor_scalar(cst[:, CJ(j, 4):CJ(j, 4) + 8], M(5), 1.0, None, ALU.add), reads=[Rmod], writes=[Rcst])
            P.op("dve", lambda e, j=j, M=M: e.tensor_scalar(cst[:, CJ(j, 3):CJ(j, 3) + 8], M(4), 1.0, None, ALU.add), reads=[Rmod], writes=[Rcst])
            P.op("dve", lambda e, j=j: e.tensor_tensor(cst[:, CJ(j, 2):CJ(j, 2) + 8], vecs[:, V_L1G:V_L1G + 8], cst[:, CJ(j, 3):CJ(j, 3) + 8], ALU.mult), reads=[Rvec], writes=[Rcst])
            P.op("dve", lambda e, j=j: e.tensor_tensor(cst[:, CJ(j, 3):CJ(j, 3) + 8], vecs[:, V_L1B:V_L1B + 8], cst[:, CJ(j, 3):CJ(j, 3) + 8], ALU.mult), reads=[Rvec], writes=[Rcst])
            P.op("dve", lambda e, j=j, M=M: e.tensor_tensor(cst[:, CJ(j, 3):CJ(j, 3) + 8], cst[:, CJ(j, 3):CJ(j, 3) + 8], M(3), ALU.add), reads=[Rmod], writes=[Rcst])
            P.op("dve", lambda e, j=j: e.tensor_tensor(cst[:, CJ(j, 5):CJ(j, 5) + 8], vecs[:, V_BD:V_BD + 8], cst[:, CJ(j, 4):CJ(j, 4) + 8], ALU.mult), reads=[Rvec], writes=[Rcst])
            P.op("dve", lambda e, j=j: e.scalar_tensor_tensor(cst[:, CJ(j, 5):CJ(j, 5) + 8], vecs[:, V_L1B:V_L1B + 8], ALPHA, cst[:, CJ(j, 5):CJ(j, 5) + 8], ALU.mult, ALU.add), reads=[Rvec], writes=[Rcst])

        for r_ in [Rvec, Rcst, Rmod, Rid, Rwg, Rcsil]:
            r_.ro = True
        wq = [0]

        converted = set()
        f32src = {}
        for g in range(14):
            f32src[id(Rs_in[g])] = (w_in[:, g * 512:(g + 1) * 512].rearrange("(k p) c -> p k c", p=128), Ss_in[g])
        for g in range(2):
            f32src[id(Rs_out[g])] = (w_out[:, g * 512:(g + 1) * 512].rearrange("(k p) c -> p k c", p=128), Ss_out[g])
        for g in range(8):
            f32src[id(Rs_up[g])] = (w_up[:, g * 512:(g + 1) * 512].rearrange("(k p) c -> p k c", p=128), Ss_up[g])

        def stream(src, Rsrc):
            i = wq[0] % NWB
            wq[0] += 1
            if id(Rsrc) not in converted:
                converted.add(id(Rsrc))
                fsrc, Ssrc = f32src[id(Rsrc)]
                P.op("pool", lambda e: e.dma_start(out=wbuf[:, i], in_=fsrc), writes=[Rwb[i]], slot=Swbp[i], n=2 << 20)
                P.op("sp", lambda e: e.dma_start(out=src, in_=wbuf[:, i]), reads=[Rwb[i]], writes=[Rsrc], slot=Ssrc)
            else:
                P.op("sp", lambda e: e.dma_start(out=wbuf[:, i], in_=src), reads=[Rsrc], writes=[Rwb[i]], slot=Swb[i])
            return i

        zq = [0]

        pzz = [pz[0], pz[1], pt[0], pt[1]]
        Rpzz = [Rpz[0], Rpz[1], Rpt[0], Rpt[1]]

        def nextz():
            i = zq[0] % 4
            zq[0] += 1
            return i

        Gv = big[:, 0:8192].bitcast(F32).rearrange("p (c t) -> p c t", t=512)
        sgb = big[:, 8192:12288].rearrange("p (c t) -> p c t", t=512)
        qT = big[:, 12288:16384].rearrange("p (c t) -> p c t", t=512)
        RG = [Res("G%d" % i) for i in range(8)]; Rsgb = [Res("sgb%d" % i) for i in range(8)]; RqT = [Res("qT%d" % i) for i in range(8)]

        def col(base, n):
            return cst[:, base + n:base + n + 1]

        def vcol(base, n):
            return vecs[:, base + n:base + n + 1]

        def tile(t, T, j):
            P.defn = T
            sample = (j == 1)
            xsrc = xs if sample else xp[t * 512:(t + 1) * 512, :]
            ydst = ys if sample else yp[t * 512:(t + 1) * 512, :]
            NS = max(1, T // 128)
            RW = min(T, 128)
            last = sample or (t == NT - 1)
            P.fence(RhT, RG + Rsgb + RqT)

            if sample:
                for s in range(4):
                    b_ = s % 2
                    P.op("sp", lambda e, s=s, b_=b_: e.dma_start(out=xst[:, b_, :], in_=ck[s * 128:(s + 1) * 128, :]), writes=[Rxst[b_]], slot=Sxst[b_])
                    for half in range(2):
                        pi_ = half
                        for f4 in range(4):
                            fc = half * 4 + f4
                            P.op("pe", lambda e, b_=b_, fc=fc, f4=f4, pi_=pi_: e.transpose(pt[pi_][:, f4 * 128:(f4 + 1) * 128], xst[:, b_, fc * 128:(fc + 1) * 128], identF),
                                 reads=[Rxst[b_], Rid], writes=[Rpt[pi_]])
                        P.op("act", lambda e, half=half, s=s, pi_=pi_: e.activation(out=kTr[:, half * 4:(half + 1) * 4, s * 128:(s + 1) * 128],
                                                                                   in_=pt[pi_].rearrange("p (f t) -> p f t", t=128), func=AF.Copy),
                             writes=[Rpt[pi_]] + RkT[half * 4:(half + 1) * 4])
                    P.op("pool", lambda e, s=s: e.dma_start(out=Vr[:, s, :, 0:128], in_=cv[s * 128:(s + 1) * 128, :].rearrange("p (h d) -> p h d", d=128)),
                         writes=[RV[s]], slot=Scv[s], n=1 << 19)
                P.op("pool", lambda e: e.tensor_copy(hist.rearrange("p c j -> p j c"), cvec[:, C_CONV:C_CONV + 24].rearrange("p (j c) -> p j c", j=3)),
                     reads=[Rvec], writes=Rhist)
                P.op("pool", lambda e: e.tensor_copy(hst, cvec[:, C_LRU:C_LRU + 8]), reads=[Rvec], writes=Rhst)
            elif t == 0:
                P.op("pool", lambda e: e.memset(hist.rearrange("p c j -> p (c j)"), 0.0), writes=Rhist)
                P.op("pool", lambda e: e.memset(hst, 0.0), writes=Rhst)

            for s in range(NS):
                b_ = s % 2
                P.op("sp", lambda e, s=s, b_=b_: e.dma_start(out=xst[0:RW, b_, :], in_=xsrc[s * 128:s * 128 + RW, :]), writes=[Rxst[b_]], slot=Sxst[b_])
                for half in range(2):
                    pi_ = half
                    for f4 in range(4):
                        fc = half * 4 + f4
                        P.op("pe", lambda e, b_=b_, fc=fc, f4=f4, pi_=pi_: e.transpose(pt[pi_][:, f4 * 128:f4 * 128 + RW], xst[0:RW, b_, fc * 128:(fc + 1) * 128], identF[0:RW, 0:RW]),
                             reads=[Rxst[b_], Rid], writes=[Rpt[pi_]], n=256)
                    P.op("act", lambda e, half=half, s=s, pi_=pi_: e.activation(out=resid[:, half * 4:(half + 1) * 4, s * 128:s * 128 + RW],
                                                                               in_=pt[pi_].rearrange("p (f t) -> p f t", t=128)[:, :, 0:RW], func=AF.Copy, scale=ALPHA),
                         writes=[Rpt[pi_]] + Rres[half * 4:(half + 1) * 4])
            for fc in range(8):
                P.op("dve", lambda e, fc=fc: e.tensor_scalar(uT[:, fc, 0:T], resid[:, fc, 0:T], col(CJ(j, 0), fc), mod[:, fc, j:j + 1], ALU.mult, ALU.add),
                     reads=[Rres[fc], Rcst, Rmod], writes=[RuT[fc]])

            def proj_block(wi, jb, rhs_of_k, Rrhs):
                z = nextz()
                for k in range(8):
                    P.op("pe", lambda e, k=k, z=z: e.matmul(pzz[z][:, 0:T], wbuf[:, wi, k, jb * 128:(jb + 1) * 128], rhs_of_k(k), start=(k == 0), stop=(k == 7)),
                         reads=[Rwb[wi], Rrhs[k]], writes=[Rpzz[z]])
                return z

            uk = lambda k: uT[:, k, 0:T]
            XC, RA, IB, AM = 0, 1, 2, 3

            def chainA(fc, wi, jb):
                xb_ = fc % 2
                tp_ = fc % 2
                Rt = Rtg[tp_]
                tgp = tg[:, tp_]
                z = proj_block(wi, jb, uk, RuT)
                P.op("pool", lambda e: e.tensor_copy(xrb[:, xb_, 0:3], hist[:, fc, :]), reads=[Rhist[fc]], writes=[Rxrb[xb_]], n=8)
                P.op("act", lambda e: e.activation(out=xrb[:, xb_, 3:3 + T], in_=pzz[z][:, 0:T], func=AF.Copy), writes=[Rpzz[z], Rxrb[xb_]])
                P.op("pool", lambda e: e.tensor_copy(hist[:, fc, :], xrb[:, xb_, T:T + 3]), reads=[Rxrb[xb_]], writes=[Rhist[fc]], n=8)
                P.op("dve", lambda e: e.tensor_scalar(tgp[:, XC, 0:T], xrb[:, xb_, 0:T], vcol(V_CW, fc), vcol(V_CB, fc), ALU.mult, ALU.add),
                     reads=[Rxrb[xb_], Rvec], writes=[Rt[XC]])
                for jj in (1, 2, 3):
                    P.op("dve", lambda e, jj=jj: e.scalar_tensor_tensor(tgp[:, XC, 0:T], xrb[:, xb_, jj:jj + T], vcol(V_CW + 8 * jj, fc), tgp[:, XC, 0:T], ALU.mult, ALU.add),
                         reads=[Rxrb[xb_], Rvec], writes=[Rt[XC]])
                P.op("act", lambda e: e.activation(out=xcb[:, tp_, 0:T], in_=tgp[:, XC, 0:T], func=AF.Copy), reads=[Rt[XC]], writes=[Rxcb[tp_]])
                P.op("pe", lambda e: e.matmul(pa[0][:, 0:T], wrg[:, fc, :], xcb[:, tp_, 0:T], start=True, stop=True), reads=[Rwg, Rxcb[tp_]], writes=[Rpa[0]], delay=5000.0)
                P.op("pe", lambda e: e.matmul(pa[1][:, 0:T], wig[:, fc, :], xcb[:, tp_, 0:T], start=True, stop=True), reads=[Rwg, Rxcb[tp_]], writes=[Rpa[1]])
                P.op("act", lambda e: e.activation(out=tgp[:, RA, 0:T], in_=pa[0][:, 0:T], func=AF.Tanh, bias=col(HBR, fc), scale=0.5), reads=[Rcst], writes=[Rpa[0], Rt[RA]], tbl="tanh")
                P.op("act", lambda e: e.activation(out=tgp[:, IB, 0:T], in_=pa[1][:, 0:T], func=AF.Tanh, bias=col(HBI, fc), scale=0.5), reads=[Rcst], writes=[Rpa[1], Rt[IB]], tbl="tanh")
                P.op("act", lambda e: e.activation(out=tgp[:, RA, 0:T], in_=tgp[:, RA, 0:T], func=AF.Exp, scale=col(CL2, fc), bias=col(CL2, fc)), reads=[Rcst], writes=[Rt[RA]], tbl="exp")
                P.op("pool", lambda e: e.tensor_tensor(tgp[:, AM, 0:T], tgp[:, RA, 0:T], tgp[:, RA, 0:T], ALU.mult), reads=[Rt[RA]], writes=[Rt[AM]])
                P.op("dve", lambda e: e.scalar_tensor_tensor(tgp[:, IB, 0:T], tgp[:, IB, 0:T], 1.0, tgp[:, XC, 0:T], ALU.add, ALU.mult), reads=[Rt[XC]], writes=[Rt[IB]])

            def chainB(fc):
                tp_ = fc % 2
                Rt = Rtg[tp_]
                tgp = tg[:, tp_]
                if (not sample) and t == 0:
                    P.op("pool", lambda e: e.memset(tgp[:, AM, 0:1], 0.5), writes=[Rt[AM]], n=1)
                    P.op("pool", lambda e: e.memset(tgp[:, RA, 0:1], 0.0), writes=[Rt[RA]], n=1)
                P.op("pool", lambda e: e.tensor_tensor(tgp[:, IB, 0:T], tgp[:, IB, 0:T], tgp[:, AM, 0:T], ALU.mult), reads=[Rt[AM]], writes=[Rt[IB]])
                P.op("dve", lambda e: e.tensor_tensor_scan(Gv[:, fc, 0:T], tgp[:, RA, 0:T], tgp[:, IB, 0:T], hst[:, fc:fc + 1], ALU.mult, ALU.add),
                     reads=[Rt[RA], Rt[IB], Rhst[fc]], writes=[RG[fc]], n=2 * T)
                P.op("pool", lambda e: e.tensor_copy(hst[:, fc:fc + 1], Gv[:, fc, T - 1:T]), reads=[RG[fc]], writes=[Rhst[fc]], n=1)

            def emit_xr(g, jp):
                wi = stream(win_b[g], Rs_in[g])
                if True:
                    fcs = (g * 4 + 2 * jp, g * 4 + 2 * jp + 1)
                    for fc in fcs:
                        chainA(fc, wi, fc % 4)
                    P.op("act", lambda e: e.activation(out=tg[:, :, AM, 0:T], in_=tg[:, :, AM, 0:T], func=AF.Sqrt, bias=0.25, scale=-0.25),
                         writes=[Rtg[0][AM], Rtg[1][AM]], tbl="sqrt", n=2 * T)
                    for fc in fcs:
                        chainB(fc)
            def emit_gl(g):
                wi = stream(win_b[g], Rs_in[g])
                for jb in range(4):
                    fc = (g - 2) * 4 + jb
                    z = proj_block(wi, jb, uk, RuT)
                    sp_ = fc % 2
                    P.op("act", lambda e, z=z, sp_=sp_: e.activation(out=tgs[:, sp_, 0:T], in_=pzz[z][:, 0:T], func=AF.Gelu_apprx_tanh), writes=[Rpzz[z], Rtgs[sp_]], tbl="gelu")
                    P.op("pool", lambda e, fc=fc, sp_=sp_: e.tensor_tensor(Gv[:, fc, 0:T], Gv[:, fc, 0:T], tgs[:, sp_, 0:T], ALU.mult), reads=[Rtgs[sp_]], writes=[RG[fc]])
            def emit_ga(g):
                wi = stream(win_b[g], Rs_in[g])
                for jb in range(4):
                    fc = (g - 10) * 4 + jb
                    z = proj_block(wi, jb, uk, RuT)
                    sp_ = fc % 2
                    P.op("act", lambda e, z=z, sp_=sp_: e.activation(out=tgs[:, sp_, 0:T], in_=pzz[z][:, 0:T], func=AF.Tanh, scale=0.5), writes=[Rpzz[z], Rtgs[sp_]], tbl="tanh")
                    P.op("dve", lambda e, fc=fc, sp_=sp_: e.scalar_tensor_tensor(Gv[:, fc, 0:T], tgs[:, sp_, 0:T], 1.0, Gv[:, fc, 0:T], ALU.add, ALU.mult), reads=[Rtgs[sp_]], writes=[RG[fc]])
            def emit_gb(g):
                wi = stream(win_b[g], Rs_in[g])
                for jb in range(4):
                    fc = (g - 12) * 4 + jb
                    z = proj_block(wi, jb, uk, RuT)
                    P.op("act", lambda e, z=z, fc=fc: e.activation(out=sgb[:, fc, 0:T], in_=pzz[z][:, 0:T], func=AF.Tanh, scale=0.5), writes=[Rpzz[z], Rsgb[fc]], tbl="tanh")
            ro = 512 if sample else (t % 2) * 512
            def emit_q(g):
                wi = stream(win_b[g], Rs_in[g])
                for jb in range(4):
                    h = (g - 4) * 4 + jb
                    z = proj_block(wi, jb, uk, RuT)
                    P.op("dve", lambda e, z=z, h=h: e.tensor_scalar(qT[:, h, 0:T], pzz[z][:, 0:T], ATT_SCALE, None, ALU.mult), writes=[Rpzz[z], RqT[h]])
            def emit_k(g):
                wi = stream(win_b[g], Rs_in[g])
                for jb in range(4):
                    h = (g - 6) * 4 + jb
                    z = proj_block(wi, jb, uk, RuT)
                    P.op("dve", lambda e, z=z, h=h: e.tensor_copy(kTr[:, h, ro:ro + T], pzz[z][:, 0:T]), writes=[Rpzz[z], RkT[h]])
                if last:
                    for s in range(NS):
                        z = nextz()
                        for k in range(8):
                            P.op("pe", lambda e, k=k, z=z, s=s, wi=wi: e.matmul(pzz[z][0:RW, :], uT[:, k, s * 128:s * 128 + RW], wbuf[:, wi, k, :], start=(k == 0), stop=(k == 7)),
                                 reads=[Rwb[wi]] + RuT, writes=[Rpzz[z]])
                        ob = (s + g) % 2
                        P.op("dve", lambda e, z=z, ob=ob: e.tensor_copy(ost[0:RW, ob, 0:512], pzz[z][0:RW, :]), writes=[Rpzz[z], Rost[ob]])
                        dst = (nks if sample else nkp)[s * 128:s * 128 + RW, (g - 6) * 512:(g - 5) * 512]
                        P.op("sp", lambda e, ob=ob, dst=dst: e.dma_start(out=dst, in_=ost[0:RW, ob, 0:512]), reads=[Rost[ob]], slot=Sost[ob], n=1 << 18)
            def emit_v(g):
                wi = stream(win_b[g], Rs_in[g])
                for s in range(NS):
                    z = nextz()
                    for k in range(8):
                        P.op("pe", lambda e, k=k, z=z, s=s, wi=wi: e.matmul(pzz[z][0:RW, :], uT[:, k, s * 128:s * 128 + RW], wbuf[:, wi, k, :], start=(k == 0), stop=(k == 7)),
                             reads=[Rwb[wi]] + RuT, writes=[Rpzz[z]])
                    rb = 4 if sample else (t % 2) * 4 + s
                    P.op("dve", lambda e, z=z, rb=rb, g=g: e.tensor_copy(Vr[0:RW, rb, (g - 8) * 4:(g - 7) * 4, 0:128],
                                                                       pzz[z][0:RW, :].rearrange("p (h d) -> p h d", d=128)),
                         writes=[Rpzz[z], RV[rb]])
                    if last:
                        ob = (s + g) % 2
                        P.op("dve", lambda e, z=z, ob=ob: e.tensor_copy(ost[0:RW, ob, 0:512], pzz[z][0:RW, :]), writes=[Rpzz[z], Rost[ob]])
                        dst = (nvs if sample else nvp)[s * 128:s * 128 + RW, (g - 8) * 512:(g - 7) * 512]
                        P.op("sp", lambda e, ob=ob, dst=dst: e.dma_start(out=dst, in_=ost[0:RW, ob, 0:512]), reads=[Rost[ob]], slot=Sost[ob], n=1 << 18)
            emit_xr(0, 0); emit_q(4); emit_q(5)
            emit_xr(0, 1); emit_k(6); emit_k(7)
            emit_xr(1, 0); emit_v(8); emit_v(9)
            emit_xr(1, 1); emit_gb(12); emit_gb(13)
            emit_gl(2); emit_gl(3); emit_ga(10); emit_ga(11)
            if last:
                P.op("sp", lambda e: e.dma_start(out=(ncs if sample else ncp), in_=hist.rearrange("p c j -> p (c j)")), reads=Rhist, slot=Sout[0 if sample else 2], n=4096)
                P.op("sp", lambda e: e.dma_start(out=(nhs if sample else nhp), in_=hst), reads=Rhst, slot=Sout[1 if sample else 3], n=4096)

            NP = 1 if sample else 4
            QW = 64 if sample else 128
            it = 0
            Sset = [(pa[0], pa[1], Rpa[0], Rpa[1]), (pt[0], pt[1], Rpt[0], Rpt[1])]
            Oset = [(po, Rpo), (pz[1], Rpz[1])]
            for pi in range(NP):
                for h in range(8):
                    bb = it % 2
                    it += 1
                    sM, sC, RsM, RsC = Sset[bb]
                    oB, RoB = Oset[bb]
                    blocks = []
                    for b in range(5):
                        if sample:
                            blocks.append((b, b))
                        else:
                            cs = 8 * t + 2 * pi - 8 + 2 * b
                            if cs >= 0:
                                blocks.append((b, (cs // 2) % 8))
                    mpos = {0: 0, 3: 1, 4: 2}
                    cpos = {1: 0, 2: 1}
                    q_ap = qT[:, h, pi * 128:pi * 128 + QW]
                    for b, rb in blocks:
                        if b in mpos:
                            o_ap = sM[:, mpos[b] * 128:mpos[b] * 128 + QW]; R_ = RsM
                        else:
                            o_ap = sC[:, cpos[b] * 128:cpos[b] * 128 + QW]; R_ = RsC
                        P.op("pe", lambda e, o_ap=o_ap, h=h, rb=rb, q_ap=q_ap: e.matmul(o_ap, kTr[:, h, rb * 128:(rb + 1) * 128], q_ap, start=True, stop=True),
                             reads=[RkT[h], RqT[h]], writes=[R_], n=QW + 64)
                    mb = [b for b, _ in blocks if b in mpos]
                    cb = [b for b, _ in blocks if b in cpos]
                    m0 = mpos[mb[0]]
                    nm = len(mb)
                    P.op("act", lambda e, bb=bb, m0=m0, nm=nm, sM=sM: e.activation(
                        out=sbm[:, bb, m0:m0 + nm, 0:QW], in_=sM.rearrange("p (b q) -> p b q", q=128)[:, m0:m0 + nm, 0:QW], func=AF.Exp),
                        writes=[RsM, Rsbm[bb]], n=nm * QW, tbl="exp")
                    P.op("pool", lambda e, bb=bb, h=h, m0=m0, nm=nm: e.tensor_tensor(
                        pT[:, bb, m0:m0 + nm, 0:QW], sbm[:, bb, m0:m0 + nm, 0:QW], biasM[:, h, m0:m0 + nm, 0:QW], ALU.mult),
                        reads=[Rvec, Rsbm[bb]], writes=[RpT[bb]], n=nm * QW)
                    if cb:
                        c0_ = cpos[cb[0]]
                        ncb = len(cb)
                        P.op("act", lambda e, bb=bb, h=h, c0_=c0_, ncb=ncb, sC=sC: e.activation(
                            out=pT[:, bb, 3 + c0_:3 + c0_ + ncb, 0:QW], in_=sC.rearrange("p (b q) -> p b q", q=128)[:, c0_:c0_ + ncb, 0:QW],
                            func=AF.Exp, bias=cbias[:, h:h + 1], scale=1.0),
                            reads=[Rvec], writes=[RsC, RpT[bb]], n=ncb * QW, tbl="exp")
                    for i_, (b, rb) in enumerate(blocks):
                        pidx = mpos[b] if b in mpos else 3 + cpos[b]
                        P.op("pe", lambda e, bb=bb, pidx=pidx, rb=rb, h=h, i_=i_, nb=len(blocks), oB=oB: e.matmul(
                            oB[0:QW, 0:129], pT[:, bb, pidx, 0:QW], Vr[:, rb, h, :], start=(i_ == 0), stop=(i_ == nb - 1)),
                            reads=[RpT[bb], RV[rb]], writes=[RoB], n=129 + 64)
                    P.op("dve", lambda e, bb=bb, oB=oB: e.reciprocal(rc[0:QW, bb:bb + 1], oB[0:QW, 128:129]), writes=[RoB, Rrc[bb]], n=1)
                    P.op("dve", lambda e, bb=bb, oB=oB: e.tensor_scalar(osb[0:QW, bb, :], oB[0:QW, 0:128], rc[0:QW, bb:bb + 1], None, ALU.mult),
                         reads=[Rrc[bb]], writes=[RoB, Rosb[bb]], n=128)
                    P.op("pe", lambda e, bb=bb: e.transpose(pob[:, bb * 128:bb * 128 + QW], osb[0:QW, bb, :], identB[0:QW, 0:QW]), reads=[Rosb[bb], Rid], writes=[Rpob], n=128)
                    P.op("dve", lambda e, bb=bb, h=h, pi=pi: e.scalar_tensor_tensor(mtmp[:, bb, 0:QW], sgb[:, h, pi * 128:pi * 128 + QW], 1.0, pob[:, bb * 128:bb * 128 + QW], ALU.add, ALU.mult),
                         reads=[Rsgb[h]], writes=[Rpob, Rmt[bb]], n=QW)
                    P.op("dve", lambda e, bb=bb, h=h, pi=pi: e.tensor_tensor(uT[:, h, pi * 128:pi * 128 + QW], Gv[:, h, pi * 128:pi * 128 + QW], mtmp[:, bb, 0:QW], ALU.add),
                         reads=[Rmt[bb], RG[h]], writes=[RuT[h]], n=QW)

            def layernorm(emit_out, inplace=False):
                for n in range(8):
                    b2 = n % 2
                    P.op("dve", lambda e, n=n, b2=b2: e.tensor_copy(xb[:, b2, 0:T], resid[:, n, 0:T]), reads=[Rres[n]], writes=[Rxb[b2]])
                    P.op("act", lambda e, n=n, b2=b2: e.activation(out=sq[:, b2, 0:T], in_=resid[:, n, 0:T], func=AF.Square), reads=[Rres[n]], writes=[Rsq[b2]])
                    P.op("pe", lambda e, n=n, b2=b2: e.matmul(pa[0][:, 0:T], onesS, xb[:, b2, 0:T], start=(n == 0), stop=(n == 7)), reads=[Rid, Rxb[b2]], writes=[Rpa[0]])
                    P.op("pe", lambda e, n=n, b2=b2: e.matmul(pa[1][:, 0:T], onesS, sq[:, b2, 0:T], start=(n == 0), stop=(n == 7)), reads=[Rid, Rsq[b2]], writes=[Rpa[1]])
                P.op("act", lambda e: e.activation(out=lnv[:, 0:T], in_=pa[0][:, 0:T], func=AF.Square), writes=[Rpa[0], Rlnv])
                P.op("dve", lambda e: e.tensor_tensor(lnv[:, 0:T], pa[1][:, 0:T], lnv[:, 0:T], ALU.subtract), writes=[Rpa[1], Rlnv])
                P.op("act", lambda e: e.activation(out=lnv[:, 0:T], in_=lnv[:, 0:T], func=AF.Sqrt, bias=LN_EPS, scale=1.0), writes=[Rlnv], tbl="sqrt")
                P.op("dve", lambda e: e.reciprocal(lnv[:, 0:T], lnv[:, 0:T]), writes=[Rlnv])
                P.op("dve", lambda e: e.scalar_tensor_tensor(lnn[:, 0:T], pa[0][:, 0:T], -1.0, lnv[:, 0:T], ALU.mult, ALU.mult), reads=[Rlnv], writes=[Rpa[0], Rlnn])
                if inplace:
                    for n in range(8):
                        P.op("dve", lambda e, n=n: e.tensor_tensor(Gv[:, n, 0:T], resid[:, n, 0:T], lnv[:, 0:T], ALU.mult), reads=[Rres[n], Rlnv], writes=[RG[n], Rln2m])
                    for n in range(8):
                        P.op("pool", lambda e, n=n: e.tensor_tensor(Gv[:, n, 0:T], Gv[:, n, 0:T], lnn[:, 0:T], ALU.add), reads=[Rlnn, Rln2m], writes=[RG[n]])
                        emit_out(n, None)
                    return
                for n in range(8):
                    b2 = n % 2
                    P.op("dve", lambda e, n=n, b2=b2: e.tensor_tensor(lnt[:, b2, 0:T], resid[:, n, 0:T], lnv[:, 0:T], ALU.mult), reads=[Rres[n], Rlnv], writes=[Rlnt[b2]])
                    P.op("dve", lambda e, b2=b2: e.tensor_tensor(lnt[:, b2, 0:T], lnt[:, b2, 0:T], lnn[:, 0:T], ALU.add), reads=[Rlnn], writes=[Rlnt[b2]])
                    emit_out(n, b2)

            mk = lambda k: uT[:, k, 0:T]
            for g in range(2):
                wi = stream(wout_b[g], Rs_out[g])
                for jb in range(4):
                    n = g * 4 + jb
                    z = proj_block(wi, jb, mk, RuT)
                    P.op("dve", lambda e, z=z, n=n: e.scalar_tensor_tensor(resid[:, n, 0:T], pzz[z][:, 0:T], col(CJ(j, 1), n), resid[:, n, 0:T], ALU.mult, ALU.add),
                         reads=[Rcst], writes=[Rpzz[z], Rres[n]])

            def ln1_out(n, b2):
                P.op("act", lambda e, n=n, b2=b2: e.activation(out=u2T[:, n, 0:T], in_=lnt[:, b2, 0:T], func=AF.Identity, scale=col(CJ(j, 2), n), bias=col(CJ(j, 3), n)),
                     reads=[Rlnt[b2], Rcst], writes=[Ru2[n]])
                P.op("act", lambda e, n=n, b2=b2: e.activation(out=resid[:, n, 0:T], in_=lnt[:, b2, 0:T], func=AF.Identity, scale=col(GA, n), bias=col(CJ(j, 5), n)),
                     reads=[Rlnt[b2], Rcst], writes=[Rres[n]])
            layernorm(ln1_out)

            P.fence(RG + Rsgb + RqT, RhT)
            u2k = lambda k: u2T[:, k, 0:T]
            for g in range(8):
                wi = stream(wup_b[g], Rs_up[g])
                if g == 0:
                    zs0 = [nextz() for _ in range(4)]
                    for k in range(8):
                        for jb in range(4):
                            P.op("pe", lambda e, k=k, jb=jb, zz=zs0[jb], wi=wi: e.matmul(pzz[zz][:, 0:T], wbuf[:, wi, k, jb * 128:(jb + 1) * 128], u2T[:, k, 0:T], start=(k == 0), stop=(k == 7)),
                                 reads=[Rwb[wi], Ru2[k]], writes=[Rpzz[zs0[jb]]])
                for jb in range(4):
                    m = g * 4 + jb
                    z = zs0[jb] if g == 0 else proj_block(wi, jb, u2k, Ru2)
                    tb = m % 2
                    P.op("act", lambda e, z=z, m=m, tb=tb: e.activation(out=tgs[:, tb, 0:T], in_=pzz[z][:, 0:T], func=AF.Relu, bias=vcol(V_BUP, m), scale=1.0),
                         reads=[Rvec], writes=[Rpzz[z], Rtgs[tb]])
                    eng = "pool" if m % 2 == 0 else "dve"
                    P.op(eng, lambda e, m=m, tb=tb: e.tensor_tensor(hT[:, m, 0:T], tgs[:, tb, 0:T], tgs[:, tb, 0:T], ALU.mult), reads=[Rtgs[tb]], writes=[RhT[m]])
            for n in range(8):
                i = wq[0] % NWB
                wq[0] += 1
                wd = wbuf[:, i].rearrange("p k c -> p (k c)").rearrange("p (k c) -> p k c", c=128)
                if id(Rs_dn[n]) not in converted:
                    converted.add(id(Rs_dn[n]))
                    for kh in range(2):
                        P.op("pool", lambda e, n=n, wd=wd, kh=kh: e.dma_start(
                            out=wd[:, kh * 16:(kh + 1) * 16, :],
                            in_=w_down[kh * 2048:(kh + 1) * 2048, n * 128:(n + 1) * 128].rearrange("(k p) c -> p k c", p=128)),
                            writes=[Rwb[i]], slot=Swbp[i], n=1 << 20)
                    P.op("sp", lambda e, n=n, wd=wd: e.dma_start(out=wdn_b[n], in_=wd), reads=[Rwb[i]], writes=[Rs_dn[n]], slot=Ss_dn[n])
                else:
                    P.op("sp", lambda e, i=i, n=n: e.dma_start(out=wbuf[:, i].rearrange("p k c -> p (k c)"), in_=wdn_b[n].rearrange("p k c -> p (k c)")),
                         reads=[Rs_dn[n]], writes=[Rwb[i]], slot=Swb[i])
                z = nextz()
                for k in range(32):
                    P.op("pe", lambda e, k=k, z=z, wd=wd: e.matmul(pzz[z][:, 0:T], wd[:, k, :], hT[:, k, 0:T], start=(k == 0), stop=(k == 31)),
                         reads=[Rwb[i], RhT[k]], writes=[Rpzz[z]])
                P.op("dve", lambda e, z=z, n=n: e.scalar_tensor_tensor(resid[:, n, 0:T], pzz[z][:, 0:T], col(CJ(j, 4), n), resid[:, n, 0:T], ALU.mult, ALU.add),
                     reads=[Rcst], writes=[Rpzz[z], Rres[n]])

            def ln2_out(n, b2):
                P.op("act", lambda e, n=n: e.activation(out=Gv[:, n, 0:T], in_=Gv[:, n, 0:T], func=AF.Identity, scale=vcol(V_L2G, n), bias=vcol(V_L2B, n)),
                     reads=[Rvec], writes=[RG[n]])
            P.fence(RhT, RG)
            layernorm(ln2_out, inplace=True)

            for s in range(NS):
                ob = s % 2
                for half in range(2):
                    pi_ = half
                    for f4 in range(4):
                        fc = half * 4 + f4
                        P.op("pe", lambda e, s=s, fc=fc, f4=f4, pi_=pi_: e.transpose(pa[pi_][0:RW, f4 * 128:(f4 + 1) * 128], Gv[:, fc, s * 128:s * 128 + RW], identF),
                             reads=[RG[fc], Rid], writes=[Rpa[pi_]], n=256)
                    eng = "act" if half == 0 else "dve"
                    if eng == "act":
                        P.op("act", lambda e, ob=ob, half=half, pi_=pi_: e.activation(out=ost[0:RW, ob, half * 512:(half + 1) * 512], in_=pa[pi_][0:RW, :], func=AF.Copy),
                             writes=[Rpa[pi_], Rost[ob]])
                    else:
                        P.op("dve", lambda e, ob=ob, half=half, pi_=pi_: e.tensor_copy(ost[0:RW, ob, half * 512:(half + 1) * 512], pa[pi_][0:RW, :]),
                             writes=[Rpa[pi_], Rost[ob]])
                P.op("sp", lambda e, ob=ob, s=s: e.dma_start(out=ydst[s * 128:s * 128 + RW, :], in_=ost[0:RW, ob, :]), reads=[Rost[ob]], slot=Sost[ob])

        tile(None, 64, 1)
        for r_ in Rs_in + Rs_out + Rs_up + Rs_dn:
            r_.ro = True
        for t in range(NT):
            tile(t, 512, 0)

        P.final_slots = Sost + Sout
        P.emit()
        build.last = (P.est_ns, getattr(P, "n_tbl", 0), len(P.ops))
    return nc


_NC_CACHE = {}


def _pcol(v):
    v = np.asarray(v, np.float32).reshape(-1, 128)
    return np.ascontiguousarray(v.T)


def _host_layout(inputs, NT):
    f = lambda k: np.asarray(inputs[k], np.float32)
    conv_w = f("conv_w")[0]
    vecs = np.concatenate([
        _pcol(conv_w[0]), _pcol(conv_w[1]), _pcol(conv_w[2]), _pcol(conv_w[3]),
        _pcol(f("conv_b")[0]), _pcol(f("b_rg")[0].reshape(-1)), _pcol(f("b_ig")[0].reshape(-1)), _pcol(f("lru_lambda")[0]),
        _pcol(f("ln1_g")[0]), _pcol(f("ln1_b")[0]), _pcol(f("ln2_g")[0]), _pcol(f("ln2_b")[0]),
        _pcol(f("b_down")[0]), _pcol(f("b_up")[0]), _pcol(f("b_ada")[0])], axis=1)
    assert vecs.shape == (128, NV)
    table = f("rel_bias")[0]
    kr = np.arange(128)[:, None]; qc = np.arange(128)[None, :]
    qch = qc // 64
    biasM = np.empty((128, 8, 3, 128), np.float32)
    for i, b in enumerate((0, 3, 4)):
        kch = 2 * b - 8 + kr // 64
        kpos = (2 * b - 8) * 64 + kr
        rel = np.clip(qc - kpos, -128, 128) + 128
        vis = (kch <= qch) & (kch >= qch - 8)
        for h in range(8):
            biasM[:, h, i, :] = np.where(vis, table[h][rel], np.float32(NEG))
    cbias = np.ascontiguousarray(np.broadcast_to(table[:, 256][None, :], (128, 8))).astype(np.float32)
    shared = {
        "vecs": vecs, "biasM": np.ascontiguousarray(biasM.reshape(128, -1)), "cbias": cbias,
        "w_ada": np.ascontiguousarray(f("w_ada")[0]), "w_in": np.ascontiguousarray(f("w_in")[0]),
        "w_rg": np.ascontiguousarray(f("w_rg")[0]), "w_ig": np.ascontiguousarray(f("w_ig")[0]),
        "w_out": np.ascontiguousarray(f("w_out")[0]), "w_up": np.ascontiguousarray(f("w_up")[0]),
        "w_down": np.ascontiguousarray(f("w_down")[0]),
    }
    in_maps = []
    nb = f("x_prompt").shape[0]
    for b in range(nb):
        sc = f("state_conv")[0, b]
        cvec = np.concatenate([_pcol(f("c_prompt")[b]), _pcol(f("c_sample")[b]), _pcol(f("state_lru")[0, b]),
                               _pcol(sc[0]), _pcol(sc[1]), _pcol(sc[2])], axis=1)
        m = dict(shared)
        m.update({
            "xp": np.ascontiguousarray(f("x_prompt")[b, :NT * 512]), "xs": np.ascontiguousarray(f("x_sample")[b]),
            "ck": np.ascontiguousarray(f("cache_k")[0, b].reshape(512, 1024)),
            "cv": np.ascontiguousarray(f("cache_v")[0, b].reshape(512, 1024)),
            "cvec": np.ascontiguousarray(cvec),
        })
        in_maps.append(m)
    return in_maps


def _unp(a, n):
    return np.ascontiguousarray(a.T).reshape(-1)


def run(inputs, NT, cores=None):
    if NT not in _NC_CACHE:
        _NC_CACHE[NT] = build(NT)
    nc = _NC_CACHE[NT]
    in_maps = _host_layout(inputs, NT)
    if cores is not None:
        in_maps = [in_maps[c] for c in cores]
    res = run_bass_kernel_spmd(nc, in_maps, core_ids=list(range(len(in_maps))))
    R = res.results
    B = len(R)
    st = lambda k: np.stack([np.asarray(r[k], np.float32) for r in R])
    yp = st("yp"); ys = st("ys")
    nkp = st("nkp").reshape(1, B, 512, 8, 128); nvp = st("nvp").reshape(1, B, 512, 8, 128)
    nks = st("nks").reshape(1, B, 64, 8, 128); nvs = st("nvs").reshape(1, B, 64, 8, 128)

    def conv(k):
        a = st(k).reshape(B, 128, 8, 3)
        return np.ascontiguousarray(a.transpose(0, 3, 2, 1)).reshape(1, B, 3, 1024)

    def lru(k):
        a = st(k)
        return np.ascontiguousarray(a.transpose(0, 2, 1)).reshape(1, B, 1024)
    return (yp, ys, nkp, nvp, conv("ncp"), lru("nhp"), nks, nvs, conv("ncs"), lru("nhs"))


def kernel(**inputs):
    return run(inputs, 16)
```

```python
import numpy as np
from contextlib import ExitStack
import concourse.bass as bass
import concourse.mybir as mybir
from concourse.bass_utils import run_bass_kernel_spmd

F32 = mybir.dt.float32
BF16 = mybir.dt.bfloat16
AF = mybir.ActivationFunctionType
ALU = mybir.AluOpType

ALPHA = 2.0 ** 0.25
ATT_SCALE = 128.0 ** -0.5
LN_EPS = 1e-5
NEG = -30000.0


class Res:
    __slots__ = ("name", "w", "r", "ro")

    def __init__(self, name, ro=False):
        self.name = name
        self.w = None
        self.r = []
        self.ro = ro


class Slot:
    __slots__ = ("sem", "count")

    def __init__(self, sem):
        self.sem = sem
        self.count = 0


class Prog:
    ENG = ("pe", "act", "dve", "pool", "sp")
    LOOK = 24
    WIN = 150.0
    LAT_TO_PE = 3000.0
    LAT_FROM_PE = 300.0
    LAT_X = 600.0

    def __init__(self, nc, es):
        self.nc = nc
        self.es = es
        self.ops = []
        self.sem = {e: es.enter_context(nc.semaphore("s_" + e)) for e in self.ENG}
        self.nslot = 0
        self.final_slots = []
        self.defn = 512

    def slot(self):
        self.nslot += 1
        return Slot(self.es.enter_context(self.nc.semaphore("d%d" % self.nslot)))

    def cost(self, eng, slot, n):
        if slot is not None:
            return float(n if n is not None else 1 << 20)
        n = self.defn if n is None else n
        if eng == "pe":
            return n / 2.4 + 10.0
        if eng == "act":
            return 230.0 + n / 1.15
        if eng == "dve":
            return 120.0 + n / 0.9
        return 300.0 + n / 0.45

    SERVED = {0: ("exp", "tanh"), 2: ("tanh", "sigmoid"), 3: ("sqrt",), 11: ("gelu", "tanh"), 5: ("ln",), 18: ("silu", "tanh")}
    LOWEST = {"exp": 0, "tanh": 0, "sigmoid": 2, "sqrt": 3, "gelu": 11, "ln": 5, "silu": 18}

    def op(self, eng, fn, reads=(), writes=(), slot=None, n=None, tbl=None, delay=0.0):
        idx = len(self.ops)
        deps = set()
        for r in reads:
            if r.w is not None:
                deps.add(r.w)
        for w in writes:
            if w.w is not None:
                deps.add(w.w)
            deps.update(w.r)
        self.ops.append((eng, fn, slot, deps, self.cost(eng, slot, n), tbl, delay))
        for r in reads:
            if not r.ro:
                r.r.append(idx)
        for w in writes:
            w.w = idx
            w.r = []
        return idx

    def fence(self, frm, to):
        toks = []
        for f in frm:
            if f.w is not None:
                toks.append(f.w)
            toks.extend(f.r)
        for t in to:
            t.r.extend(toks)

    def schedule(self):
        import bisect
        ops = self.ops
        N = len(ops)
        succ = [[] for _ in range(N)]
        indeg = [0] * N
        for i, o in enumerate(ops):
            for d in o[3]:
                succ[d].append(i)
            indeg[i] = len(o[3])
        ready = {e: [] for e in self.ENG}
        rtime = [0.0] * N
        fin = [0.0] * N
        efree = {e: 0.0 for e in self.ENG}
        order = {e: [] for e in self.ENG}
        for i in range(N):
            if indeg[i] == 0:
                ready[ops[i][0]].append(i)
        bw_free = 0.0
        left = N
        LOOK, WIN = self.LOOK, self.WIN
        cur_set = -1
        SERVED, LOWEST = self.SERVED, self.LOWEST
        while left:
            best = None
            for e in self.ENG:
                L = ready[e]
                if not L:
                    continue
                te = efree[e]
                c = None
                cr = None
                cfall = None
                for i in L[:LOOK]:
                    rt = rtime[i]
                    if rt <= te + WIN:
                        if e == "act":
                            tb = ops[i][5]
                            if tb is not None and (cur_set < 0 or tb not in SERVED[cur_set]):
                                if cfall is None:
                                    cfall = i
                                continue
                        c = i
                        break
                    if cr is None or rt < cr:
                        cr = rt
                        c2 = i
                if c is None:
                    c = cfall if cfall is not None else c2
                st = te if rtime[c] < te else rtime[c]
                if best is None or (st, c) < (best[0], best[2]):
                    best = (st, e, c)
            st, e, c = best
            L = ready[e]
            L.pop(bisect.bisect_left(L, c))
            o = ops[c]
            if o[2] is not None:
                issue = 60.0 if e == "sp" else 600.0
                efree[e] = st + issue
                b0 = bw_free if bw_free > st else st
                bw_free = b0 + o[4] / 160.0
                f = max(st + 2000.0, bw_free)
            else:
                dur = o[4]
                if e == "act" and o[5] is not None and (cur_set < 0 or o[5] not in SERVED[cur_set]):
                    cur_set = LOWEST[o[5]]
                    dur += 1283.0
                    self.n_tbl = getattr(self, "n_tbl", 0) + 1
                efree[e] = st + dur
                f = st + dur + (60.0 if e == "pe" else 0.0)
            fin[c] = f
            order[e].append(c)
            left -= 1
            for s_ in succ[c]:
                ce = ops[s_][0]
                if o[2] is not None:
                    lat = f + 200.0
                elif ce == e:
                    lat = f
                elif ce == "pe":
                    lat = f + self.LAT_TO_PE
                elif e == "pe":
                    lat = f + self.LAT_FROM_PE
                else:
                    lat = f + self.LAT_X
                if lat > rtime[s_]:
                    rtime[s_] = lat
                indeg[s_] -= 1
                if indeg[s_] == 0:
                    rtime[s_] += ops[s_][6]
                    bisect.insort(ready[ops[s_][0]], s_)
        self.est_ns = max(fin) if fin else 0.0
        return order

    def emit(self):
        nc = self.nc
        ops = self.ops
        order = self.schedule()
        tok = [None] * len(ops)
        sig = [False] * len(ops)
        for i, o in enumerate(ops):
            for d in o[3]:
                if ops[d][2] is None and not (o[0] == "pe" and ops[d][0] == "pe"):
                    sig[d] = True
        cnt = {e: 0 for e in self.ENG}
        for e in self.ENG:
            for i in order[e]:
                sl = ops[i][2]
                if sl is None:
                    if sig[i]:
                        cnt[e] += 1
                        tok[i] = (self.sem[e], cnt[e], e)
                else:
                    sl.count += 16
                    tok[i] = (sl.sem, sl.count, None)
        self.n_sig = sum(sig)

        def run(eng, name):
            waited = {}
            own = self.sem[name]
            for i in order[name]:
                o = ops[i]
                need = {}
                for d in o[3]:
                    if name == "pe" and ops[d][0] == "pe" and ops[d][2] is None:
                        continue
                    s, v, de = tok[d]
                    if need.get(s, 0) < v:
                        need[s] = v
                for s, v in need.items():
                    if waited.get(s, 0) < v:
                        waited[s] = v
                        eng.wait_ge(s, v)
                ins = o[1](eng)
                t = tok[i]
                if t is not None:
                    ins.then_inc(t[0], 16 if o[2] is not None else 1)
            if name == "sp":
                for sl in self.final_slots:
                    eng.wait_ge(sl.sem, sl.count)

        with nc.Block() as block:
            @block.tensor
            def _(e):
                run(e, "pe")

            @block.scalar
            def _(e):
                run(e, "act")

            @block.vector
            def _(e):
                run(e, "dve")

            @block.gpsimd
            def _(e):
                run(e, "pool")

            @block.sync
            def _(e):
                run(e, "sp")


V_CW, V_CB, V_BRG, V_BIG, V_LAM, V_L1G, V_L1B, V_L2G, V_L2B, V_BD, V_BUP, V_BADA, NV = \
    0, 32, 40, 48, 56, 64, 72, 80, 88, 96, 104, 136, 184
C_C, C_LRU, C_CONV, NCV = 0, 16, 24, 48


def build(NT):
    nc = bass.Bass("TRN2", target_bir_lowering=False)
    SEQ = NT * 512

    def din(name, shape, dt=F32):
        return nc.dram_tensor(name, shape, dt, kind="ExternalInput").ap()

    def dout(name, shape):
        return nc.dram_tensor(name, shape, F32, kind="ExternalOutput").ap()

    xp = din("xp", [SEQ, 1024]); xs = din("xs", [64, 1024])
    ck = din("ck", [512, 1024]); cv = din("cv", [512, 1024])
    vecs_d = din("vecs", [128, NV]); cvec_d = din("cvec", [128, NCV])
    biasM_d = din("biasM", [128, 8 * 3 * 128]); cbias_d = din("cbias", [128, 8])
    w_ada = din("w_ada", [1024, 6144]); w_in = din("w_in", [1024, 7168])
    w_rg = din("w_rg", [8, 128, 128]); w_ig = din("w_ig", [8, 128, 128])
    w_out = din("w_out", [1024, 1024]); w_up = din("w_up", [1024, 4096]); w_down = din("w_down", [4096, 1024])
    yp = dout("yp", [SEQ, 1024]); ys = dout("ys", [64, 1024])
    nkp = dout("nkp", [512, 1024]); nvp = dout("nvp", [512, 1024])
    ncp = dout("ncp", [128, 24]); nhp = dout("nhp", [128, 8])
    nks = dout("nks", [64, 1024]); nvs = dout("nvs", [64, 1024])
    ncs = dout("ncs", [128, 24]); nhs = dout("nhs", [128, 8])
    win_b = nc.dram_tensor("win_b", [14, 128, 8, 512], BF16, kind="Internal").ap()
    wout_b = nc.dram_tensor("wout_b", [2, 128, 8, 512], BF16, kind="Internal").ap()
    wup_b = nc.dram_tensor("wup_b", [8, 128, 8, 512], BF16, kind="Internal").ap()
    wdn_b = nc.dram_tensor("wdn_b", [8, 128, 32, 128], BF16, kind="Internal").ap()

    with ExitStack() as es:
        P = Prog(nc, es)

        def sb(name, shape, dt=F32):
            return nc.alloc_sbuf_tensor(name, shape, dt).ap()

        xst = sb("xst", [128, 2, 1024]); Rxst = [Res("xst0"), Res("xst1")]; Sxst = [P.slot(), P.slot()]
        ost = sb("ost", [128, 2, 1024]); Rost = [Res("ost0"), Res("ost1")]; Sost = [P.slot(), P.slot()]
        resid = sb("resid", [128, 8, 512]); Rres = [Res("res%d" % i) for i in range(8)]
        uT = sb("uT", [128, 8, 512], BF16); RuT = [Res("uT%d" % i) for i in range(8)]
        u2T = sb("u2T", [128, 8, 512], BF16); Ru2 = [Res("u2T%d" % i) for i in range(8)]
        big = sb("big", [128, 16384], BF16)
        hT = big.rearrange("p (m t) -> p m t", t=512); RhT = [Res("hT%d" % i) for i in range(32)]
        kTr = sb("kTr", [128, 8, 1024], BF16); RkT = [Res("kT%d" % i) for i in range(8)]
        Vr = sb("Vr", [128, 8, 8, 129], BF16); RV = [Res("V%d" % i) for i in range(8)]
        biasM = sb("biasM_s", [128, 8, 3, 128]); cbias = sb("cbias_s", [128, 8])
        pT = sb("pT", [128, 2, 5, 128], BF16); RpT = [Res("pT0"), Res("pT1")]
        sbm = sb("sbm", [128, 2, 3, 128]); Rsbm = [Res("sbm0"), Res("sbm1")]
        osb = sb("osb", [128, 2, 128], BF16); Rosb = [Res("osb0"), Res("osb1")]
        rc = sb("rc", [128, 2]); Rrc = [Res("rc0"), Res("rc1")]
        mtmp = sb("mtmp", [128, 2, 128]); Rmt = [Res("mt0"), Res("mt1")]
        xb = sb("xb", [128, 2, 512], BF16); Rxb = [Res("xb0"), Res("xb1")]
        sq = sb("sq", [128, 2, 512], BF16); Rsq = [Res("sq0"), Res("sq1")]
        lnm = sb("lnm", [128, 512]); lnv = sb("lnv", [128, 512]); lnn = sb("lnn", [128, 512])
        lnt = sb("lnt", [128, 2, 512]); Rlnt = [Res("lnt0"), Res("lnt1")]
        Rlnm, Rlnv, Rlnn = Res("lnm"), Res("lnv"), Res("lnn")
        Rln2m = Res("ln2m")
        tg = sb("tg", [128, 2, 4, 512]); Rtg = [[Res("tg%d_%d" % (p_, i)) for i in range(4)] for p_ in range(2)]
        tgs = sb("tgs", [128, 2, 512]); Rtgs = [Res("tgs0"), Res("tgs1")]
        xcb = sb("xcb", [128, 2, 512], BF16); Rxcb = [Res("xcb0"), Res("xcb1")]
        xrb = sb("xrb", [128, 2, 515]); Rxrb = [Res("xrb0"), Res("xrb1")]
        hist = sb("hist", [128, 8, 3]); Rhist = [Res("hist%d" % i) for i in range(8)]
        hst = sb("hst", [128, 8]); Rhst = [Res("hst%d" % i) for i in range(8)]
        NWB = 3
        wbuf = sb("wbuf", [128, NWB, 8, 512], BF16); Rwb = [Res("wb%d" % i) for i in range(NWB)]; Swb = [P.slot() for _ in range(NWB)]; Swbp = [P.slot() for _ in range(NWB)]
        wrg = sb("wrg", [128, 8, 128], BF16); wig = sb("wig", [128, 8, 128], BF16); Rwg = Res("wg"); Swg = P.slot()
        vecs = sb("vecs_s", [128, NV]); cvec = sb("cvec_s", [128, NCV]); Rvec = Res("vec"); Svec = P.slot()
        mod = sb("mod", [128, 48, 2]); Rmod = Res("mod")
        cst = sb("cst", [128, 136]); Rcst = Res("cst")
        csil = sb("csil", [128, 8, 2]); Rcsil = Res("csil")
        identF = sb("identF", [128, 128]); identB = sb("identB", [128, 128], BF16); onesS = sb("onesS", [128, 128], BF16)
        Rid = Res("ident")

        def pb(name, dt=F32, n=512):
            return nc.alloc_psum_tensor(name, [128, n], dt).ap()
        pt = [pb("pt0"), pb("pt1")]; Rpt = [Res("pt0"), Res("pt1")]
        pz = [pb("pz0"), pb("pz1")]; Rpz = [Res("pz0"), Res("pz1")]
        pa = [pb("pa0"), pb("pa1")]; Rpa = [Res("pa0"), Res("pa1")]
        po = pb("po"); Rpo = Res("po")
        pob = pb("pob", BF16, 1024); Rpob = Res("pob")

        Rs_in = [Res("s_in%d" % g) for g in range(14)]; Ss_in = [P.slot() for _ in range(14)]
        Rs_out = [Res("s_out%d" % g) for g in range(2)]; Ss_out = [P.slot() for _ in range(2)]
        Rs_up = [Res("s_up%d" % g) for g in range(8)]; Ss_up = [P.slot() for _ in range(8)]
        Rs_dn = [Res("s_dn%d" % g) for g in range(8)]; Ss_dn = [P.slot() for _ in range(8)]
        Scv = [P.slot() for _ in range(4)]
        Sout = [P.slot() for _ in range(4)]

        P.op("sp", lambda e: e.dma_start(out=vecs, in_=vecs_d), writes=[Rvec], slot=Svec)
        P.op("sp", lambda e: e.dma_start(out=cvec, in_=cvec_d), writes=[Rvec], slot=Svec)
        P.op("sp", lambda e: e.dma_start(out=biasM.rearrange("p a b c -> p (a b c)"), in_=biasM_d), writes=[Rvec], slot=Svec)
        P.op("sp", lambda e: e.dma_start(out=cbias, in_=cbias_d), writes=[Rvec], slot=Svec)
        P.op("act", lambda e: e.activation(out=biasM.rearrange("p a b c -> p (a b c)"), in_=biasM.rearrange("p a b c -> p (a b c)"), func=AF.Exp),
             writes=[Rvec], tbl="exp", n=3072)
        P.op("pool", lambda e: e.dma_start(out=wrg, in_=w_rg.rearrange("n c d -> c n d")), writes=[Rwg], slot=Swg)
        P.op("pool", lambda e: e.dma_start(out=wig, in_=w_ig.rearrange("n c d -> c n d")), writes=[Rwg], slot=Swg)
        P.op("pool", lambda e: e.memset(identF, 0.0), writes=[Rid])
        P.op("pool", lambda e: e.affine_select(out=identF, in_=identF, pattern=[[-1, 128]], compare_op=ALU.not_equal,
                                                fill=1.0, base=0, channel_multiplier=1), writes=[Rid])
        P.op("pool", lambda e: e.tensor_copy(identB, identF), writes=[Rid])
        P.op("pool", lambda e: e.memset(onesS, 1.0 / 1024.0), writes=[Rid])
        P.op("pool", lambda e: e.memset(kTr.rearrange("p a b -> p (a b)"), 0.0), writes=RkT)
        P.op("pool", lambda e: e.memset(Vr.rearrange("p a b c -> p (a b c)"), 0.0), writes=RV)
        P.op("pool", lambda e: e.memset(Vr[:, :, :, 128:129].rearrange("p a b c -> p (a b c)"), 1.0), writes=RV)
        P.op("pool", lambda e: e.memset(pT.rearrange("p a b c -> p (a b c)"), 0.0), writes=RpT)

        P.op("act", lambda e: e.activation(out=csil.rearrange("p k j -> p j k"),
                                           in_=cvec[:, C_C:C_C + 16].rearrange("p (j k) -> p j k", j=2), func=AF.Silu),
             reads=[Rvec], writes=[Rcsil], tbl="silu")
        csil_b = sb("csil_b", [128, 8, 2], BF16)
        P.op("dve", lambda e: e.tensor_copy(csil_b.rearrange("p k j -> p (k j)"), csil.rearrange("p k j -> p (k j)")), reads=[Rcsil], writes=[Rcsil], n=16)
        Smod = [P.slot() for _ in range(4)]
        stg = [(xst[:, 0, :], Rxst[0]), (xst[:, 1, :], Rxst[1]), (ost[:, 0, :], Rost[0]), (ost[:, 1, :], Rost[1])]
        for c in range(24):
            buf, Rb = stg[c % 4]
            wv = buf.bitcast(BF16).rearrange("p (k c) -> p k c", k=8)
            P.op("pool", lambda e, c=c, wv=wv: e.dma_start(out=wv, in_=w_ada[:, c * 256:(c + 1) * 256].rearrange("(k p) c -> p k c", p=128)),
                 writes=[Rb], slot=Smod[c % 4], n=1 << 20)
            for jb in range(2):
                n = c * 2 + jb
                for k in range(8):
                    P.op("pe", lambda e, k=k, n=n, jb=jb, wv=wv: e.matmul(pa[0][:, n * 2:n * 2 + 2], wv[:, k, jb * 128:(jb + 1) * 128], csil_b[:, k, :], start=(k == 0), stop=(k == 7)),
                         reads=[Rb, Rcsil], writes=[Rpa[0]], n=200)
        P.op("dve", lambda e: e.tensor_tensor(mod[:, :, 0], pa[0][:, 0:96].rearrange("p (n j) -> p n j", j=2)[:, :, 0], vecs[:, V_BADA:V_BADA + 48], ALU.add),
             reads=[Rvec], writes=[Rpa[0], Rmod])
        P.op("dve", lambda e: e.tensor_tensor(mod[:, :, 1], pa[0][:, 0:96].rearrange("p (n j) -> p n j", j=2)[:, :, 1], vecs[:, V_BADA:V_BADA + 48], ALU.add),
             reads=[Rvec], writes=[Rpa[0], Rmod])
        CL, GA = 0, 8
        def CJ(j, i):
            return 16 + j * 48 + i * 8 - 0
        P.op("act", lambda e: e.activation(out=cst[:, CL:CL + 8], in_=vecs[:, V_LAM:V_LAM + 8], func=AF.Exp, scale=-1.0), reads=[Rvec], writes=[Rcst], tbl="exp")
        P.op("act", lambda e: e.activation(out=cst[:, CL:CL + 8], in_=cst[:, CL:CL + 8], func=AF.Ln, bias=1.0, scale=1.0), writes=[Rcst], tbl="ln")
        P.op("dve", lambda e: e.tensor_scalar(cst[:, CL:CL + 8], cst[:, CL:CL + 8], -8.0, None, ALU.mult), writes=[Rcst])
        P.op("dve", lambda e: e.tensor_scalar(cst[:, GA:GA + 8], vecs[:, V_L1G:V_L1G + 8], ALPHA, None, ALU.mult), reads=[Rvec], writes=[Rcst])
        CL2, HBR, HBI = 112, 120, 128
        P.op("dve", lambda e: e.tensor_scalar(cst[:, CL2:CL2 + 8], cst[:, CL:CL + 8], 0.5, None, ALU.mult), writes=[Rcst])
        P.op("dve", lambda e: e.tensor_scalar(cst[:, HBR:HBR + 8], vecs[:, V_BRG:V_BRG + 8], 0.5, None, ALU.mult), reads=[Rvec], writes=[Rcst])
        P.op("dve", lambda e: e.tensor_scalar(cst[:, HBI:HBI + 8], vecs[:, V_BIG:V_BIG + 8], 0.5, None, ALU.mult), reads=[Rvec], writes=[Rcst])
        for j in range(2):
            def M(blk, j=j):
                return mod[:, blk * 8:(blk + 1) * 8, j]
            P.op("dve", lambda e, j=j, M=M: e.tensor_scalar(cst[:, CJ(j, 0):CJ(j, 0) + 8], M(1), 1.0, 1.0 / ALPHA, ALU.add, ALU.mult), reads=[Rmod], writes=[Rcst])
            P.op("dve", lambda e, j=j, M=M: e.tensor_scalar(cst[:, CJ(j, 1):CJ(j, 1) + 8], M(2), 1.0, 0.5, ALU.add, ALU.mult), reads=[Rmod], writes=[Rcst])
            P.op("dve", lambda e, j=j, M=M: e.tensor_scalar(cst[:, CJ(j, 4):CJ(j, 4) + 8], M(5), 1.0, None, ALU.add), reads=[Rmod], writes=[Rcst])
            P.op("dve", lambda e, j=j, M=M: e.tensor_scalar(cst[:, CJ(j, 3):CJ(j, 3) + 8], M(4), 1.0, None, ALU.add), reads=[Rmod], writes=[Rcst])
            P.op("dve", lambda e, j=j: e.tensor_tensor(cst[:, CJ(j, 2):CJ(j, 2) + 8], vecs[:, V_L1G:V_L1G + 8], cst[:, CJ(j, 3):CJ(j, 3) + 8], ALU.mult), reads=[Rvec], writes=[Rcst])
            P.op("dve", lambda e, j=j: e.tensor_tensor(cst[:, CJ(j, 3):CJ(j, 3) + 8], vecs[:, V_L1B:V_L1B + 8], cst[:, CJ(j, 3):CJ(j, 3) + 8], ALU.mult), reads=[Rvec], writes=[Rcst])
            P.op("dve", lambda e, j=j, M=M: e.tensor_tensor(cst[:, CJ(j, 3):CJ(j, 3) + 8], cst[:, CJ(j, 3):CJ(j, 3) + 8], M(3), ALU.add), reads=[Rmod], writes=[Rcst])
            P.op("dve", lambda e, j=j: e.tensor_tensor(cst[:, CJ(j, 5):CJ(j, 5) + 8], vecs[:, V_BD:V_BD + 8], cst[:, CJ(j, 4):CJ(j, 4) + 8], ALU.mult), reads=[Rvec], writes=[Rcst])
            P.op("dve", lambda e, j=j: e.scalar_tensor_tensor(cst[:, CJ(j, 5):CJ(j, 5) + 8], vecs[:, V_L1B:V_L1B + 8], ALPHA, cst[:, CJ(j, 5):CJ(j, 5) + 8], ALU.mult, ALU.add), reads=[Rvec], writes=[Rcst])

        for r_ in [Rvec, Rcst, Rmod, Rid, Rwg, Rcsil]:
            r_.ro = True
        wq = [0]

        converted = set()
        f32src = {}
        for g in range(14):
            f32src[id(Rs_in[g])] = (w_in[:, g * 512:(g + 1) * 512].rearrange("(k p) c -> p k c", p=128), Ss_in[g])
        for g in range(2):
            f32src[id(Rs_out[g])] = (w_out[:, g * 512:(g + 1) * 512].rearrange("(k p) c -> p k c", p=128), Ss_out[g])
        for g in range(8):
            f32src[id(Rs_up[g])] = (w_up[:, g * 512:(g + 1) * 512].rearrange("(k p) c -> p k c", p=128), Ss_up[g])

        def stream(src, Rsrc):
            i = wq[0] % NWB
            wq[0] += 1
            if id(Rsrc) not in converted:
                converted.add(id(Rsrc))
                fsrc, Ssrc = f32src[id(Rsrc)]
                P.op("pool", lambda e: e.dma_start(out=wbuf[:, i], in_=fsrc), writes=[Rwb[i]], slot=Swbp[i], n=2 << 20)
                P.op("sp", lambda e: e.dma_start(out=src, in_=wbuf[:, i]), reads=[Rwb[i]], writes=[Rsrc], slot=Ssrc)
            else:
                P.op("sp", lambda e: e.dma_start(out=wbuf[:, i], in_=src), reads=[Rsrc], writes=[Rwb[i]], slot=Swb[i])
            return i

        zq = [0]

        pzz = [pz[0], pz[1], pt[0], pt[1]]
        Rpzz = [Rpz[0], Rpz[1], Rpt[0], Rpt[1]]

        def nextz():
            i = zq[0] % 4
            zq[0] += 1
            return i

        Gv = big[:, 0:8192].bitcast(F32).rearrange("p (c t) -> p c t", t=512)
        sgb = big[:, 8192:12288].rearrange("p (c t) -> p c t", t=512)
        qT = big[:, 12288:16384].rearrange("p (c t) -> p c t", t=512)
        RG = [Res("G%d" % i) for i in range(8)]; Rsgb = [Res("sgb%d" % i) for i in range(8)]; RqT = [Res("qT%d" % i) for i in range(8)]

        def col(base, n):
            return cst[:, base + n:base + n + 1]

        def vcol(base, n):
            return vecs[:, base + n:base + n + 1]

        def tile(t, T, j):
            P.defn = T
            sample = (j == 1)
            xsrc = xs if sample else xp[t * 512:(t + 1) * 512, :]
            ydst = ys if sample else yp[t * 512:(t + 1) * 512, :]
            NS = max(1, T // 128)
            RW = min(T, 128)
            last = sample or (t == NT - 1)
            P.fence(RhT, RG + Rsgb + RqT)

            if sample:
                for s in range(4):
                    b_ = s % 2
                    P.op("sp", lambda e, s=s, b_=b_: e.dma_start(out=xst[:, b_, :], in_=ck[s * 128:(s + 1) * 128, :]), writes=[Rxst[b_]], slot=Sxst[b_])
                    for half in range(2):
                        pi_ = half
                        for f4 in range(4):
                            fc = half * 4 + f4
                            P.op("pe", lambda e, b_=b_, fc=fc, f4=f4, pi_=pi_: e.transpose(pt[pi_][:, f4 * 128:(f4 + 1) * 128], xst[:, b_, fc * 128:(fc + 1) * 128], identF),
                                 reads=[Rxst[b_], Rid], writes=[Rpt[pi_]])
                        P.op("act", lambda e, half=half, s=s, pi_=pi_: e.activation(out=kTr[:, half * 4:(half + 1) * 4, s * 128:(s + 1) * 128],
                                                                                   in_=pt[pi_].rearrange("p (f t) -> p f t", t=128), func=AF.Copy),
                             writes=[Rpt[pi_]] + RkT[half * 4:(half + 1) * 4])
                    P.op("pool", lambda e, s=s: e.dma_start(out=Vr[:, s, :, 0:128], in_=cv[s * 128:(s + 1) * 128, :].rearrange("p (h d) -> p h d", d=128)),
                         writes=[RV[s]], slot=Scv[s], n=1 << 19)
                P.op("pool", lambda e: e.tensor_copy(hist.rearrange("p c j -> p j c"), cvec[:, C_CONV:C_CONV + 24].rearrange("p (j c) -> p j c", j=3)),
                     reads=[Rvec], writes=Rhist)
                P.op("pool", lambda e: e.tensor_copy(hst, cvec[:, C_LRU:C_LRU + 8]), reads=[Rvec], writes=Rhst)
            elif t == 0:
                P.op("pool", lambda e: e.memset(hist.rearrange("p c j -> p (c j)"), 0.0), writes=Rhist)
                P.op("pool", lambda e: e.memset(hst, 0.0), writes=Rhst)

            for s in range(NS):
                b_ = s % 2
                P.op("sp", lambda e, s=s, b_=b_: e.dma_start(out=xst[0:RW, b_, :], in_=xsrc[s * 128:s * 128 + RW, :]), writes=[Rxst[b_]], slot=Sxst[b_])
                for half in range(2):
                    pi_ = half
                    for f4 in range(4):
                        fc = half * 4 + f4
                        P.op("pe", lambda e, b_=b_, fc=fc, f4=f4, pi_=pi_: e.transpose(pt[pi_][:, f4 * 128:f4 * 128 + RW], xst[0:RW, b_, fc * 128:(fc + 1) * 128], identF[0:RW, 0:RW]),
                             reads=[Rxst[b_], Rid], writes=[Rpt[pi_]], n=256)
                    P.op("act", lambda e, half=half, s=s, pi_=pi_: e.activation(out=resid[:, half * 4:(half + 1) * 4, s * 128:s * 128 + RW],
                                                                               in_=pt[pi_].rearrange("p (f t) -> p f t", t=128)[:, :, 0:RW], func=AF.Copy, scale=ALPHA),
                         writes=[Rpt[pi_]] + Rres[half * 4:(half + 1) * 4])
            for fc in range(8):
                P.op("dve", lambda e, fc=fc: e.tensor_scalar(uT[:, fc, 0:T], resid[:, fc, 0:T], col(CJ(j, 0), fc), mod[:, fc, j:j + 1], ALU.mult, ALU.add),
                     reads=[Rres[fc], Rcst, Rmod], writes=[RuT[fc]])

            def proj_block(wi, jb, rhs_of_k, Rrhs):
                z = nextz()
                for k in range(8):
                    P.op("pe", lambda e, k=k, z=z: e.matmul(pzz[z][:, 0:T], wbuf[:, wi, k, jb * 128:(jb + 1) * 128], rhs_of_k(k), start=(k == 0), stop=(k == 7)),
                         reads=[Rwb[wi], Rrhs[k]], writes=[Rpzz[z]])
                return z

            uk = lambda k: uT[:, k, 0:T]
            XC, RA, IB, AM = 0, 1, 2, 3

            def chainA(fc, wi, jb):
                xb_ = fc % 2
                tp_ = fc % 2
                Rt = Rtg[tp_]
                tgp = tg[:, tp_]
                z = proj_block(wi, jb, uk, RuT)
                P.op("pool", lambda e: e.tensor_copy(xrb[:, xb_, 0:3], hist[:, fc, :]), reads=[Rhist[fc]], writes=[Rxrb[xb_]], n=8)
                P.op("act", lambda e: e.activation(out=xrb[:, xb_, 3:3 + T], in_=pzz[z][:, 0:T], func=AF.Copy), writes=[Rpzz[z], Rxrb[xb_]])
                P.op("pool", lambda e: e.tensor_copy(hist[:, fc, :], xrb[:, xb_, T:T + 3]), reads=[Rxrb[xb_]], writes=[Rhist[fc]], n=8)
                P.op("dve", lambda e: e.tensor_scalar(tgp[:, XC, 0:T], xrb[:, xb_, 0:T], vcol(V_CW, fc), vcol(V_CB, fc), ALU.mult, ALU.add),
                     reads=[Rxrb[xb_], Rvec], writes=[Rt[XC]])
                for jj in (1, 2, 3):
                    P.op("dve", lambda e, jj=jj: e.scalar_tensor_tensor(tgp[:, XC, 0:T], xrb[:, xb_, jj:jj + T], vcol(V_CW + 8 * jj, fc), tgp[:, XC, 0:T], ALU.mult, ALU.add),
                         reads=[Rxrb[xb_], Rvec], writes=[Rt[XC]])
                P.op("act", lambda e: e.activation(out=xcb[:, tp_, 0:T], in_=tgp[:, XC, 0:T], func=AF.Copy), reads=[Rt[XC]], writes=[Rxcb[tp_]])
                P.op("pe", lambda e: e.matmul(pa[0][:, 0:T], wrg[:, fc, :], xcb[:, tp_, 0:T], start=True, stop=True), reads=[Rwg, Rxcb[tp_]], writes=[Rpa[0]], delay=5000.0)
                P.op("pe", lambda e: e.matmul(pa[1][:, 0:T], wig[:, fc, :], xcb[:, tp_, 0:T], start=True, stop=True), reads=[Rwg, Rxcb[tp_]], writes=[Rpa[1]])
                P.op("act", lambda e: e.activation(out=tgp[:, RA, 0:T], in_=pa[0][:, 0:T], func=AF.Tanh, bias=col(HBR, fc), scale=0.5), reads=[Rcst], writes=[Rpa[0], Rt[RA]], tbl="tanh")
                P.op("act", lambda e: e.activation(out=tgp[:, IB, 0:T], in_=pa[1][:, 0:T], func=AF.Tanh, bias=col(HBI, fc), scale=0.5), reads=[Rcst], writes=[Rpa[1], Rt[IB]], tbl="tanh")
                P.op("act", lambda e: e.activation(out=tgp[:, RA, 0:T], in_=tgp[:, RA, 0:T], func=AF.Exp, scale=col(CL2, fc), bias=col(CL2, fc)), reads=[Rcst], writes=[Rt[RA]], tbl="exp")
                P.op("pool", lambda e: e.tensor_tensor(tgp[:, AM, 0:T], tgp[:, RA, 0:T], tgp[:, RA, 0:T], ALU.mult), reads=[Rt[RA]], writes=[Rt[AM]])
                P.op("dve", lambda e: e.scalar_tensor_tensor(tgp[:, IB, 0:T], tgp[:, IB, 0:T], 1.0, tgp[:, XC, 0:T], ALU.add, ALU.mult), reads=[Rt[XC]], writes=[Rt[IB]])

            def chainB(fc):
                tp_ = fc % 2
                Rt = Rtg[tp_]
                tgp = tg[:, tp_]
                if (not sample) and t == 0:
                    P.op("pool", lambda e: e.memset(tgp[:, AM, 0:1], 0.5), writes=[Rt[AM]], n=1)
                    P.op("pool", lambda e: e.memset(tgp[:, RA, 0:1], 0.0), writes=[Rt[RA]], n=1)
                P.op("pool", lambda e: e.tensor_tensor(tgp[:, IB, 0:T], tgp[:, IB, 0:T], tgp[:, AM, 0:T], ALU.mult), reads=[Rt[AM]], writes=[Rt[IB]])
                P.op("dve", lambda e: e.tensor_tensor_scan(Gv[:, fc, 0:T], tgp[:, RA, 0:T], tgp[:, IB, 0:T], hst[:, fc:fc + 1], ALU.mult, ALU.add),
                     reads=[Rt[RA], Rt[IB], Rhst[fc]], writes=[RG[fc]], n=2 * T)
                P.op("pool", lambda e: e.tensor_copy(hst[:, fc:fc + 1], Gv[:, fc, T - 1:T]), reads=[RG[fc]], writes=[Rhst[fc]], n=1)

            def emit_xr(g, jp):
                wi = stream(win_b[g], Rs_in[g])
                if True:
                    fcs = (g * 4 + 2 * jp, g * 4 + 2 * jp + 1)
                    for fc in fcs:
                        chainA(fc, wi, fc % 4)
                    P.op("act", lambda e: e.activation(out=tg[:, :, AM, 0:T], in_=tg[:, :, AM, 0:T], func=AF.Sqrt, bias=0.25, scale=-0.25),
                         writes=[Rtg[0][AM], Rtg[1][AM]], tbl="sqrt", n=2 * T)
                    for fc in fcs:
                        chainB(fc)
            def emit_gl(g):
                wi = stream(win_b[g], Rs_in[g])
                for jb in range(4):
                    fc = (g - 2) * 4 + jb
                    z = proj_block(wi, jb, uk, RuT)
                    sp_ = fc % 2
                    P.op("act", lambda e, z=z, sp_=sp_: e.activation(out=tgs[:, sp_, 0:T], in_=pzz[z][:, 0:T], func=AF.Gelu_apprx_tanh), writes=[Rpzz[z], Rtgs[sp_]], tbl="gelu")
                    P.op("pool", lambda e, fc=fc, sp_=sp_: e.tensor_tensor(Gv[:, fc, 0:T], Gv[:, fc, 0:T], tgs[:, sp_, 0:T], ALU.mult), reads=[Rtgs[sp_]], writes=[RG[fc]])
            def emit_ga(g):
                wi = stream(win_b[g], Rs_in[g])
                for jb in range(4):
                    fc = (g - 10) * 4 + jb
                    z = proj_block(wi, jb, uk, RuT)
                    sp_ = fc % 2
                    P.op("act", lambda e, z=z, sp_=sp_: e.activation(out=tgs[:, sp_, 0:T], in_=pzz[z][:, 0:T], func=AF.Tanh, scale=0.5), writes=[Rpzz[z], Rtgs[sp_]], tbl="tanh")
                    P.op("dve", lambda e, fc=fc, sp_=sp_: e.scalar_tensor_tensor(Gv[:, fc, 0:T], tgs[:, sp_, 0:T], 1.0, Gv[:, fc, 0:T], ALU.add, ALU.mult), reads=[Rtgs[sp_]], writes=[RG[fc]])
            def emit_gb(g):
                wi = stream(win_b[g], Rs_in[g])
                for jb in range(4):
                    fc = (g - 12) * 4 + jb
                    z = proj_block(wi, jb, uk, RuT)
                    P.op("act", lambda e, z=z, fc=fc: e.activation(out=sgb[:, fc, 0:T], in_=pzz[z][:, 0:T], func=AF.Tanh, scale=0.5), writes=[Rpzz[z], Rsgb[fc]], tbl="tanh")
            ro = 512 if sample else (t % 2) * 512
            def emit_q(g):
                wi = stream(win_b[g], Rs_in[g])
                for jb in range(4):
                    h = (g - 4) * 4 + jb
                    z = proj_block(wi, jb, uk, RuT)
                    P.op("dve", lambda e, z=z, h=h: e.tensor_scalar(qT[:, h, 0:T], pzz[z][:, 0:T], ATT_SCALE, None, ALU.mult), writes=[Rpzz[z], RqT[h]])
            def emit_k(g):
                wi = stream(win_b[g], Rs_in[g])
                for jb in range(4):
                    h = (g - 6) * 4 + jb
                    z = proj_block(wi, jb, uk, RuT)
                    P.op("dve", lambda e, z=z, h=h: e.tensor_copy(kTr[:, h, ro:ro + T], pzz[z][:, 0:T]), writes=[Rpzz[z], RkT[h]])
                if last:
                    for s in range(NS):
                        z = nextz()
                        for k in range(8):
                            P.op("pe", lambda e, k=k, z=z, s=s, wi=wi: e.matmul(pzz[z][0:RW, :], uT[:, k, s * 128:s * 128 + RW], wbuf[:, wi, k, :], start=(k == 0), stop=(k == 7)),
                                 reads=[Rwb[wi]] + RuT, writes=[Rpzz[z]])
                        ob = (s + g) % 2
                        P.op("dve", lambda e, z=z, ob=ob: e.tensor_copy(ost[0:RW, ob, 0:512], pzz[z][0:RW, :]), writes=[Rpzz[z], Rost[ob]])
                        dst = (nks if sample else nkp)[s * 128:s * 128 + RW, (g - 6) * 512:(g - 5) * 512]
                        P.op("sp", lambda e, ob=ob, dst=dst: e.dma_start(out=dst, in_=ost[0:RW, ob, 0:512]), reads=[Rost[ob]], slot=Sost[ob], n=1 << 18)
            def emit_v(g):
                wi = stream(win_b[g], Rs_in[g])
                for s in range(NS):
                    z = nextz()
                    for k in range(8):
                        P.op("pe", lambda e, k=k, z=z, s=s, wi=wi: e.matmul(pzz[z][0:RW, :], uT[:, k, s * 128:s * 128 + RW], wbuf[:, wi, k, :], start=(k == 0), stop=(k == 7)),
                             reads=[Rwb[wi]] + RuT, writes=[Rpzz[z]])
                    rb = 4 if sample else (t % 2) * 4 + s
                    P.op("dve", lambda e, z=z, rb=rb, g=g: e.tensor_copy(Vr[0:RW, rb, (g - 8) * 4:(g - 7) * 4, 0:128],
                                                                       pzz[z][0:RW, :].rearrange("p (h d) -> p h d", d=128)),
                         writes=[Rpzz[z], RV[rb]])
                    if last:
                        ob = (s + g) % 2
                        P.op("dve", lambda e, z=z, ob=ob: e.tensor_copy(ost[0:RW, ob, 0:512], pzz[z][0:RW, :]), writes=[Rpzz[z], Rost[ob]])
                        dst = (nvs if sample else nvp)[s * 128:s * 128 + RW, (g - 8) * 512:(g - 7) * 512]
                        P.op("sp", lambda e, ob=ob, dst=dst: e.dma_start(out=dst, in_=ost[0:RW, ob, 0:512]), reads=[Rost[ob]], slot=Sost[ob], n=1 << 18)
            emit_xr(0, 0); emit_q(4); emit_q(5)
            emit_xr(0, 1); emit_k(6); emit_k(7)
            emit_xr(1, 0); emit_v(8); emit_v(9)
            emit_xr(1, 1); emit_gb(12); emit_gb(13)
            emit_gl(2); emit_gl(3); emit_ga(10); emit_ga(11)
            if last:
                P.op("sp", lambda e: e.dma_start(out=(ncs if sample else ncp), in_=hist.rearrange("p c j -> p (c j)")), reads=Rhist, slot=Sout[0 if sample else 2], n=4096)
                P.op("sp", lambda e: e.dma_start(out=(nhs if sample else nhp), in_=hst), reads=Rhst, slot=Sout[1 if sample else 3], n=4096)

            NP = 1 if sample else 4
            QW = 64 if sample else 128
            it = 0
            Sset = [(pa[0], pa[1], Rpa[0], Rpa[1]), (pt[0], pt[1], Rpt[0], Rpt[1])]
            Oset = [(po, Rpo), (pz[1], Rpz[1])]
            for pi in range(NP):
                for h in range(8):
                    bb = it % 2
                    it += 1
                    sM, sC, RsM, RsC = Sset[bb]
                    oB, RoB = Oset[bb]
                    blocks = []
                    for b in range(5):
                        if sample:
                            blocks.append((b, b))
                        else:
                            cs = 8 * t + 2 * pi - 8 + 2 * b
                            if cs >= 0:
                                blocks.append((b, (cs // 2) % 8))
                    mpos = {0: 0, 3: 1, 4: 2}
                    cpos = {1: 0, 2: 1}
                    q_ap = qT[:, h, pi * 128:pi * 128 + QW]
                    for b, rb in blocks:
                        if b in mpos:
                            o_ap = sM[:, mpos[b] * 128:mpos[b] * 128 + QW]; R_ = RsM
                        else:
                            o_ap = sC[:, cpos[b] * 128:cpos[b] * 128 + QW]; R_ = RsC
                        P.op("pe", lambda e, o_ap=o_ap, h=h, rb=rb, q_ap=q_ap: e.matmul(o_ap, kTr[:, h, rb * 128:(rb + 1) * 128], q_ap, start=True, stop=True),
                             reads=[RkT[h], RqT[h]], writes=[R_], n=QW + 64)
                    mb = [b for b, _ in blocks if b in mpos]
                    cb = [b for b, _ in blocks if b in cpos]
                    m0 = mpos[mb[0]]
                    nm = len(mb)
                    P.op("act", lambda e, bb=bb, m0=m0, nm=nm, sM=sM: e.activation(
                        out=sbm[:, bb, m0:m0 + nm, 0:QW], in_=sM.rearrange("p (b q) -> p b q", q=128)[:, m0:m0 + nm, 0:QW], func=AF.Exp),
                        writes=[RsM, Rsbm[bb]], n=nm * QW, tbl="exp")
                    P.op("pool", lambda e, bb=bb, h=h, m0=m0, nm=nm: e.tensor_tensor(
                        pT[:, bb, m0:m0 + nm, 0:QW], sbm[:, bb, m0:m0 + nm, 0:QW], biasM[:, h, m0:m0 + nm, 0:QW], ALU.mult),
                        reads=[Rvec, Rsbm[bb]], writes=[RpT[bb]], n=nm * QW)
                    if cb:
                        c0_ = cpos[cb[0]]
                        ncb = len(cb)
                        P.op("act", lambda e, bb=bb, h=h, c0_=c0_, ncb=ncb, sC=sC: e.activation(
                            out=pT[:, bb, 3 + c0_:3 + c0_ + ncb, 0:QW], in_=sC.rearrange("p (b q) -> p b q", q=128)[:, c0_:c0_ + ncb, 0:QW],
                            func=AF.Exp, bias=cbias[:, h:h + 1], scale=1.0),
                            reads=[Rvec], writes=[RsC, RpT[bb]], n=ncb * QW, tbl="exp")
                    for i_, (b, rb) in enumerate(blocks):
                        pidx = mpos[b] if b in mpos else 3 + cpos[b]
                        P.op("pe", lambda e, bb=bb, pidx=pidx, rb=rb, h=h, i_=i_, nb=len(blocks), oB=oB: e.matmul(
                            oB[0:QW, 0:129], pT[:, bb, pidx, 0:QW], Vr[:, rb, h, :], start=(i_ == 0), stop=(i_ == nb - 1)),
                            reads=[RpT[bb], RV[rb]], writes=[RoB], n=129 + 64)
                    P.op("dve", lambda e, bb=bb, oB=oB: e.reciprocal(rc[0:QW, bb:bb + 1], oB[0:QW, 128:129]), writes=[RoB, Rrc[bb]], n=1)
                    P.op("dve", lambda e, bb=bb, oB=oB: e.tensor_scalar(osb[0:QW, bb, :], oB[0:QW, 0:128], rc[0:QW, bb:bb + 1], None, ALU.mult),
                         reads=[Rrc[bb]], writes=[RoB, Rosb[bb]], n=128)
                    P.op("pe", lambda e, bb=bb: e.transpose(pob[:, bb * 128:bb * 128 + QW], osb[0:QW, bb, :], identB[0:QW, 0:QW]), reads=[Rosb[bb], Rid], writes=[Rpob], n=128)
                    P.op("dve", lambda e, bb=bb, h=h, pi=pi: e.scalar_tensor_tensor(mtmp[:, bb, 0:QW], sgb[:, h, pi * 128:pi * 128 + QW], 1.0, pob[:, bb * 128:bb * 128 + QW], ALU.add, ALU.mult),
                         reads=[Rsgb[h]], writes=[Rpob, Rmt[bb]], n=QW)
                    P.op("dve", lambda e, bb=bb, h=h, pi=pi: e.tensor_tensor(uT[:, h, pi * 128:pi * 128 + QW], Gv[:, h, pi * 128:pi * 128 + QW], mtmp[:, bb, 0:QW], ALU.add),
                         reads=[Rmt[bb], RG[h]], writes=[RuT[h]], n=QW)

            def layernorm(emit_out, inplace=False):
                for n in range(8):
                    b2 = n % 2
                    P.op("dve", lambda e, n=n, b2=b2: e.tensor_copy(xb[:, b2, 0:T], resid[:, n, 0:T]), reads=[Rres[n]], writes=[Rxb[b2]])
                    P.op("act", lambda e, n=n, b2=b2: e.activation(out=sq[:, b2, 0:T], in_=resid[:, n, 0:T], func=AF.Square), reads=[Rres[n]], writes=[Rsq[b2]])
                    P.op("pe", lambda e, n=n, b2=b2: e.matmul(pa[0][:, 0:T], onesS, xb[:, b2, 0:T], start=(n == 0), stop=(n == 7)), reads=[Rid, Rxb[b2]], writes=[Rpa[0]])
                    P.op("pe", lambda e, n=n, b2=b2: e.matmul(pa[1][:, 0:T], onesS, sq[:, b2, 0:T], start=(n == 0), stop=(n == 7)), reads=[Rid, Rsq[b2]], writes=[Rpa[1]])
                P.op("act", lambda e: e.activation(out=lnv[:, 0:T], in_=pa[0][:, 0:T], func=AF.Square), writes=[Rpa[0], Rlnv])
                P.op("dve", lambda e: e.tensor_tensor(lnv[:, 0:T], pa[1][:, 0:T], lnv[:, 0:T], ALU.subtract), writes=[Rpa[1], Rlnv])
                P.op("act", lambda e: e.activation(out=lnv[:, 0:T], in_=lnv[:, 0:T], func=AF.Sqrt, bias=LN_EPS, scale=1.0), writes=[Rlnv], tbl="sqrt")
                P.op("dve", lambda e: e.reciprocal(lnv[:, 0:T], lnv[:, 0:T]), writes=[Rlnv])
                P.op("dve", lambda e: e.scalar_tensor_tensor(lnn[:, 0:T], pa[0][:, 0:T], -1.0, lnv[:, 0:T], ALU.mult, ALU.mult), reads=[Rlnv], writes=[Rpa[0], Rlnn])
                if inplace:
                    for n in range(8):
                        P.op("dve", lambda e, n=n: e.tensor_tensor(Gv[:, n, 0:T], resid[:, n, 0:T], lnv[:, 0:T], ALU.mult), reads=[Rres[n], Rlnv], writes=[RG[n], Rln2m])
                    for n in range(8):
                        P.op("pool", lambda e, n=n: e.tensor_tensor(Gv[:, n, 0:T], Gv[:, n, 0:T], lnn[:, 0:T], ALU.add), reads=[Rlnn, Rln2m], writes=[RG[n]])
                        emit_out(n, None)
                    return
                for n in range(8):
                    b2 = n % 2
                    P.op("dve", lambda e, n=n, b2=b2: e.tensor_tensor(lnt[:, b2, 0:T], resid[:, n, 0:T], lnv[:, 0:T], ALU.mult), reads=[Rres[n], Rlnv], writes=[Rlnt[b2]])
                    P.op("dve", lambda e, b2=b2: e.tensor_tensor(lnt[:, b2, 0:T], lnt[:, b2, 0:T], lnn[:, 0:T], ALU.add), reads=[Rlnn], writes=[Rlnt[b2]])
                    emit_out(n, b2)

            mk = lambda k: uT[:, k, 0:T]
            for g in range(2):
                wi = stream(wout_b[g], Rs_out[g])
                for jb in range(4):
                    n = g * 4 + jb
                    z = proj_block(wi, jb, mk, RuT)
                    P.op("dve", lambda e, z=z, n=n: e.scalar_tensor_tensor(resid[:, n, 0:T], pzz[z][:, 0:T], col(CJ(j, 1), n), resid[:, n, 0:T], ALU.mult, ALU.add),
                         reads=[Rcst], writes=[Rpzz[z], Rres[n]])

            def ln1_out(n, b2):
                P.op("act", lambda e, n=n, b2=b2: e.activation(out=u2T[:, n, 0:T], in_=lnt[:, b2, 0:T], func=AF.Identity, scale=col(CJ(j, 2), n), bias=col(CJ(j, 3), n)),
                     reads=[Rlnt[b2], Rcst], writes=[Ru2[n]])
                P.op("act", lambda e, n=n, b2=b2: e.activation(out=resid[:, n, 0:T], in_=lnt[:, b2, 0:T], func=AF.Identity, scale=col(GA, n), bias=col(CJ(j, 5), n)),
                     reads=[Rlnt[b2], Rcst], writes=[Rres[n]])
            layernorm(ln1_out)

            P.fence(RG + Rsgb + RqT, RhT)
            u2k = lambda k: u2T[:, k, 0:T]
            for g in range(8):
                wi = stream(wup_b[g], Rs_up[g])
                if g == 0:
                    zs0 = [nextz() for _ in range(4)]
                    for k in range(8):
                        for jb in range(4):
                            P.op("pe", lambda e, k=k, jb=jb, zz=zs0[jb], wi=wi: e.matmul(pzz[zz][:, 0:T], wbuf[:, wi, k, jb * 128:(jb + 1) * 128], u2T[:, k, 0:T], start=(k == 0), stop=(k == 7)),
                                 reads=[Rwb[wi], Ru2[k]], writes=[Rpzz[zs0[jb]]])
                for jb in range(4):
                    m = g * 4 + jb
                    z = zs0[jb] if g == 0 else proj_block(wi, jb, u2k, Ru2)
                    tb = m % 2
                    P.op("act", lambda e, z=z, m=m, tb=tb: e.activation(out=tgs[:, tb, 0:T], in_=pzz[z][:, 0:T], func=AF.Relu, bias=vcol(V_BUP, m), scale=1.0),
                         reads=[Rvec], writes=[Rpzz[z], Rtgs[tb]])
                    eng = "pool" if m % 2 == 0 else "dve"
                    P.op(eng, lambda e, m=m, tb=tb: e.tensor_tensor(hT[:, m, 0:T], tgs[:, tb, 0:T], tgs[:, tb, 0:T], ALU.mult), reads=[Rtgs[tb]], writes=[RhT[m]])
            for n in range(8):
                i = wq[0] % NWB
                wq[0] += 1
                wd = wbuf[:, i].rearrange("p k c -> p (k c)").rearrange("p (k c) -> p k c", c=128)
                if id(Rs_dn[n]) not in converted:
                    converted.add(id(Rs_dn[n]))
                    for kh in range(2):
                        P.op("pool", lambda e, n=n, wd=wd, kh=kh: e.dma_start(
                            out=wd[:, kh * 16:(kh + 1) * 16, :],
                            in_=w_down[kh * 2048:(kh + 1) * 2048, n * 128:(n + 1) * 128].rearrange("(k p) c -> p k c", p=128)),
                            writes=[Rwb[i]], slot=Swbp[i], n=1 << 20)
                    P.op("sp", lambda e, n=n, wd=wd: e.dma_start(out=wdn_b[n], in_=wd), reads=[Rwb[i]], writes=[Rs_dn[n]], slot=Ss_dn[n])
                else:
                    P.op("sp", lambda e, i=i, n=n: e.dma_start(out=wbuf[:, i].rearrange("p k c -> p (k c)"), in_=wdn_b[n].rearrange("p k c -> p (k c)")),
                         reads=[Rs_dn[n]], writes=[Rwb[i]], slot=Swb[i])
                z = nextz()
                for k in range(32):
                    P.op("pe", lambda e, k=k, z=z, wd=wd: e.matmul(pzz[z][:, 0:T], wd[:, k, :], hT[:, k, 0:T], start=(k == 0), stop=(k == 31)),
                         reads=[Rwb[i], RhT[k]], writes=[Rpzz[z]])
                P.op("dve", lambda e, z=z, n=n: e.scalar_tensor_tensor(resid[:, n, 0:T], pzz[z][:, 0:T], col(CJ(j, 4), n), resid[:, n, 0:T], ALU.mult, ALU.add),
                     reads=[Rcst], writes=[Rpzz[z], Rres[n]])

            def ln2_out(n, b2):
                P.op("act", lambda e, n=n: e.activation(out=Gv[:, n, 0:T], in_=Gv[:, n, 0:T], func=AF.Identity, scale=vcol(V_L2G, n), bias=vcol(V_L2B, n)),
                     reads=[Rvec], writes=[RG[n]])
            P.fence(RhT, RG)
            layernorm(ln2_out, inplace=True)

            for s in range(NS):
                ob = s % 2
                for half in range(2):
                    pi_ = half
                    for f4 in range(4):
                        fc = half * 4 + f4
                        P.op("pe", lambda e, s=s, fc=fc, f4=f4, pi_=pi_: e.transpose(pa[pi_][0:RW, f4 * 128:(f4 + 1) * 128], Gv[:, fc, s * 128:s * 128 + RW], identF),
                             reads=[RG[fc], Rid], writes=[Rpa[pi_]], n=256)
                    eng = "act" if half == 0 else "dve"
                    if eng == "act":
                        P.op("act", lambda e, ob=ob, half=half, pi_=pi_: e.activation(out=ost[0:RW, ob, half * 512:(half + 1) * 512], in_=pa[pi_][0:RW, :], func=AF.Copy),
                             writes=[Rpa[pi_], Rost[ob]])
                    else:
                        P.op("dve", lambda e, ob=ob, half=half, pi_=pi_: e.tensor_copy(ost[0:RW, ob, half * 512:(half + 1) * 512], pa[pi_][0:RW, :]),
                             writes=[Rpa[pi_], Rost[ob]])
                P.op("sp", lambda e, ob=ob, s=s: e.dma_start(out=ydst[s * 128:s * 128 + RW, :], in_=ost[0:RW, ob, :]), reads=[Rost[ob]], slot=Sost[ob])

        for t in range(NT):
            tile(t, 512, 0)
            if t == 0:
                for r_ in Rs_in + Rs_out + Rs_up + Rs_dn:
                    r_.ro = True
        tile(None, 64, 1)

        P.final_slots = Sost + Sout
        P.emit()
        build.last = (P.est_ns, getattr(P, "n_tbl", 0), len(P.ops))
    return nc


_NC_CACHE = {}


def _pcol(v):
    v = np.asarray(v, np.float32).reshape(-1, 128)
    return np.ascontiguousarray(v.T)


def _host_layout(inputs, NT):
    f = lambda k: np.asarray(inputs[k], np.float32)
    conv_w = f("conv_w")[0]
    vecs = np.concatenate([
        _pcol(conv_w[0]), _pcol(conv_w[1]), _pcol(conv_w[2]), _pcol(conv_w[3]),
        _pcol(f("conv_b")[0]), _pcol(f("b_rg")[0].reshape(-1)), _pcol(f("b_ig")[0].reshape(-1)), _pcol(f("lru_lambda")[0]),
        _pcol(f("ln1_g")[0]), _pcol(f("ln1_b")[0]), _pcol(f("ln2_g")[0]), _pcol(f("ln2_b")[0]),
        _pcol(f("b_down")[0]), _pcol(f("b_up")[0]), _pcol(f("b_ada")[0])], axis=1)
    assert vecs.shape == (128, NV)
    table = f("rel_bias")[0]
    kr = np.arange(128)[:, None]; qc = np.arange(128)[None, :]
    qch = qc // 64
    biasM = np.empty((128, 8, 3, 128), np.float32)
    for i, b in enumerate((0, 3, 4)):
        kch = 2 * b - 8 + kr // 64
        kpos = (2 * b - 8) * 64 + kr
        rel = np.clip(qc - kpos, -128, 128) + 128
        vis = (kch <= qch) & (kch >= qch - 8)
        for h in range(8):
            biasM[:, h, i, :] = np.where(vis, table[h][rel], np.float32(NEG))
    cbias = np.ascontiguousarray(np.broadcast_to(table[:, 256][None, :], (128, 8))).astype(np.float32)
    shared = {
        "vecs": vecs, "biasM": np.ascontiguousarray(biasM.reshape(128, -1)), "cbias": cbias,
        "w_ada": np.ascontiguousarray(f("w_ada")[0]), "w_in": np.ascontiguousarray(f("w_in")[0]),
        "w_rg": np.ascontiguousarray(f("w_rg")[0]), "w_ig": np.ascontiguousarray(f("w_ig")[0]),
        "w_out": np.ascontiguousarray(f("w_out")[0]), "w_up": np.ascontiguousarray(f("w_up")[0]),
        "w_down": np.ascontiguousarray(f("w_down")[0]),
    }
    in_maps = []
    nb = f("x_prompt").shape[0]
    for b in range(nb):
        sc = f("state_conv")[0, b]
        cvec = np.concatenate([_pcol(f("c_prompt")[b]), _pcol(f("c_sample")[b]), _pcol(f("state_lru")[0, b]),
                               _pcol(sc[0]), _pcol(sc[1]), _pcol(sc[2])], axis=1)
        m = dict(shared)
        m.update({
            "xp": np.ascontiguousarray(f("x_prompt")[b, :NT * 512]), "xs": np.ascontiguousarray(f("x_sample")[b]),
            "ck": np.ascontiguousarray(f("cache_k")[0, b].reshape(512, 1024)),
            "cv": np.ascontiguousarray(f("cache_v")[0, b].reshape(512, 1024)),
            "cvec": np.ascontiguousarray(cvec),
        })
        in_maps.append(m)
    return in_maps


def _unp(a, n):
    return np.ascontiguousarray(a.T).reshape(-1)


def run(inputs, NT, cores=None):
    if NT not in _NC_CACHE:
        _NC_CACHE[NT] = build(NT)
    nc = _NC_CACHE[NT]
    in_maps = _host_layout(inputs, NT)
    if cores is not None:
        in_maps = [in_maps[c] for c in cores]
    res = run_bass_kernel_spmd(nc, in_maps, core_ids=list(range(len(in_maps))))
    R = res.results
    B = len(R)
    st = lambda k: np.stack([np.asarray(r[k], np.float32) for r in R])
    yp = st("yp"); ys = st("ys")
    nkp = st("nkp").reshape(1, B, 512, 8, 128); nvp = st("nvp").reshape(1, B, 512, 8, 128)
    nks = st("nks").reshape(1, B, 64, 8, 128); nvs = st("nvs").reshape(1, B, 64, 8, 128)

    def conv(k):
        a = st(k).reshape(B, 128, 8, 3)
        return np.ascontiguousarray(a.transpose(0, 3, 2, 1)).reshape(1, B, 3, 1024)

    def lru(k):
        a = st(k)
        return np.ascontiguousarray(a.transpose(0, 2, 1)).reshape(1, B, 1024)
    return (yp, ys, nkp, nvp, conv("ncp"), lru("nhp"), nks, nvs, conv("ncs"), lru("nhs"))


def kernel(**inputs):
    return run(inputs, 16)
```

```python
import numpy as np
from contextlib import ExitStack
import concourse.bass as bass
import concourse.mybir as mybir
from concourse.bass_utils import run_bass_kernel_spmd

F32 = mybir.dt.float32
BF16 = mybir.dt.bfloat16
AF = mybir.ActivationFunctionType
ALU = mybir.AluOpType

ALPHA = 2.0 ** 0.25
ATT_SCALE = 128.0 ** -0.5
LN_EPS = 1e-5
NEG = -30000.0


class Res:
    __slots__ = ("name", "w", "r", "ro")

    def __init__(self, name, ro=False):
        self.name = name
        self.w = None
        self.r = []
        self.ro = ro


class Slot:
    __slots__ = ("sem", "count")

    def __init__(self, sem):
        self.sem = sem
        self.count = 0


class Prog:
    ENG = ("pe", "act", "dve", "pool", "sp")
    LOOK = 24
    WIN = 150.0
    LAT_TO_PE = 3000.0
    LAT_FROM_PE = 300.0
    LAT_X = 600.0

    def __init__(self, nc, es):
        self.nc = nc
        self.es = es
        self.ops = []
        self.sem = {e: es.enter_context(nc.semaphore("s_" + e)) for e in self.ENG}
        self.nslot = 0
        self.final_slots = []
        self.defn = 512

    def slot(self):
        self.nslot += 1
        return Slot(self.es.enter_context(self.nc.semaphore("d%d" % self.nslot)))

    def cost(self, eng, slot, n):
        if slot is not None:
            return float(n if n is not None else 1 << 20)
        n = self.defn if n is None else n
        if eng == "pe":
            return n / 2.4 + 10.0
        if eng == "act":
            return 230.0 + n / 1.15
        if eng == "dve":
            return 120.0 + n / 0.9
        return 300.0 + n / 0.45

    SERVED = {0: ("exp", "tanh"), 2: ("tanh", "sigmoid"), 3: ("sqrt",), 11: ("gelu", "tanh"), 5: ("ln",), 18: ("silu", "tanh")}
    LOWEST = {"exp": 0, "tanh": 0, "sigmoid": 2, "sqrt": 3, "gelu": 11, "ln": 5, "silu": 18}

    def op(self, eng, fn, reads=(), writes=(), slot=None, n=None, tbl=None, delay=0.0):
        idx = len(self.ops)
        deps = set()
        for r in reads:
            if r.w is not None:
                deps.add(r.w)
        for w in writes:
            if w.w is not None:
                deps.add(w.w)
            deps.update(w.r)
        self.ops.append((eng, fn, slot, deps, self.cost(eng, slot, n), tbl, delay))
        for r in reads:
            if not r.ro:
                r.r.append(idx)
        for w in writes:
            w.w = idx
            w.r = []
        return idx

    def fence(self, frm, to):
        toks = []
        for f in frm:
            if f.w is not None:
                toks.append(f.w)
            toks.extend(f.r)
        for t in to:
            t.r.extend(toks)

    def schedule(self):
        import bisect
        ops = self.ops
        N = len(ops)
        succ = [[] for _ in range(N)]
        indeg = [0] * N
        for i, o in enumerate(ops):
            for d in o[3]:
                succ[d].append(i)
            indeg[i] = len(o[3])
        ready = {e: [] for e in self.ENG}
        rtime = [0.0] * N
        fin = [0.0] * N
        efree = {e: 0.0 for e in self.ENG}
        order = {e: [] for e in self.ENG}
        for i in range(N):
            if indeg[i] == 0:
                ready[ops[i][0]].append(i)
        bw_free = 0.0
        left = N
        LOOK, WIN = self.LOOK, self.WIN
        cur_set = -1
        SERVED, LOWEST = self.SERVED, self.LOWEST
        while left:
            best = None
            for e in self.ENG:
                L = ready[e]
                if not L:
                    continue
                te = efree[e]
                c = None
                cr = None
                cfall = None
                for i in L[:LOOK]:
                    rt = rtime[i]
                    if rt <= te + WIN:
                        if e == "act":
                            tb = ops[i][5]
                            if tb is not None and (cur_set < 0 or tb not in SERVED[cur_set]):
                                if cfall is None:
                                    cfall = i
                                continue
                        c = i
                        break
                    if cr is None or rt < cr:
                        cr = rt
                        c2 = i
                if c is None:
                    c = cfall if cfall is not None else c2
                st = te if rtime[c] < te else rtime[c]
                if best is None or (st, c) < (best[0], best[2]):
                    best = (st, e, c)
            st, e, c = best
            L = ready[e]
            L.pop(bisect.bisect_left(L, c))
            o = ops[c]
            if o[2] is not None:
                issue = 60.0 if e == "sp" else 600.0
                efree[e] = st + issue
                b0 = bw_free if bw_free > st else st
                bw_free = b0 + o[4] / 160.0
                f = max(st + 2000.0, bw_free)
            else:
                dur = o[4]
                if e == "act" and o[5] is not None and (cur_set < 0 or o[5] not in SERVED[cur_set]):
                    cur_set = LOWEST[o[5]]
                    dur += 1283.0
                    self.n_tbl = getattr(self, "n_tbl", 0) + 1
                efree[e] = st + dur
                f = st + dur + (60.0 if e == "pe" else 0.0)
            fin[c] = f
            order[e].append(c)
            left -= 1
            for s_ in succ[c]:
                ce = ops[s_][0]
                if o[2] is not None:
                    lat = f + 200.0
                elif ce == e:
                    lat = f
                elif ce == "pe":
                    lat = f + self.LAT_TO_PE
                elif e == "pe":
                    lat = f + self.LAT_FROM_PE
                else:
                    lat = f + self.LAT_X
                if lat > rtime[s_]:
                    rtime[s_] = lat
                indeg[s_] -= 1
                if indeg[s_] == 0:
                    rtime[s_] += ops[s_][6]
                    bisect.insort(ready[ops[s_][0]], s_)
        self.est_ns = max(fin) if fin else 0.0
        return order

    def emit(self):
        nc = self.nc
        ops = self.ops
        order = self.schedule()
        tok = [None] * len(ops)
        sig = [False] * len(ops)
        for i, o in enumerate(ops):
            for d in o[3]:
                if ops[d][2] is None and not (o[0] == "pe" and ops[d][0] == "pe"):
                    sig[d] = True
        cnt = {e: 0 for e in self.ENG}
        for e in self.ENG:
            for i in order[e]:
                sl = ops[i][2]
                if sl is None:
                    if sig[i]:
                        cnt[e] += 1
                        tok[i] = (self.sem[e], cnt[e], e)
                else:
                    sl.count += 16
                    tok[i] = (sl.sem, sl.count, None)
        self.n_sig = sum(sig)

        def run(eng, name):
            waited = {}
            own = self.sem[name]
            for i in order[name]:
                o = ops[i]
                need = {}
                for d in o[3]:
                    if name == "pe" and ops[d][0] == "pe" and ops[d][2] is None:
                        continue
                    s, v, de = tok[d]
                    if need.get(s, 0) < v:
                        need[s] = v
                for s, v in need.items():
                    if waited.get(s, 0) < v:
                        waited[s] = v
                        eng.wait_ge(s, v)
                ins = o[1](eng)
                t = tok[i]
                if t is not None:
                    ins.then_inc(t[0], 16 if o[2] is not None else 1)
            if name == "sp":
                for sl in self.final_slots:
                    eng.wait_ge(sl.sem, sl.count)

        with nc.Block() as block:
            @block.tensor
            def _(e):
                run(e, "pe")

            @block.scalar
            def _(e):
                run(e, "act")

            @block.vector
            def _(e):
                run(e, "dve")

            @block.gpsimd
            def _(e):
                run(e, "pool")

            @block.sync
            def _(e):
                run(e, "sp")


V_CW, V_CB, V_BRG, V_BIG, V_LAM, V_L1G, V_L1B, V_L2G, V_L2B, V_BD, V_BUP, V_BADA, NV = \
    0, 32, 40, 48, 56, 64, 72, 80, 88, 96, 104, 136, 184
C_C, C_LRU, C_CONV, NCV = 0, 16, 24, 48


def build(NT):
    nc = bass.Bass("TRN2", target_bir_lowering=False)
    SEQ = NT * 512

    def din(name, shape, dt=F32):
        return nc.dram_tensor(name, shape, dt, kind="ExternalInput").ap()

    def dout(name, shape):
        return nc.dram_tensor(name, shape, F32, kind="ExternalOutput").ap()

    xp = din("xp", [SEQ, 1024]); xs = din("xs", [64, 1024])
    ck = din("ck", [512, 1024]); cv = din("cv", [512, 1024])
    vecs_d = din("vecs", [128, NV]); cvec_d = din("cvec", [128, NCV])
    biasM_d = din("biasM", [128, 8 * 3 * 128]); cbias_d = din("cbias", [128, 8])
    w_ada = din("w_ada", [1024, 6144]); w_in = din("w_in", [1024, 7168])
    w_rg = din("w_rg", [8, 128, 128]); w_ig = din("w_ig", [8, 128, 128])
    w_out = din("w_out", [1024, 1024]); w_up = din("w_up", [1024, 4096]); w_down = din("w_down", [4096, 1024])
    yp = dout("yp", [SEQ, 1024]); ys = dout("ys", [64, 1024])
    nkp = dout("nkp", [512, 1024]); nvp = dout("nvp", [512, 1024])
    ncp = dout("ncp", [128, 24]); nhp = dout("nhp", [128, 8])
    nks = dout("nks", [64, 1024]); nvs = dout("nvs", [64, 1024])
    ncs = dout("ncs", [128, 24]); nhs = dout("nhs", [128, 8])
    win_b = nc.dram_tensor("win_b", [14, 128, 8, 512], BF16, kind="Internal").ap()
    wout_b = nc.dram_tensor("wout_b", [2, 128, 8, 512], BF16, kind="Internal").ap()
    wup_b = nc.dram_tensor("wup_b", [8, 128, 8, 512], BF16, kind="Internal").ap()
    wdn_b = nc.dram_tensor("wdn_b", [8, 128, 32, 128], BF16, kind="Internal").ap()

    with ExitStack() as es:
        P = Prog(nc, es)

        def sb(name, shape, dt=F32):
            return nc.alloc_sbuf_tensor(name, shape, dt).ap()

        xst = sb("xst", [128, 2, 1024]); Rxst = [Res("xst0"), Res("xst1")]; Sxst = [P.slot(), P.slot()]
        ost = sb("ost", [128, 2, 1024]); Rost = [Res("ost0"), Res("ost1")]; Sost = [P.slot(), P.slot()]
        resid = sb("resid", [128, 8, 512]); Rres = [Res("res%d" % i) for i in range(8)]
        uT = sb("uT", [128, 8, 512], BF16); RuT = [Res("uT%d" % i) for i in range(8)]
        u2T = sb("u2T", [128, 8, 512], BF16); Ru2 = [Res("u2T%d" % i) for i in range(8)]
        big = sb("big", [128, 16384], BF16)
        hT = big.rearrange("p (m t) -> p m t", t=512); RhT = [Res("hT%d" % i) for i in range(32)]
        kTr = sb("kTr", [128, 8, 1024], BF16); RkT = [Res("kT%d" % i) for i in range(8)]
        Vr = sb("Vr", [128, 8, 8, 129], BF16); RV = [Res("V%d" % i) for i in range(8)]
        biasM = sb("biasM_s", [128, 8, 3, 128]); cbias = sb("cbias_s", [128, 8])
        pT = sb("pT", [128, 2, 5, 128], BF16); RpT = [Res("pT0"), Res("pT1")]
        sbm = sb("sbm", [128, 2, 3, 128]); Rsbm = [Res("sbm0"), Res("sbm1")]
        osb = sb("osb", [128, 2, 128], BF16); Rosb = [Res("osb0"), Res("osb1")]
        rc = sb("rc", [128, 2]); Rrc = [Res("rc0"), Res("rc1")]
        mtmp = sb("mtmp", [128, 2, 128]); Rmt = [Res("mt0"), Res("mt1")]
        xb = sb("xb", [128, 2, 512], BF16); Rxb = [Res("xb0"), Res("xb1")]
        sq = sb("sq", [128, 2, 512], BF16); Rsq = [Res("sq0"), Res("sq1")]
        lnv = sb("lnv", [128, 512]); lnn = sb("lnn", [128, 512])
        lnt = sb("lnt", [128, 2, 512]); Rlnt = [Res("lnt0"), Res("lnt1")]
        Rlnm, Rlnv, Rlnn = Res("lnm"), Res("lnv"), Res("lnn")
        Rln2m = Res("ln2m")
        tg = sb("tg", [128, 2, 4, 512]); Rtg = [[Res("tg%d_%d" % (p_, i)) for i in range(4)] for p_ in range(2)]
        tgs = sb("tgs", [128, 2, 512]); Rtgs = [Res("tgs0"), Res("tgs1")]
        xcb = sb("xcb", [128, 2, 512], BF16); Rxcb = [Res("xcb0"), Res("xcb1")]
        xrb = sb("xrb", [128, 2, 515]); Rxrb = [Res("xrb0"), Res("xrb1")]
        hist = sb("hist", [128, 8, 3]); Rhist = [Res("hist%d" % i) for i in range(8)]
        hst = sb("hst", [128, 8]); Rhst = [Res("hst%d" % i) for i in range(8)]
        NWB = 3
        wbuf = sb("wbuf", [128, NWB, 8, 512], BF16); Rwb = [Res("wb%d" % i) for i in range(NWB)]; Swb = [P.slot() for _ in range(NWB)]; Swbp = [P.slot() for _ in range(NWB)]
        wrg = sb("wrg", [128, 8, 128], BF16); wig = sb("wig", [128, 8, 128], BF16); Rwg = Res("wg"); Swg = P.slot()
        vecs = sb("vecs_s", [128, NV]); cvec = sb("cvec_s", [128, NCV]); Rvec = Res("vec"); Svec = P.slot()
        mod = sb("mod", [128, 48, 2]); Rmod = Res("mod")
        cst = sb("cst", [128, 136]); Rcst = Res("cst")
        csil = sb("csil", [128, 8, 2]); Rcsil = Res("csil")
        identF = sb("identF", [128, 128]); identB = sb("identB", [128, 128], BF16); onesS = sb("onesS", [128, 128], BF16)
        Rid = Res("ident")

        def pb(name, dt=F32, n=512):
            return nc.alloc_psum_tensor(name, [128, n], dt).ap()
        pt = [pb("pt0"), pb("pt1")]; Rpt = [Res("pt0"), Res("pt1")]
        pz = [pb("pz0"), pb("pz1")]; Rpz = [Res("pz0"), Res("pz1")]
        pa = [pb("pa0"), pb("pa1")]; Rpa = [Res("pa0"), Res("pa1")]
        po = pb("po"); Rpo = Res("po")
        pob = pb("pob", BF16, 1024); Rpob = Res("pob")

        Rs_in = [Res("s_in%d" % g) for g in range(14)]; Ss_in = [P.slot() for _ in range(14)]
        Rs_out = [Res("s_out%d" % g) for g in range(2)]; Ss_out = [P.slot() for _ in range(2)]
        Rs_up = [Res("s_up%d" % g) for g in range(8)]; Ss_up = [P.slot() for _ in range(8)]
        Rs_dn = [Res("s_dn%d" % g) for g in range(8)]; Ss_dn = [P.slot() for _ in range(8)]
        Scv = [P.slot() for _ in range(4)]
        Sout = [P.slot() for _ in range(4)]

        P.op("sp", lambda e: e.dma_start(out=vecs, in_=vecs_d), writes=[Rvec], slot=Svec)
        P.op("sp", lambda e: e.dma_start(out=cvec, in_=cvec_d), writes=[Rvec], slot=Svec)
        P.op("sp", lambda e: e.dma_start(out=biasM.rearrange("p a b c -> p (a b c)"), in_=biasM_d), writes=[Rvec], slot=Svec)
        P.op("sp", lambda e: e.dma_start(out=cbias, in_=cbias_d), writes=[Rvec], slot=Svec)
        P.op("act", lambda e: e.activation(out=biasM.rearrange("p a b c -> p (a b c)"), in_=biasM.rearrange("p a b c -> p (a b c)"), func=AF.Exp),
             writes=[Rvec], tbl="exp", n=3072)
        P.op("pool", lambda e: e.dma_start(out=wrg, in_=w_rg.rearrange("n c d -> c n d")), writes=[Rwg], slot=Swg)
        P.op("pool", lambda e: e.dma_start(out=wig, in_=w_ig.rearrange("n c d -> c n d")), writes=[Rwg], slot=Swg)
        P.op("pool", lambda e: e.memset(identF, 0.0), writes=[Rid])
        P.op("pool", lambda e: e.affine_select(out=identF, in_=identF, pattern=[[-1, 128]], compare_op=ALU.not_equal,
                                                fill=1.0, base=0, channel_multiplier=1), writes=[Rid])
        P.op("pool", lambda e: e.tensor_copy(identB, identF), writes=[Rid])
        P.op("pool", lambda e: e.memset(onesS, 1.0 / 1024.0), writes=[Rid])
        P.op("pool", lambda e: e.memset(kTr.rearrange("p a b -> p (a b)"), 0.0), writes=RkT)
        P.op("pool", lambda e: e.memset(Vr.rearrange("p a b c -> p (a b c)"), 0.0), writes=RV)
        P.op("pool", lambda e: e.memset(Vr[:, :, :, 128:129].rearrange("p a b c -> p (a b c)"), 1.0), writes=RV)
        P.op("pool", lambda e: e.memset(pT.rearrange("p a b c -> p (a b c)"), 0.0), writes=RpT)

        P.op("act", lambda e: e.activation(out=csil.rearrange("p k j -> p j k"),
                                           in_=cvec[:, C_C:C_C + 16].rearrange("p (j k) -> p j k", j=2), func=AF.Silu),
             reads=[Rvec], writes=[Rcsil], tbl="silu")
        csil_b = sb("csil_b", [128, 8, 2], BF16)
        P.op("dve", lambda e: e.tensor_copy(csil_b.rearrange("p k j -> p (k j)"), csil.rearrange("p k j -> p (k j)")), reads=[Rcsil], writes=[Rcsil], n=16)
        Smod = [P.slot() for _ in range(4)]
        stg = [(xst[:, 0, :], Rxst[0]), (xst[:, 1, :], Rxst[1]), (ost[:, 0, :], Rost[0]), (ost[:, 1, :], Rost[1])]
        for c in range(24):
            buf, Rb = stg[c % 4]
            wv = buf.bitcast(BF16).rearrange("p (k c) -> p k c", k=8)
            P.op("pool", lambda e, c=c, wv=wv: e.dma_start(out=wv, in_=w_ada[:, c * 256:(c + 1) * 256].rearrange("(k p) c -> p k c", p=128)),
                 writes=[Rb], slot=Smod[c % 4], n=1 << 20)
            for jb in range(2):
                n = c * 2 + jb
                for k in range(8):
                    P.op("pe", lambda e, k=k, n=n, jb=jb, wv=wv: e.matmul(pa[0][:, n * 2:n * 2 + 2], wv[:, k, jb * 128:(jb + 1) * 128], csil_b[:, k, :], start=(k == 0), stop=(k == 7)),
                         reads=[Rb, Rcsil], writes=[Rpa[0]], n=200)
        P.op("dve", lambda e: e.tensor_tensor(mod[:, :, 0], pa[0][:, 0:96].rearrange("p (n j) -> p n j", j=2)[:, :, 0], vecs[:, V_BADA:V_BADA + 48], ALU.add),
             reads=[Rvec], writes=[Rpa[0], Rmod])
        P.op("dve", lambda e: e.tensor_tensor(mod[:, :, 1], pa[0][:, 0:96].rearrange("p (n j) -> p n j", j=2)[:, :, 1], vecs[:, V_BADA:V_BADA + 48], ALU.add),
             reads=[Rvec], writes=[Rpa[0], Rmod])
        CL, GA = 0, 8
        def CJ(j, i):
            return 16 + j * 48 + i * 8 - 0
        P.op("act", lambda e: e.activation(out=cst[:, CL:CL + 8], in_=vecs[:, V_LAM:V_LAM + 8], func=AF.Exp, scale=-1.0), reads=[Rvec], writes=[Rcst], tbl="exp")
        P.op("act", lambda e: e.activation(out=cst[:, CL:CL + 8], in_=cst[:, CL:CL + 8], func=AF.Ln, bias=1.0, scale=1.0), writes=[Rcst], tbl="ln")
        P.op("dve", lambda e: e.tensor_scalar(cst[:, CL:CL + 8], cst[:, CL:CL + 8], -8.0, None, ALU.mult), writes=[Rcst])
        P.op("dve", lambda e: e.tensor_scalar(cst[:, GA:GA + 8], vecs[:, V_L1G:V_L1G + 8], ALPHA, None, ALU.mult), reads=[Rvec], writes=[Rcst])
        CL2, HBR, HBI = 112, 120, 128
        P.op("dve", lambda e: e.tensor_scalar(cst[:, CL2:CL2 + 8], cst[:, CL:CL + 8], 0.5, None, ALU.mult), writes=[Rcst])
        P.op("dve", lambda e: e.tensor_scalar(cst[:, HBR:HBR + 8], vecs[:, V_BRG:V_BRG + 8], 0.5, None, ALU.mult), reads=[Rvec], writes=[Rcst])
        P.op("dve", lambda e: e.tensor_scalar(cst[:, HBI:HBI + 8], vecs[:, V_BIG:V_BIG + 8], 0.5, None, ALU.mult), reads=[Rvec], writes=[Rcst])
        for j in range(2):
            def M(blk, j=j):
                return mod[:, blk * 8:(blk + 1) * 8, j]
            P.op("dve", lambda e, j=j, M=M: e.tensor_scalar(cst[:, CJ(j, 0):CJ(j, 0) + 8], M(1), 1.0, 1.0 / ALPHA, ALU.add, ALU.mult), reads=[Rmod], writes=[Rcst])
            P.op("dve", lambda e, j=j, M=M: e.tensor_scalar(cst[:, CJ(j, 1):CJ(j, 1) + 8], M(2), 1.0, 0.5, ALU.add, ALU.mult), reads=[Rmod], writes=[Rcst])
            P.op("dve", lambda e, j=j, M=M: e.tensor_scalar(cst[:, CJ(j, 4):CJ(j, 4) + 8], M(5), 1.0, None, ALU.add), reads=[Rmod], writes=[Rcst])
            P.op("dve", lambda e, j=j, M=M: e.tensor_scalar(cst[:, CJ(j, 3):CJ(j, 3) + 8], M(4), 1.0, None, ALU.add), reads=[Rmod], writes=[Rcst])
            P.op("dve", lambda e, j=j: e.tensor_tensor(cst[:, CJ(j, 2):CJ(j, 2) + 8], vecs[:, V_L1G:V_L1G + 8], cst[:, CJ(j, 3):CJ(j, 3) + 8], ALU.mult), reads=[Rvec], writes=[Rcst])
            P.op("dve", lambda e, j=j: e.tensor_tensor(cst[:, CJ(j, 3):CJ(j, 3) + 8], vecs[:, V_L1B:V_L1B + 8], cst[:, CJ(j, 3):CJ(j, 3) + 8], ALU.mult), reads=[Rvec], writes=[Rcst])
            P.op("dve", lambda e, j=j, M=M: e.tensor_tensor(cst[:, CJ(j, 3):CJ(j, 3) + 8], cst[:, CJ(j, 3):CJ(j, 3) + 8], M(3), ALU.add), reads=[Rmod], writes=[Rcst])
            P.op("dve", lambda e, j=j: e.tensor_tensor(cst[:, CJ(j, 5):CJ(j, 5) + 8], vecs[:, V_BD:V_BD + 8], cst[:, CJ(j, 4):CJ(j, 4) + 8], ALU.mult), reads=[Rvec], writes=[Rcst])
            P.op("dve", lambda e, j=j: e.scalar_tensor_tensor(cst[:, CJ(j, 5):CJ(j, 5) + 8], vecs[:, V_L1B:V_L1B + 8], ALPHA, cst[:, CJ(j, 5):CJ(j, 5) + 8], ALU.mult, ALU.add), reads=[Rvec], writes=[Rcst])

        for r_ in [Rvec, Rcst, Rmod, Rid, Rwg, Rcsil]:
            r_.ro = True
        wq = [0]

        converted = set()
        f32src = {}
        for g in range(14):
            f32src[id(Rs_in[g])] = (w_in[:, g * 512:(g + 1) * 512].rearrange("(k p) c -> p k c", p=128), Ss_in[g])
        for g in range(2):
            f32src[id(Rs_out[g])] = (w_out[:, g * 512:(g + 1) * 512].rearrange("(k p) c -> p k c", p=128), Ss_out[g])
        for g in range(8):
            f32src[id(Rs_up[g])] = (w_up[:, g * 512:(g + 1) * 512].rearrange("(k p) c -> p k c", p=128), Ss_up[g])

        def stream(src, Rsrc):
            i = wq[0] % NWB
            wq[0] += 1
            if id(Rsrc) not in converted:
                converted.add(id(Rsrc))
                fsrc, Ssrc = f32src[id(Rsrc)]
                P.op("pool", lambda e: e.dma_start(out=wbuf[:, i], in_=fsrc), writes=[Rwb[i]], slot=Swbp[i], n=2 << 20)
                P.op("sp", lambda e: e.dma_start(out=src, in_=wbuf[:, i]), reads=[Rwb[i]], writes=[Rsrc], slot=Ssrc)
            else:
                P.op("sp", lambda e: e.dma_start(out=wbuf[:, i], in_=src), reads=[Rsrc], writes=[Rwb[i]], slot=Swb[i])
            return i

        zq = [0]

        pzz = [pz[0], pz[1], pt[0], pt[1]]
        Rpzz = [Rpz[0], Rpz[1], Rpt[0], Rpt[1]]

        def nextz():
            i = zq[0] % 4
            zq[0] += 1
            return i

        Gv = big[:, 0:8192].bitcast(F32).rearrange("p (c t) -> p c t", t=512)
        sgb = big[:, 8192:12288].rearrange("p (c t) -> p c t", t=512)
        qT = big[:, 12288:16384].rearrange("p (c t) -> p c t", t=512)
        RG = [Res("G%d" % i) for i in range(8)]; Rsgb = [Res("sgb%d" % i) for i in range(8)]; RqT = [Res("qT%d" % i) for i in range(8)]

        def col(base, n):
            return cst[:, base + n:base + n + 1]

        def vcol(base, n):
            return vecs[:, base + n:base + n + 1]

        def tile(t, T, j):
            P.defn = T
            sample = (j == 1)
            xsrc = xs if sample else xp[t * 512:(t + 1) * 512, :]
            ydst = ys if sample else yp[t * 512:(t + 1) * 512, :]
            NS = max(1, T // 128)
            RW = min(T, 128)
            last = sample or (t == NT - 1)
            P.fence(RhT, RG + Rsgb + RqT)

            if sample:
                for s in range(4):
                    b_ = s % 2
                    P.op("sp", lambda e, s=s, b_=b_: e.dma_start(out=xst[:, b_, :], in_=ck[s * 128:(s + 1) * 128, :]), writes=[Rxst[b_]], slot=Sxst[b_])
                    for half in range(2):
                        pi_ = half
                        for f4 in range(4):
                            fc = half * 4 + f4
                            P.op("pe", lambda e, b_=b_, fc=fc, f4=f4, pi_=pi_: e.transpose(pt[pi_][:, f4 * 128:(f4 + 1) * 128], xst[:, b_, fc * 128:(fc + 1) * 128], identF),
                                 reads=[Rxst[b_], Rid], writes=[Rpt[pi_]])
                        P.op("act", lambda e, half=half, s=s, pi_=pi_: e.activation(out=kTr[:, half * 4:(half + 1) * 4, s * 128:(s + 1) * 128],
                                                                                   in_=pt[pi_].rearrange("p (f t) -> p f t", t=128), func=AF.Copy),
                             writes=[Rpt[pi_]] + RkT[half * 4:(half + 1) * 4])
                    P.op("pool", lambda e, s=s: e.dma_start(out=Vr[:, s, :, 0:128], in_=cv[s * 128:(s + 1) * 128, :].rearrange("p (h d) -> p h d", d=128)),
                         writes=[RV[s]], slot=Scv[s], n=1 << 19)
                P.op("pool", lambda e: e.tensor_copy(hist.rearrange("p c j -> p j c"), cvec[:, C_CONV:C_CONV + 24].rearrange("p (j c) -> p j c", j=3)),
                     reads=[Rvec], writes=Rhist)
                P.op("pool", lambda e: e.tensor_copy(hst, cvec[:, C_LRU:C_LRU + 8]), reads=[Rvec], writes=Rhst)
            elif t == 0:
                P.op("pool", lambda e: e.memset(hist.rearrange("p c j -> p (c j)"), 0.0), writes=Rhist)
                P.op("pool", lambda e: e.memset(hst, 0.0), writes=Rhst)

            for s in range(NS):
                b_ = s % 2
                P.op("sp", lambda e, s=s, b_=b_: e.dma_start(out=xst[0:RW, b_, :], in_=xsrc[s * 128:s * 128 + RW, :]), writes=[Rxst[b_]], slot=Sxst[b_])
                for half in range(2):
                    pi_ = half
                    for f4 in range(4):
                        fc = half * 4 + f4
                        P.op("pe", lambda e, b_=b_, fc=fc, f4=f4, pi_=pi_: e.transpose(pt[pi_][:, f4 * 128:f4 * 128 + RW], xst[0:RW, b_, fc * 128:(fc + 1) * 128], identF[0:RW, 0:RW]),
                             reads=[Rxst[b_], Rid], writes=[Rpt[pi_]], n=256)
                    P.op("act", lambda e, half=half, s=s, pi_=pi_: e.activation(out=resid[:, half * 4:(half + 1) * 4, s * 128:s * 128 + RW],
                                                                               in_=pt[pi_].rearrange("p (f t) -> p f t", t=128)[:, :, 0:RW], func=AF.Copy, scale=ALPHA),
                         writes=[Rpt[pi_]] + Rres[half * 4:(half + 1) * 4])
            for fc in range(8):
                P.op("dve", lambda e, fc=fc: e.tensor_scalar(uT[:, fc, 0:T], resid[:, fc, 0:T], col(CJ(j, 0), fc), mod[:, fc, j:j + 1], ALU.mult, ALU.add),
                     reads=[Rres[fc], Rcst, Rmod], writes=[RuT[fc]])

            def proj_block(wi, jb, rhs_of_k, Rrhs):
                z = nextz()
                for k in range(8):
                    P.op("pe", lambda e, k=k, z=z: e.matmul(pzz[z][:, 0:T], wbuf[:, wi, k, jb * 128:(jb + 1) * 128], rhs_of_k(k), start=(k == 0), stop=(k == 7)),
                         reads=[Rwb[wi], Rrhs[k]], writes=[Rpzz[z]])
                return z

            uk = lambda k: uT[:, k, 0:T]
            XC, RA, IB, AM = 0, 1, 2, 3

            def chainA(fc, wi, jb):
                xb_ = fc % 2
                tp_ = fc % 2
                Rt = Rtg[tp_]
                tgp = tg[:, tp_]
                z = proj_block(wi, jb, uk, RuT)
                P.op("pool", lambda e: e.tensor_copy(xrb[:, xb_, 0:3], hist[:, fc, :]), reads=[Rhist[fc]], writes=[Rxrb[xb_]], n=8)
                P.op("act", lambda e: e.activation(out=xrb[:, xb_, 3:3 + T], in_=pzz[z][:, 0:T], func=AF.Copy), writes=[Rpzz[z], Rxrb[xb_]])
                P.op("pool", lambda e: e.tensor_copy(hist[:, fc, :], xrb[:, xb_, T:T + 3]), reads=[Rxrb[xb_]], writes=[Rhist[fc]], n=8)
                P.op("dve", lambda e: e.tensor_scalar(tgp[:, XC, 0:T], xrb[:, xb_, 0:T], vcol(V_CW, fc), vcol(V_CB, fc), ALU.mult, ALU.add),
                     reads=[Rxrb[xb_], Rvec], writes=[Rt[XC]])
                for jj in (1, 2, 3):
                    P.op("dve", lambda e, jj=jj: e.scalar_tensor_tensor(tgp[:, XC, 0:T], xrb[:, xb_, jj:jj + T], vcol(V_CW + 8 * jj, fc), tgp[:, XC, 0:T], ALU.mult, ALU.add),
                         reads=[Rxrb[xb_], Rvec], writes=[Rt[XC]])
                P.op("act", lambda e: e.activation(out=xcb[:, tp_, 0:T], in_=tgp[:, XC, 0:T], func=AF.Copy), reads=[Rt[XC]], writes=[Rxcb[tp_]])
                P.op("pe", lambda e: e.matmul(pa[0][:, 0:T], wrg[:, fc, :], xcb[:, tp_, 0:T], start=True, stop=True), reads=[Rwg, Rxcb[tp_]], writes=[Rpa[0]], delay=5000.0)
                P.op("pe", lambda e: e.matmul(pa[1][:, 0:T], wig[:, fc, :], xcb[:, tp_, 0:T], start=True, stop=True), reads=[Rwg, Rxcb[tp_]], writes=[Rpa[1]])
                P.op("act", lambda e: e.activation(out=tgp[:, RA, 0:T], in_=pa[0][:, 0:T], func=AF.Tanh, bias=col(HBR, fc), scale=0.5), reads=[Rcst], writes=[Rpa[0], Rt[RA]], tbl="tanh")
                P.op("act", lambda e: e.activation(out=tgp[:, IB, 0:T], in_=pa[1][:, 0:T], func=AF.Tanh, bias=col(HBI, fc), scale=0.5), reads=[Rcst], writes=[Rpa[1], Rt[IB]], tbl="tanh")
                P.op("act", lambda e: e.activation(out=tgp[:, RA, 0:T], in_=tgp[:, RA, 0:T], func=AF.Exp, scale=col(CL2, fc), bias=col(CL2, fc)), reads=[Rcst], writes=[Rt[RA]], tbl="exp")
                P.op("pool", lambda e: e.tensor_tensor(tgp[:, AM, 0:T], tgp[:, RA, 0:T], tgp[:, RA, 0:T], ALU.mult), reads=[Rt[RA]], writes=[Rt[AM]])
                P.op("dve", lambda e: e.scalar_tensor_tensor(tgp[:, IB, 0:T], tgp[:, IB, 0:T], 1.0, tgp[:, XC, 0:T], ALU.add, ALU.mult), reads=[Rt[XC]], writes=[Rt[IB]])

            def chainB(fc):
                tp_ = fc % 2
                Rt = Rtg[tp_]
                tgp = tg[:, tp_]
                if (not sample) and t == 0:
                    P.op("pool", lambda e: e.memset(tgp[:, AM, 0:1], 0.5), writes=[Rt[AM]], n=1)
                    P.op("pool", lambda e: e.memset(tgp[:, RA, 0:1], 0.0), writes=[Rt[RA]], n=1)
                P.op("pool", lambda e: e.tensor_tensor(tgp[:, IB, 0:T], tgp[:, IB, 0:T], tgp[:, AM, 0:T], ALU.mult), reads=[Rt[AM]], writes=[Rt[IB]])
                P.op("dve", lambda e: e.tensor_tensor_scan(Gv[:, fc, 0:T], tgp[:, RA, 0:T], tgp[:, IB, 0:T], hst[:, fc:fc + 1], ALU.mult, ALU.add),
                     reads=[Rt[RA], Rt[IB], Rhst[fc]], writes=[RG[fc]], n=2 * T)
                P.op("pool", lambda e: e.tensor_copy(hst[:, fc:fc + 1], Gv[:, fc, T - 1:T]), reads=[RG[fc]], writes=[Rhst[fc]], n=1)

            def emit_xr(g, jp):
                wi = stream(win_b[g], Rs_in[g])
                if True:
                    fcs = (g * 4 + 2 * jp, g * 4 + 2 * jp + 1)
                    for fc in fcs:
                        chainA(fc, wi, fc % 4)
                    P.op("act", lambda e: e.activation(out=tg[:, :, AM, 0:T], in_=tg[:, :, AM, 0:T], func=AF.Sqrt, bias=0.25, scale=-0.25),
                         writes=[Rtg[0][AM], Rtg[1][AM]], tbl="sqrt", n=2 * T)
                    for fc in fcs:
                        chainB(fc)
            def emit_gl(g):
                wi = stream(win_b[g], Rs_in[g])
                for jb in range(4):
                    fc = (g - 2) * 4 + jb
                    z = proj_block(wi, jb, uk, RuT)
                    sp_ = fc % 2
                    P.op("act", lambda e, z=z, sp_=sp_: e.activation(out=tgs[:, sp_, 0:T], in_=pzz[z][:, 0:T], func=AF.Gelu_apprx_tanh), writes=[Rpzz[z], Rtgs[sp_]], tbl="gelu")
                    P.op("pool", lambda e, fc=fc, sp_=sp_: e.tensor_tensor(Gv[:, fc, 0:T], Gv[:, fc, 0:T], tgs[:, sp_, 0:T], ALU.mult), reads=[Rtgs[sp_]], writes=[RG[fc]])
            def emit_ga(g):
                wi = stream(win_b[g], Rs_in[g])
                for jb in range(4):
                    fc = (g - 10) * 4 + jb
                    z = proj_block(wi, jb, uk, RuT)
                    sp_ = fc % 2
                    P.op("act", lambda e, z=z, sp_=sp_: e.activation(out=tgs[:, sp_, 0:T], in_=pzz[z][:, 0:T], func=AF.Tanh, scale=0.5), writes=[Rpzz[z], Rtgs[sp_]], tbl="tanh")
                    P.op("dve", lambda e, fc=fc, sp_=sp_: e.scalar_tensor_tensor(Gv[:, fc, 0:T], tgs[:, sp_, 0:T], 1.0, Gv[:, fc, 0:T], ALU.add, ALU.mult), reads=[Rtgs[sp_]], writes=[RG[fc]])
            def emit_gb(g):
                wi = stream(win_b[g], Rs_in[g])
                for jb in range(4):
                    fc = (g - 12) * 4 + jb
                    z = proj_block(wi, jb, uk, RuT)
                    P.op("act", lambda e, z=z, fc=fc: e.activation(out=sgb[:, fc, 0:T], in_=pzz[z][:, 0:T], func=AF.Tanh, scale=0.5), writes=[Rpzz[z], Rsgb[fc]], tbl="tanh")
            ro = 512 if sample else (t % 2) * 512
            def emit_q(g):
                wi = stream(win_b[g], Rs_in[g])
                for jb in range(4):
                    h = (g - 4) * 4 + jb
                    z = proj_block(wi, jb, uk, RuT)
                    P.op("dve", lambda e, z=z, h=h: e.tensor_scalar(qT[:, h, 0:T], pzz[z][:, 0:T], ATT_SCALE, None, ALU.mult), writes=[Rpzz[z], RqT[h]])
            def emit_k(g):
                wi = stream(win_b[g], Rs_in[g])
                for jb in range(4):
                    h = (g - 6) * 4 + jb
                    z = proj_block(wi, jb, uk, RuT)
                    P.op("dve", lambda e, z=z, h=h: e.tensor_copy(kTr[:, h, ro:ro + T], pzz[z][:, 0:T]), writes=[Rpzz[z], RkT[h]])
                if last:
                    for s in range(NS):
                        z = nextz()
                        for k in range(8):
                            P.op("pe", lambda e, k=k, z=z, s=s, wi=wi: e.matmul(pzz[z][0:RW, :], uT[:, k, s * 128:s * 128 + RW], wbuf[:, wi, k, :], start=(k == 0), stop=(k == 7)),
                                 reads=[Rwb[wi]] + RuT, writes=[Rpzz[z]])
                        ob = (s + g) % 2
                        P.op("dve", lambda e, z=z, ob=ob: e.tensor_copy(ost[0:RW, ob, 0:512], pzz[z][0:RW, :]), writes=[Rpzz[z], Rost[ob]])
                        dst = (nks if sample else nkp)[s * 128:s * 128 + RW, (g - 6) * 512:(g - 5) * 512]
                        P.op("sp", lambda e, ob=ob, dst=dst: e.dma_start(out=dst, in_=ost[0:RW, ob, 0:512]), reads=[Rost[ob]], slot=Sost[ob], n=1 << 18)
            def emit_v(g):
                wi = stream(win_b[g], Rs_in[g])
                for s in range(NS):
                    z = nextz()
                    for k in range(8):
                        P.op("pe", lambda e, k=k, z=z, s=s, wi=wi: e.matmul(pzz[z][0:RW, :], uT[:, k, s * 128:s * 128 + RW], wbuf[:, wi, k, :], start=(k == 0), stop=(k == 7)),
                             reads=[Rwb[wi]] + RuT, writes=[Rpzz[z]])
                    rb = 4 if sample else (t % 2) * 4 + s
                    P.op("dve", lambda e, z=z, rb=rb, g=g: e.tensor_copy(Vr[0:RW, rb, (g - 8) * 4:(g - 7) * 4, 0:128],
                                                                       pzz[z][0:RW, :].rearrange("p (h d) -> p h d", d=128)),
                         writes=[Rpzz[z], RV[rb]])
                    if last:
                        ob = (s + g) % 2
                        P.op("dve", lambda e, z=z, ob=ob: e.tensor_copy(ost[0:RW, ob, 0:512], pzz[z][0:RW, :]), writes=[Rpzz[z], Rost[ob]])
                        dst = (nvs if sample else nvp)[s * 128:s * 128 + RW, (g - 8) * 512:(g - 7) * 512]
                        P.op("sp", lambda e, ob=ob, dst=dst: e.dma_start(out=dst, in_=ost[0:RW, ob, 0:512]), reads=[Rost[ob]], slot=Sost[ob], n=1 << 18)
            emit_xr(0, 0); emit_q(4); emit_q(5)
            emit_xr(0, 1); emit_k(6); emit_k(7)
            emit_xr(1, 0); emit_v(8); emit_v(9)
            emit_xr(1, 1); emit_gb(12); emit_gb(13)
            emit_gl(2); emit_gl(3); emit_ga(10); emit_ga(11)
            if last:
                P.op("sp", lambda e: e.dma_start(out=(ncs if sample else ncp), in_=hist.rearrange("p c j -> p (c j)")), reads=Rhist, slot=Sout[0 if sample else 2], n=4096)
                P.op("sp", lambda e: e.dma_start(out=(nhs if sample else nhp), in_=hst), reads=Rhst, slot=Sout[1 if sample else 3], n=4096)

            NP = 1 if sample else 4
            QW = 64 if sample else 128
            it = 0
            Sset = [(pa[0], pa[1], Rpa[0], Rpa[1]), (pt[0], pt[1], Rpt[0], Rpt[1])]
            Oset = [(po, Rpo), (pz[1], Rpz[1])]
            for pi in range(NP):
                for h in range(8):
                    bb = it % 2
                    it += 1
                    sM, sC, RsM, RsC = Sset[bb]
                    oB, RoB = Oset[bb]
                    blocks = []
                    for b in range(5):
                        if sample:
                            blocks.append((b, b))
                        else:
                            cs = 8 * t + 2 * pi - 8 + 2 * b
                            if cs >= 0:
                                blocks.append((b, (cs // 2) % 8))
                    mpos = {0: 0, 3: 1, 4: 2}
                    cpos = {1: 0, 2: 1}
                    q_ap = qT[:, h, pi * 128:pi * 128 + QW]
                    for b, rb in blocks:
                        if b in mpos:
                            o_ap = sM[:, mpos[b] * 128:mpos[b] * 128 + QW]; R_ = RsM
                        else:
                            o_ap = sC[:, cpos[b] * 128:cpos[b] * 128 + QW]; R_ = RsC
                        P.op("pe", lambda e, o_ap=o_ap, h=h, rb=rb, q_ap=q_ap: e.matmul(o_ap, kTr[:, h, rb * 128:(rb + 1) * 128], q_ap, start=True, stop=True),
                             reads=[RkT[h], RqT[h]], writes=[R_], n=QW + 64)
                    mb = [b for b, _ in blocks if b in mpos]
                    cb = [b for b, _ in blocks if b in cpos]
                    m0 = mpos[mb[0]]
                    nm = len(mb)
                    P.op("act", lambda e, bb=bb, m0=m0, nm=nm, sM=sM: e.activation(
                        out=sbm[:, bb, m0:m0 + nm, 0:QW], in_=sM.rearrange("p (b q) -> p b q", q=128)[:, m0:m0 + nm, 0:QW], func=AF.Exp),
                        writes=[RsM, Rsbm[bb]], n=nm * QW, tbl="exp")
                    P.op("pool", lambda e, bb=bb, h=h, m0=m0, nm=nm: e.tensor_tensor(
                        pT[:, bb, m0:m0 + nm, 0:QW], sbm[:, bb, m0:m0 + nm, 0:QW], biasM[:, h, m0:m0 + nm, 0:QW], ALU.mult),
                        reads=[Rvec, Rsbm[bb]], writes=[RpT[bb]], n=nm * QW)
                    if cb:
                        c0_ = cpos[cb[0]]
                        ncb = len(cb)
                        P.op("act", lambda e, bb=bb, h=h, c0_=c0_, ncb=ncb, sC=sC: e.activation(
                            out=pT[:, bb, 3 + c0_:3 + c0_ + ncb, 0:QW], in_=sC.rearrange("p (b q) -> p b q", q=128)[:, c0_:c0_ + ncb, 0:QW],
                            func=AF.Exp, bias=cbias[:, h:h + 1], scale=1.0),
                            reads=[Rvec], writes=[RsC, RpT[bb]], n=ncb * QW, tbl="exp")
                    for i_, (b, rb) in enumerate(blocks):
                        pidx = mpos[b] if b in mpos else 3 + cpos[b]
                        P.op("pe", lambda e, bb=bb, pidx=pidx, rb=rb, h=h, i_=i_, nb=len(blocks), oB=oB: e.matmul(
                            oB[0:QW, 0:129], pT[:, bb, pidx, 0:QW], Vr[:, rb, h, :], start=(i_ == 0), stop=(i_ == nb - 1)),
                            reads=[RpT[bb], RV[rb]], writes=[RoB], n=129 + 64)
                    P.op("dve", lambda e, bb=bb, oB=oB: e.reciprocal(rc[0:QW, bb:bb + 1], oB[0:QW, 128:129]), writes=[RoB, Rrc[bb]], n=1)
                    P.op("dve", lambda e, bb=bb, oB=oB: e.tensor_scalar(osb[0:QW, bb, :], oB[0:QW, 0:128], rc[0:QW, bb:bb + 1], None, ALU.mult),
                         reads=[Rrc[bb]], writes=[RoB, Rosb[bb]], n=128)
                    P.op("pe", lambda e, bb=bb: e.transpose(pob[:, bb * 128:bb * 128 + QW], osb[0:QW, bb, :], identB[0:QW, 0:QW]), reads=[Rosb[bb], Rid], writes=[Rpob], n=128)
                    P.op("dve", lambda e, bb=bb, h=h, pi=pi: e.scalar_tensor_tensor(mtmp[:, bb, 0:QW], sgb[:, h, pi * 128:pi * 128 + QW], 1.0, pob[:, bb * 128:bb * 128 + QW], ALU.add, ALU.mult),
                         reads=[Rsgb[h]], writes=[Rpob, Rmt[bb]], n=QW)
                    P.op("dve", lambda e, bb=bb, h=h, pi=pi: e.tensor_tensor(uT[:, h, pi * 128:pi * 128 + QW], Gv[:, h, pi * 128:pi * 128 + QW], mtmp[:, bb, 0:QW], ALU.add),
                         reads=[Rmt[bb], RG[h]], writes=[RuT[h]], n=QW)

            def layernorm(emit_out, inplace=False):
                for n in range(8):
                    b2 = n % 2
                    P.op("dve", lambda e, n=n, b2=b2: e.tensor_copy(xb[:, b2, 0:T], resid[:, n, 0:T]), reads=[Rres[n]], writes=[Rxb[b2]])
                    P.op("act", lambda e, n=n, b2=b2: e.activation(out=sq[:, b2, 0:T], in_=resid[:, n, 0:T], func=AF.Square), reads=[Rres[n]], writes=[Rsq[b2]])
                    P.op("pe", lambda e, n=n, b2=b2: e.matmul(pa[0][:, 0:T], onesS, xb[:, b2, 0:T], start=(n == 0), stop=(n == 7)), reads=[Rid, Rxb[b2]], writes=[Rpa[0]])
                    P.op("pe", lambda e, n=n, b2=b2: e.matmul(pa[1][:, 0:T], onesS, sq[:, b2, 0:T], start=(n == 0), stop=(n == 7)), reads=[Rid, Rsq[b2]], writes=[Rpa[1]])
                P.op("act", lambda e: e.activation(out=lnv[:, 0:T], in_=pa[0][:, 0:T], func=AF.Square), writes=[Rpa[0], Rlnv])
                P.op("dve", lambda e: e.tensor_tensor(lnv[:, 0:T], pa[1][:, 0:T], lnv[:, 0:T], ALU.subtract), writes=[Rpa[1], Rlnv])
                P.op("act", lambda e: e.activation(out=lnv[:, 0:T], in_=lnv[:, 0:T], func=AF.Sqrt, bias=LN_EPS, scale=1.0), writes=[Rlnv], tbl="sqrt")
                P.op("dve", lambda e: e.reciprocal(lnv[:, 0:T], lnv[:, 0:T]), writes=[Rlnv])
                P.op("dve", lambda e: e.scalar_tensor_tensor(lnn[:, 0:T], pa[0][:, 0:T], -1.0, lnv[:, 0:T], ALU.mult, ALU.mult), reads=[Rlnv], writes=[Rpa[0], Rlnn])
                if inplace:
                    for n in range(8):
                        P.op("dve", lambda e, n=n: e.tensor_tensor(Gv[:, n, 0:T], resid[:, n, 0:T], lnv[:, 0:T], ALU.mult), reads=[Rres[n], Rlnv], writes=[RG[n], Rln2m])
                    for n in range(8):
                        P.op("pool", lambda e, n=n: e.tensor_tensor(Gv[:, n, 0:T], Gv[:, n, 0:T], lnn[:, 0:T], ALU.add), reads=[Rlnn, Rln2m], writes=[RG[n]])
                        emit_out(n, None)
                    return
                for n in range(8):
                    b2 = n % 2
                    P.op("dve", lambda e, n=n, b2=b2: e.tensor_tensor(lnt[:, b2, 0:T], resid[:, n, 0:T], lnv[:, 0:T], ALU.mult), reads=[Rres[n], Rlnv], writes=[Rlnt[b2]])
                    P.op("dve", lambda e, b2=b2: e.tensor_tensor(lnt[:, b2, 0:T], lnt[:, b2, 0:T], lnn[:, 0:T], ALU.add), reads=[Rlnn], writes=[Rlnt[b2]])
                    emit_out(n, b2)

            mk = lambda k: uT[:, k, 0:T]
            for g in range(2):
                wi = stream(wout_b[g], Rs_out[g])
                for jb in range(4):
                    n = g * 4 + jb
                    z = proj_block(wi, jb, mk, RuT)
                    P.op("dve", lambda e, z=z, n=n: e.scalar_tensor_tensor(resid[:, n, 0:T], pzz[z][:, 0:T], col(CJ(j, 1), n), resid[:, n, 0:T], ALU.mult, ALU.add),
                         reads=[Rcst], writes=[Rpzz[z], Rres[n]])

            def ln1_out(n, b2):
                P.op("act", lambda e, n=n, b2=b2: e.activation(out=u2T[:, n, 0:T], in_=lnt[:, b2, 0:T], func=AF.Identity, scale=col(CJ(j, 2), n), bias=col(CJ(j, 3), n)),
                     reads=[Rlnt[b2], Rcst], writes=[Ru2[n]])
                P.op("act", lambda e, n=n, b2=b2: e.activation(out=resid[:, n, 0:T], in_=lnt[:, b2, 0:T], func=AF.Identity, scale=col(GA, n), bias=col(CJ(j, 5), n)),
                     reads=[Rlnt[b2], Rcst], writes=[Rres[n]])
            layernorm(ln1_out)

            P.fence(RG + Rsgb + RqT, RhT)
            u2k = lambda k: u2T[:, k, 0:T]
            for g in range(8):
                wi = stream(wup_b[g], Rs_up[g])
                if g == 0:
                    zs0 = [nextz() for _ in range(4)]
                    for k in range(8):
                        for jb in range(4):
                            P.op("pe", lambda e, k=k, jb=jb, zz=zs0[jb], wi=wi: e.matmul(pzz[zz][:, 0:T], wbuf[:, wi, k, jb * 128:(jb + 1) * 128], u2T[:, k, 0:T], start=(k == 0), stop=(k == 7)),
                                 reads=[Rwb[wi], Ru2[k]], writes=[Rpzz[zs0[jb]]])
                for jb in range(4):
                    m = g * 4 + jb
                    z = zs0[jb] if g == 0 else proj_block(wi, jb, u2k, Ru2)
                    tb = m % 2
                    P.op("act", lambda e, z=z, m=m, tb=tb: e.activation(out=tgs[:, tb, 0:T], in_=pzz[z][:, 0:T], func=AF.Relu, bias=vcol(V_BUP, m), scale=1.0),
                         reads=[Rvec], writes=[Rpzz[z], Rtgs[tb]])
                    eng = "pool" if m % 2 == 0 else "dve"
                    P.op(eng, lambda e, m=m, tb=tb: e.tensor_tensor(hT[:, m, 0:T], tgs[:, tb, 0:T], tgs[:, tb, 0:T], ALU.mult), reads=[Rtgs[tb]], writes=[RhT[m]])
            for n in range(8):
                i = wq[0] % NWB
                wq[0] += 1
                wd = wbuf[:, i].rearrange("p k c -> p (k c)").rearrange("p (k c) -> p k c", c=128)
                if id(Rs_dn[n]) not in converted:
                    converted.add(id(Rs_dn[n]))
                    for kh in range(2):
                        P.op("pool", lambda e, n=n, wd=wd, kh=kh: e.dma_start(
                            out=wd[:, kh * 16:(kh + 1) * 16, :],
                            in_=w_down[kh * 2048:(kh + 1) * 2048, n * 128:(n + 1) * 128].rearrange("(k p) c -> p k c", p=128)),
                            writes=[Rwb[i]], slot=Swbp[i], n=1 << 20)
                    P.op("sp", lambda e, n=n, wd=wd: e.dma_start(out=wdn_b[n], in_=wd), reads=[Rwb[i]], writes=[Rs_dn[n]], slot=Ss_dn[n])
                else:
                    P.op("sp", lambda e, i=i, n=n: e.dma_start(out=wbuf[:, i].rearrange("p k c -> p (k c)"), in_=wdn_b[n].rearrange("p k c -> p (k c)")),
                         reads=[Rs_dn[n]], writes=[Rwb[i]], slot=Swb[i])
                z = nextz()
                for k in range(32):
                    P.op("pe", lambda e, k=k, z=z, wd=wd: e.matmul(pzz[z][:, 0:T], wd[:, k, :], hT[:, k, 0:T], start=(k == 0), stop=(k == 31)),
                         reads=[Rwb[i], RhT[k]], writes=[Rpzz[z]])
                P.op("dve", lambda e, z=z, n=n: e.scalar_tensor_tensor(resid[:, n, 0:T], pzz[z][:, 0:T], col(CJ(j, 4), n), resid[:, n, 0:T], ALU.mult, ALU.add),
                     reads=[Rcst], writes=[Rpzz[z], Rres[n]])

            def ln2_out(n, b2):
                P.op("act", lambda e, n=n: e.activation(out=Gv[:, n, 0:T], in_=Gv[:, n, 0:T], func=AF.Identity, scale=vcol(V_L2G, n), bias=vcol(V_L2B, n)),
                     reads=[Rvec], writes=[RG[n]])
            P.fence(RhT, RG)
            layernorm(ln2_out, inplace=True)

            for s in range(NS):
                ob = s % 2
                for half in range(2):
                    pi_ = half
                    for f4 in range(4):
                        fc = half * 4 + f4
                        P.op("pe", lambda e, s=s, fc=fc, f4=f4, pi_=pi_: e.transpose(pa[pi_][0:RW, f4 * 128:(f4 + 1) * 128], Gv[:, fc, s * 128:s * 128 + RW], identF),
                             reads=[RG[fc], Rid], writes=[Rpa[pi_]], n=256)
                    eng = "act" if half == 0 else "dve"
                    if eng == "act":
                        P.op("act", lambda e, ob=ob, half=half, pi_=pi_: e.activation(out=ost[0:RW, ob, half * 512:(half + 1) * 512], in_=pa[pi_][0:RW, :], func=AF.Copy),
                             writes=[Rpa[pi_], Rost[ob]])
                    else:
                        P.op("dve", lambda e, ob=ob, half=half, pi_=pi_: e.tensor_copy(ost[0:RW, ob, half * 512:(half + 1) * 512], pa[pi_][0:RW, :]),
                             writes=[Rpa[pi_], Rost[ob]])
                P.op("sp", lambda e, ob=ob, s=s: e.dma_start(out=ydst[s * 128:s * 128 + RW, :], in_=ost[0:RW, ob, :]), reads=[Rost[ob]], slot=Sost[ob])

        for t in range(NT):
            tile(t, 512, 0)
            if t == 0:
                for r_ in Rs_in + Rs_out + Rs_up + Rs_dn:
                    r_.ro = True
        tile(None, 64, 1)

        P.final_slots = Sost + Sout
        P.emit()
        build.last = (P.est_ns, getattr(P, "n_tbl", 0), len(P.ops))
    return nc


_NC_CACHE = {}


def _pcol(v):
    v = np.asarray(v, np.float32).reshape(-1, 128)
    return np.ascontiguousarray(v.T)


def _host_layout(inputs, NT):
    f = lambda k: np.asarray(inputs[k], np.float32)
    conv_w = f("conv_w")[0]
    vecs = np.concatenate([
        _pcol(conv_w[0]), _pcol(conv_w[1]), _pcol(conv_w[2]), _pcol(conv_w[3]),
        _pcol(f("conv_b")[0]), _pcol(f("b_rg")[0].reshape(-1)), _pcol(f("b_ig")[0].reshape(-1)), _pcol(f("lru_lambda")[0]),
        _pcol(f("ln1_g")[0]), _pcol(f("ln1_b")[0]), _pcol(f("ln2_g")[0]), _pcol(f("ln2_b")[0]),
        _pcol(f("b_down")[0]), _pcol(f("b_up")[0]), _pcol(f("b_ada")[0])], axis=1)
    assert vecs.shape == (128, NV)
    table = f("rel_bias")[0]
    kr = np.arange(128)[:, None]; qc = np.arange(128)[None, :]
    qch = qc // 64
    biasM = np.empty((128, 8, 3, 128), np.float32)
    for i, b in enumerate((0, 3, 4)):
        kch = 2 * b - 8 + kr // 64
        kpos = (2 * b - 8) * 64 + kr
        rel = np.clip(qc - kpos, -128, 128) + 128
        vis = (kch <= qch) & (kch >= qch - 8)
        for h in range(8):
            biasM[:, h, i, :] = np.where(vis, table[h][rel], np.float32(NEG))
    cbias = np.ascontiguousarray(np.broadcast_to(table[:, 256][None, :], (128, 8))).astype(np.float32)
    shared = {
        "vecs": vecs, "biasM": np.ascontiguousarray(biasM.reshape(128, -1)), "cbias": cbias,
        "w_ada": np.ascontiguousarray(f("w_ada")[0]), "w_in": np.ascontiguousarray(f("w_in")[0]),
        "w_rg": np.ascontiguousarray(f("w_rg")[0]), "w_ig": np.ascontiguousarray(f("w_ig")[0]),
        "w_out": np.ascontiguousarray(f("w_out")[0]), "w_up": np.ascontiguousarray(f("w_up")[0]),
        "w_down": np.ascontiguousarray(f("w_down")[0]),
    }
    in_maps = []
    nb = f("x_prompt").shape[0]
    for b in range(nb):
        sc = f("state_conv")[0, b]
        cvec = np.concatenate([_pcol(f("c_prompt")[b]), _pcol(f("c_sample")[b]), _pcol(f("state_lru")[0, b]),
                               _pcol(sc[0]), _pcol(sc[1]), _pcol(sc[2])], axis=1)
        m = dict(shared)
        m.update({
            "xp": np.ascontiguousarray(f("x_prompt")[b, :NT * 512]), "xs": np.ascontiguousarray(f("x_sample")[b]),
            "ck": np.ascontiguousarray(f("cache_k")[0, b].reshape(512, 1024)),
            "cv": np.ascontiguousarray(f("cache_v")[0, b].reshape(512, 1024)),
            "cvec": np.ascontiguousarray(cvec),
        })
        in_maps.append(m)
    return in_maps


def _unp(a, n):
    return np.ascontiguousarray(a.T).reshape(-1)


def run(inputs, NT, cores=None):
    if NT not in _NC_CACHE:
        _NC_CACHE[NT] = build(NT)
    nc = _NC_CACHE[NT]
    in_maps = _host_layout(inputs, NT)
    if cores is not None:
        in_maps = [in_maps[c] for c in cores]
    res = run_bass_kernel_spmd(nc, in_maps, core_ids=list(range(len(in_maps))))
    R = res.results
    B = len(R)
    st = lambda k: np.stack([np.asarray(r[k], np.float32) for r in R])
    yp = st("yp"); ys = st("ys")
    nkp = st("nkp").reshape(1, B, 512, 8, 128); nvp = st("nvp").reshape(1, B, 512, 8, 128)
    nks = st("nks").reshape(1, B, 64, 8, 128); nvs = st("nvs").reshape(1, B, 64, 8, 128)

    def conv(k):
        a = st(k).reshape(B, 128, 8, 3)
        return np.ascontiguousarray(a.transpose(0, 3, 2, 1)).reshape(1, B, 3, 1024)

    def lru(k):
        a = st(k)
        return np.ascontiguousarray(a.transpose(0, 2, 1)).reshape(1, B, 1024)
    return (yp, ys, nkp, nvp, conv("ncp"), lru("nhp"), nks, nvs, conv("ncs"), lru("nhs"))


def kernel(**inputs):
    return run(inputs, 16)
```

```python
import numpy as np
from contextlib import ExitStack
import concourse.bass as bass
import concourse.mybir as mybir
from concourse.bass_utils import run_bass_kernel_spmd

F32 = mybir.dt.float32
BF16 = mybir.dt.bfloat16
AF = mybir.ActivationFunctionType
ALU = mybir.AluOpType

ALPHA = 2.0 ** 0.25
ATT_SCALE = 128.0 ** -0.5
LN_EPS = 1e-5
NEG = -30000.0


class Res:
    __slots__ = ("name", "w", "r", "ro")

    def __init__(self, name, ro=False):
        self.name = name
        self.w = None
        self.r = []
        self.ro = ro


class Slot:
    __slots__ = ("sem", "count")

    def __init__(self, sem):
        self.sem = sem
        self.count = 0


class Prog:
    ENG = ("pe", "act", "dve", "pool", "sp")
    LOOK = 24
    WIN = 150.0
    LAT_TO_PE = 3000.0
    LAT_FROM_PE = 300.0
    LAT_X = 600.0

    def __init__(self, nc, es):
        self.nc = nc
        self.es = es
        self.ops = []
        self.sem = {e: es.enter_context(nc.semaphore("s_" + e)) for e in self.ENG}
        self.nslot = 0
        self.final_slots = []
        self.defn = 512

    def slot(self):
        self.nslot += 1
        return Slot(self.es.enter_context(self.nc.semaphore("d%d" % self.nslot)))

    def cost(self, eng, slot, n):
        if slot is not None:
            return float(n if n is not None else 1 << 20)
        n = self.defn if n is None else n
        if eng == "pe":
            return n / 2.4 + 10.0
        if eng == "act":
            return 230.0 + n / 1.15
        if eng == "dve":
            return 120.0 + n / 0.9
        return 300.0 + n / 0.45

    SERVED = {0: ("exp", "tanh"), 2: ("tanh", "sigmoid"), 3: ("sqrt",), 11: ("gelu", "tanh"), 5: ("ln",), 18: ("silu", "tanh")}
    LOWEST = {"exp": 0, "tanh": 0, "sigmoid": 2, "sqrt": 3, "gelu": 11, "ln": 5, "silu": 18}

    def op(self, eng, fn, reads=(), writes=(), slot=None, n=None, tbl=None, delay=0.0):
        idx = len(self.ops)
        deps = set()
        for r in reads:
            if r.w is not None:
                deps.add(r.w)
        for w in writes:
            if w.w is not None:
                deps.add(w.w)
            deps.update(w.r)
        self.ops.append((eng, fn, slot, deps, self.cost(eng, slot, n), tbl, delay))
        for r in reads:
            if not r.ro:
                r.r.append(idx)
        for w in writes:
            w.w = idx
            w.r = []
        return idx

    def fence(self, frm, to):
        toks = []
        for f in frm:
            if f.w is not None:
                toks.append(f.w)
            toks.extend(f.r)
        for t in to:
            t.r.extend(toks)

    def schedule(self):
        import bisect
        ops = self.ops
        N = len(ops)
        succ = [[] for _ in range(N)]
        indeg = [0] * N
        for i, o in enumerate(ops):
            for d in o[3]:
                succ[d].append(i)
            indeg[i] = len(o[3])
        ready = {e: [] for e in self.ENG}
        rtime = [0.0] * N
        fin = [0.0] * N
        efree = {e: 0.0 for e in self.ENG}
        order = {e: [] for e in self.ENG}
        for i in range(N):
            if indeg[i] == 0:
                ready[ops[i][0]].append(i)
        bw_free = 0.0
        left = N
        LOOK, WIN = self.LOOK, self.WIN
        cur_set = -1
        SERVED, LOWEST = self.SERVED, self.LOWEST
        while left:
            best = None
            for e in self.ENG:
                L = ready[e]
                if not L:
                    continue
                te = efree[e]
                c = None
                cr = None
                cfall = None
                for i in L[:LOOK]:
                    rt = rtime[i]
                    if rt <= te + WIN:
                        if e == "act":
                            tb = ops[i][5]
                            if tb is not None and (cur_set < 0 or tb not in SERVED[cur_set]):
                                if cfall is None:
                                    cfall = i
                                continue
                        c = i
                        break
                    if cr is None or rt < cr:
                        cr = rt
                        c2 = i
                if c is None:
                    c = cfall if cfall is not None else c2
                st = te if rtime[c] < te else rtime[c]
                if best is None or (st, c) < (best[0], best[2]):
                    best = (st, e, c)
            st, e, c = best
            L = ready[e]
            L.pop(bisect.bisect_left(L, c))
            o = ops[c]
            if o[2] is not None:
                issue = 60.0 if e == "sp" else 600.0
                efree[e] = st + issue
                b0 = bw_free if bw_free > st else st
                bw_free = b0 + o[4] / 160.0
                f = max(st + 2000.0, bw_free)
            else:
                dur = o[4]
                if e == "act" and o[5] is not None and (cur_set < 0 or o[5] not in SERVED[cur_set]):
                    cur_set = LOWEST[o[5]]
                    dur += 1283.0
                    self.n_tbl = getattr(self, "n_tbl", 0) + 1
                efree[e] = st + dur
                f = st + dur + (60.0 if e == "pe" else 0.0)
            fin[c] = f
            order[e].append(c)
            left -= 1
            for s_ in succ[c]:
                ce = ops[s_][0]
                if o[2] is not None:
                    lat = f + 200.0
                elif ce == e:
                    lat = f
                elif ce == "pe":
                    lat = f + self.LAT_TO_PE
                elif e == "pe":
                    lat = f + self.LAT_FROM_PE
                else:
                    lat = f + self.LAT_X
                if lat > rtime[s_]:
                    rtime[s_] = lat
                indeg[s_] -= 1
                if indeg[s_] == 0:
                    rtime[s_] += ops[s_][6]
                    bisect.insort(ready[ops[s_][0]], s_)
        self.est_ns = max(fin) if fin else 0.0
        return order

    def emit(self):
        nc = self.nc
        ops = self.ops
        order = self.schedule()
        tok = [None] * len(ops)
        sig = [False] * len(ops)
        for i, o in enumerate(ops):
            for d in o[3]:
                if ops[d][2] is None and not (o[0] == "pe" and ops[d][0] == "pe"):
                    sig[d] = True
        cnt = {e: 0 for e in self.ENG}
        for e in self.ENG:
            for i in order[e]:
                sl = ops[i][2]
                if sl is None:
                    if sig[i]:
                        cnt[e] += 1
                        tok[i] = (self.sem[e], cnt[e], e)
                else:
                    sl.count += 16
                    tok[i] = (sl.sem, sl.count, None)
        self.n_sig = sum(sig)

        def run(eng, name):
            waited = {}
            own = self.sem[name]
            for i in order[name]:
                o = ops[i]
                need = {}
                for d in o[3]:
                    if name == "pe" and ops[d][0] == "pe" and ops[d][2] is None:
                        continue
                    s, v, de = tok[d]
                    if need.get(s, 0) < v:
                        need[s] = v
                for s, v in need.items():
                    if waited.get(s, 0) < v:
                        waited[s] = v
                        eng.wait_ge(s, v)
                ins = o[1](eng)
                t = tok[i]
                if t is not None:
                    ins.then_inc(t[0], 16 if o[2] is not None else 1)
            if name == "sp":
                for sl in self.final_slots:
                    eng.wait_ge(sl.sem, sl.count)

        with nc.Block() as block:
            @block.tensor
            def _(e):
                run(e, "pe")

            @block.scalar
            def _(e):
                run(e, "act")

            @block.vector
            def _(e):
                run(e, "dve")

            @block.gpsimd
            def _(e):
                run(e, "pool")

            @block.sync
            def _(e):
                run(e, "sp")


V_CW, V_CB, V_BRG, V_BIG, V_LAM, V_L1G, V_L1B, V_L2G, V_L2B, V_BD, V_BUP, V_BADA, NV = \
    0, 32, 40, 48, 56, 64, 72, 80, 88, 96, 104, 136, 184
C_C, C_LRU, C_CONV, NCV = 0, 16, 24, 48


def build(NT):
    nc = bass.Bass("TRN2", target_bir_lowering=False)
    SEQ = NT * 512

    def din(name, shape, dt=F32):
        return nc.dram_tensor(name, shape, dt, kind="ExternalInput").ap()

    def dout(name, shape):
        return nc.dram_tensor(name, shape, F32, kind="ExternalOutput").ap()

    xp = din("xp", [SEQ, 1024]); xs = din("xs", [64, 1024])
    ck = din("ck", [512, 1024]); cv = din("cv", [512, 1024])
    vecs_d = din("vecs", [128, NV]); cvec_d = din("cvec", [128, NCV])
    biasM_d = din("biasM", [128, 8 * 3 * 128]); cbias_d = din("cbias", [128, 8])
    w_ada = din("w_ada", [1024, 6144]); w_in = din("w_in", [1024, 7168])
    w_rg = din("w_rg", [8, 128, 128]); w_ig = din("w_ig", [8, 128, 128])
    w_out = din("w_out", [1024, 1024]); w_up = din("w_up", [1024, 4096]); w_down = din("w_down", [4096, 1024])
    yp = dout("yp", [SEQ, 1024]); ys = dout("ys", [64, 1024])
    nkp = dout("nkp", [512, 1024]); nvp = dout("nvp", [512, 1024])
    ncp = dout("ncp", [128, 24]); nhp = dout("nhp", [128, 8])
    nks = dout("nks", [64, 1024]); nvs = dout("nvs", [64, 1024])
    ncs = dout("ncs", [128, 24]); nhs = dout("nhs", [128, 8])
    win_b = nc.dram_tensor("win_b", [14, 128, 8, 512], BF16, kind="Internal").ap()
    wout_b = nc.dram_tensor("wout_b", [2, 128, 8, 512], BF16, kind="Internal").ap()
    wup_b = nc.dram_tensor("wup_b", [8, 128, 8, 512], BF16, kind="Internal").ap()
    wdn_b = nc.dram_tensor("wdn_b", [8, 128, 32, 128], BF16, kind="Internal").ap()

    with ExitStack() as es:
        P = Prog(nc, es)

        def sb(name, shape, dt=F32):
            return nc.alloc_sbuf_tensor(name, shape, dt).ap()

        xst = sb("xst", [128, 2, 1024]); Rxst = [Res("xst0"), Res("xst1")]; Sxst = [P.slot(), P.slot()]
        ost = sb("ost", [128, 2, 1024]); Rost = [Res("ost0"), Res("ost1")]; Sost = [P.slot(), P.slot()]
        resid = sb("resid", [128, 8, 512]); Rres = [Res("res%d" % i) for i in range(8)]
        uT = sb("uT", [128, 8, 512], BF16); RuT = [Res("uT%d" % i) for i in range(8)]
        u2T = sb("u2T", [128, 8, 512], BF16); Ru2 = [Res("u2T%d" % i) for i in range(8)]
        big = sb("big", [128, 16384], BF16)
        hT = big.rearrange("p (m t) -> p m t", t=512); RhT = [Res("hT%d" % i) for i in range(32)]
        kTr = sb("kTr", [128, 8, 1024], BF16); RkT = [Res("kT%d" % i) for i in range(8)]
        Vr = sb("Vr", [128, 8, 8, 129], BF16); RV = [Res("V%d" % i) for i in range(8)]
        biasM = sb("biasM_s", [128, 8, 3, 128]); cbias = sb("cbias_s", [128, 8])
        pT = sb("pT", [128, 2, 5, 128], BF16); RpT = [Res("pT0"), Res("pT1")]
        sbm = sb("sbm", [128, 2, 3, 128]); Rsbm = [Res("sbm0"), Res("sbm1")]
        osb = sb("osb", [128, 2, 128], BF16); Rosb = [Res("osb0"), Res("osb1")]
        rc = sb("rc", [128, 2]); Rrc = [Res("rc0"), Res("rc1")]
        mtmp = sb("mtmp", [128, 2, 128]); Rmt = [Res("mt0"), Res("mt1")]
        xb = sb("xb", [128, 2, 512], BF16); Rxb = [Res("xb0"), Res("xb1")]
        sq = sb("sq", [128, 2, 512], BF16); Rsq = [Res("sq0"), Res("sq1")]
        lnv = sb("lnv", [128, 512]); lnn = sb("lnn", [128, 512])
        lnt = sb("lnt", [128, 2, 512]); Rlnt = [Res("lnt0"), Res("lnt1")]
        Rlnm, Rlnv, Rlnn = Res("lnm"), Res("lnv"), Res("lnn")
        Rln2m = Res("ln2m")
        tg = sb("tg", [128, 2, 4, 512]); Rtg = [[Res("tg%d_%d" % (p_, i)) for i in range(4)] for p_ in range(2)]
        tgs = sb("tgs", [128, 2, 512]); Rtgs = [Res("tgs0"), Res("tgs1")]
        xcb = sb("xcb", [128, 2, 512], BF16); Rxcb = [Res("xcb0"), Res("xcb1")]
        xrb = sb("xrb", [128, 2, 515]); Rxrb = [Res("xrb0"), Res("xrb1")]
        hist = sb("hist", [128, 8, 3]); Rhist = [Res("hist%d" % i) for i in range(8)]
        hst = sb("hst", [128, 8]); Rhst = [Res("hst%d" % i) for i in range(8)]
        NWB = 3
        wbuf = sb("wbuf", [128, NWB, 8, 512], BF16); Rwb = [Res("wb%d" % i) for i in range(NWB)]; Swb = [P.slot() for _ in range(NWB)]; Swbp = [P.slot() for _ in range(NWB)]
        wrg = sb("wrg", [128, 8, 128], BF16); wig = sb("wig", [128, 8, 128], BF16); Rwg = Res("wg"); Swg = P.slot()
        vecs = sb("vecs_s", [128, NV]); cvec = sb("cvec_s", [128, NCV]); Rvec = Res("vec"); Svec = P.slot()
        mod = sb("mod", [128, 48, 2]); Rmod = Res("mod")
        cst = sb("cst", [128, 136]); Rcst = Res("cst")
        csil = sb("csil", [128, 8, 2]); Rcsil = Res("csil")
        identF = sb("identF", [128, 128]); identB = sb("identB", [128, 128], BF16); onesS = sb("onesS", [128, 128], BF16)
        Rid = Res("ident")

        def pb(name, dt=F32, n=512):
            return nc.alloc_psum_tensor(name, [128, n], dt).ap()
        pt = [pb("pt0"), pb("pt1")]; Rpt = [Res("pt0"), Res("pt1")]
        pz = [pb("pz0"), pb("pz1")]; Rpz = [Res("pz0"), Res("pz1")]
        pa = [pb("pa0"), pb("pa1")]; Rpa = [Res("pa0"), Res("pa1")]
        po = pb("po"); Rpo = Res("po")
        pob = pb("pob", BF16, 1024); Rpob = Res("pob")

        Rs_in = [Res("s_in%d" % g) for g in range(14)]; Ss_in = [P.slot() for _ in range(14)]
        Rs_out = [Res("s_out%d" % g) for g in range(2)]; Ss_out = [P.slot() for _ in range(2)]
        Rs_up = [Res("s_up%d" % g) for g in range(8)]; Ss_up = [P.slot() for _ in range(8)]
        Rs_dn = [Res("s_dn%d" % g) for g in range(8)]; Ss_dn = [P.slot() for _ in range(8)]
        Scv = [P.slot() for _ in range(4)]
        Sout = [P.slot() for _ in range(4)]

        P.op("sp", lambda e: e.dma_start(out=vecs, in_=vecs_d), writes=[Rvec], slot=Svec)
        P.op("sp", lambda e: e.dma_start(out=cvec, in_=cvec_d), writes=[Rvec], slot=Svec)
        P.op("sp", lambda e: e.dma_start(out=biasM.rearrange("p a b c -> p (a b c)"), in_=biasM_d), writes=[Rvec], slot=Svec)
        P.op("sp", lambda e: e.dma_start(out=cbias, in_=cbias_d), writes=[Rvec], slot=Svec)
        P.op("act", lambda e: e.activation(out=biasM.rearrange("p a b c -> p (a b c)"), in_=biasM.rearrange("p a b c -> p (a b c)"), func=AF.Exp),
             writes=[Rvec], tbl="exp", n=3072)
        P.op("pool", lambda e: e.dma_start(out=wrg, in_=w_rg.rearrange("n c d -> c n d")), writes=[Rwg], slot=Swg)
        P.op("pool", lambda e: e.dma_start(out=wig, in_=w_ig.rearrange("n c d -> c n d")), writes=[Rwg], slot=Swg)
        P.op("pool", lambda e: e.memset(identF, 0.0), writes=[Rid])
        P.op("pool", lambda e: e.affine_select(out=identF, in_=identF, pattern=[[-1, 128]], compare_op=ALU.not_equal,
                                                fill=1.0, base=0, channel_multiplier=1), writes=[Rid])
        P.op("pool", lambda e: e.tensor_copy(identB, identF), writes=[Rid])
        P.op("pool", lambda e: e.memset(onesS, 1.0 / 1024.0), writes=[Rid])
        P.op("pool", lambda e: e.memset(kTr.rearrange("p a b -> p (a b)"), 0.0), writes=RkT)
        P.op("pool", lambda e: e.memset(Vr.rearrange("p a b c -> p (a b c)"), 0.0), writes=RV)
        P.op("pool", lambda e: e.memset(Vr[:, :, :, 128:129].rearrange("p a b c -> p (a b c)"), 1.0), writes=RV)
        P.op("pool", lambda e: e.memset(pT.rearrange("p a b c -> p (a b c)"), 0.0), writes=RpT)

        P.op("act", lambda e: e.activation(out=csil.rearrange("p k j -> p j k"),
                                           in_=cvec[:, C_C:C_C + 16].rearrange("p (j k) -> p j k", j=2), func=AF.Silu),
             reads=[Rvec], writes=[Rcsil], tbl="silu")
        csil_b = sb("csil_b", [128, 8, 2], BF16)
        P.op("dve", lambda e: e.tensor_copy(csil_b.rearrange("p k j -> p (k j)"), csil.rearrange("p k j -> p (k j)")), reads=[Rcsil], writes=[Rcsil], n=16)
        Smod = [P.slot() for _ in range(4)]
        stg = [(xst[:, 0, :], Rxst[0]), (xst[:, 1, :], Rxst[1]), (ost[:, 0, :], Rost[0]), (ost[:, 1, :], Rost[1])]
        RmodB = Res("modB"); RcstB = Res("cstB")
        bada = vecs[:, V_BADA:V_BADA + 48]
        for c in range(24):
            late = c >= 8
            bi = (2 + (c % 2)) if late else (c % 4)
            buf, Rb = stg[bi]
            pacc, Racc = (po, Rpo) if late else (pa[0], Rpa[0])
            wv = buf.bitcast(BF16).rearrange("p (k c) -> p k c", k=8)
            P.op("pool", lambda e, c=c, wv=wv: e.dma_start(out=wv, in_=w_ada[:, c * 256:(c + 1) * 256].rearrange("(k p) c -> p k c", p=128)),
                 writes=[Rb], slot=Smod[bi], n=1 << 20)
            for jb in range(2):
                n = c * 2 + jb
                for k in range(8):
                    P.op("pe", lambda e, k=k, n=n, jb=jb, wv=wv, pacc=pacc: e.matmul(pacc[:, n * 2:n * 2 + 2], wv[:, k, jb * 128:(jb + 1) * 128], csil_b[:, k, :], start=(k == 0), stop=(k == 7)),
                         reads=[Rb, Rcsil], writes=[Racc], n=200)
            if c == 7:
                for jj in range(2):
                    P.op("dve", lambda e, jj=jj: e.tensor_tensor(mod[:, 0:16, jj], pa[0][:, 0:32].rearrange("p (n j) -> p n j", j=2)[:, :, jj], bada[:, 0:16], ALU.add),
                         reads=[Rvec], writes=[Rpa[0], Rmod], n=16)
        for jj in range(2):
            P.op("dve", lambda e, jj=jj: e.tensor_tensor(mod[:, 16:48, jj], po[:, 32:96].rearrange("p (n j) -> p n j", j=2)[:, :, jj], bada[:, 16:48], ALU.add),
                 reads=[Rvec], writes=[Rpo, RmodB], n=32)
        CL, GA = 0, 8
        def CJ(j, i):
            return 16 + j * 48 + i * 8 - 0
        P.op("act", lambda e: e.activation(out=cst[:, CL:CL + 8], in_=vecs[:, V_LAM:V_LAM + 8], func=AF.Exp, scale=-1.0), reads=[Rvec], writes=[Rcst], tbl="exp")
        P.op("act", lambda e: e.activation(out=cst[:, CL:CL + 8], in_=cst[:, CL:CL + 8], func=AF.Ln, bias=1.0, scale=1.0), writes=[Rcst], tbl="ln")
        P.op("dve", lambda e: e.tensor_scalar(cst[:, CL:CL + 8], cst[:, CL:CL + 8], -8.0, None, ALU.mult), writes=[Rcst])
        P.op("dve", lambda e: e.tensor_scalar(cst[:, GA:GA + 8], vecs[:, V_L1G:V_L1G + 8], ALPHA, None, ALU.mult), reads=[Rvec], writes=[Rcst])
        CL2, HBR, HBI = 112, 120, 128
        P.op("dve", lambda e: e.tensor_scalar(cst[:, CL2:CL2 + 8], cst[:, CL:CL + 8], 0.5, None, ALU.mult), writes=[Rcst])
        P.op("dve", lambda e: e.tensor_scalar(cst[:, HBR:HBR + 8], vecs[:, V_BRG:V_BRG + 8], 0.5, None, ALU.mult), reads=[Rvec], writes=[Rcst])
        P.op("dve", lambda e: e.tensor_scalar(cst[:, HBI:HBI + 8], vecs[:, V_BIG:V_BIG + 8], 0.5, None, ALU.mult), reads=[Rvec], writes=[Rcst])
        for j in range(2):
            def M(blk, j=j):
                return mod[:, blk * 8:(blk + 1) * 8, j]
            P.op("dve", lambda e, j=j, M=M: e.tensor_scalar(cst[:, CJ(j, 0):CJ(j, 0) + 8], M(1), 1.0, 1.0 / ALPHA, ALU.add, ALU.mult), reads=[Rmod], writes=[Rcst])
            P.op("dve", lambda e, j=j, M=M: e.tensor_scalar(cst[:, CJ(j, 1):CJ(j, 1) + 8], M(2), 1.0, 0.5, ALU.add, ALU.mult), reads=[RmodB], writes=[RcstB])
            P.op("dve", lambda e, j=j, M=M: e.tensor_scalar(cst[:, CJ(j, 4):CJ(j, 4) + 8], M(5), 1.0, None, ALU.add), reads=[RmodB], writes=[RcstB])
            P.op("dve", lambda e, j=j, M=M: e.tensor_scalar(cst[:, CJ(j, 3):CJ(j, 3) + 8], M(4), 1.0, None, ALU.add), reads=[RmodB], writes=[RcstB])
            P.op("dve", lambda e, j=j: e.tensor_tensor(cst[:, CJ(j, 2):CJ(j, 2) + 8], vecs[:, V_L1G:V_L1G + 8], cst[:, CJ(j, 3):CJ(j, 3) + 8], ALU.mult), reads=[Rvec], writes=[RcstB])
            P.op("dve", lambda e, j=j: e.tensor_tensor(cst[:, CJ(j, 3):CJ(j, 3) + 8], vecs[:, V_L1B:V_L1B + 8], cst[:, CJ(j, 3):CJ(j, 3) + 8], ALU.mult), reads=[Rvec], writes=[RcstB])
            P.op("dve", lambda e, j=j, M=M: e.tensor_tensor(cst[:, CJ(j, 3):CJ(j, 3) + 8], cst[:, CJ(j, 3):CJ(j, 3) + 8], M(3), ALU.add), reads=[RmodB], writes=[RcstB])
            P.op("dve", lambda e, j=j: e.tensor_tensor(cst[:, CJ(j, 5):CJ(j, 5) + 8], vecs[:, V_BD:V_BD + 8], cst[:, CJ(j, 4):CJ(j, 4) + 8], ALU.mult), reads=[Rvec], writes=[RcstB])
            P.op("dve", lambda e, j=j: e.scalar_tensor_tensor(cst[:, CJ(j, 5):CJ(j, 5) + 8], vecs[:, V_L1B:V_L1B + 8], ALPHA, cst[:, CJ(j, 5):CJ(j, 5) + 8], ALU.mult, ALU.add), reads=[Rvec], writes=[RcstB])

        for r_ in [Rvec, Rcst, Rmod, RmodB, RcstB, Rid, Rwg, Rcsil]:
            r_.ro = True
        wq = [0]

        converted = set()
        f32src = {}
        for g in range(14):
            f32src[id(Rs_in[g])] = (w_in[:, g * 512:(g + 1) * 512].rearrange("(k p) c -> p k c", p=128), Ss_in[g])
        for g in range(2):
            f32src[id(Rs_out[g])] = (w_out[:, g * 512:(g + 1) * 512].rearrange("(k p) c -> p k c", p=128), Ss_out[g])
        for g in range(8):
            f32src[id(Rs_up[g])] = (w_up[:, g * 512:(g + 1) * 512].rearrange("(k p) c -> p k c", p=128), Ss_up[g])

        def stream(src, Rsrc):
            i = wq[0] % NWB
            wq[0] += 1
            if id(Rsrc) not in converted:
                converted.add(id(Rsrc))
                fsrc, Ssrc = f32src[id(Rsrc)]
                P.op("pool", lambda e: e.dma_start(out=wbuf[:, i], in_=fsrc), writes=[Rwb[i]], slot=Swbp[i], n=2 << 20)
                P.op("sp", lambda e: e.dma_start(out=src, in_=wbuf[:, i]), reads=[Rwb[i]], writes=[Rsrc], slot=Ssrc)
            else:
                P.op("sp", lambda e: e.dma_start(out=wbuf[:, i], in_=src), reads=[Rsrc], writes=[Rwb[i]], slot=Swb[i])
            return i

        zq = [0]

        pzz = [pz[0], pz[1], pt[0], pt[1]]
        Rpzz = [Rpz[0], Rpz[1], Rpt[0], Rpt[1]]

        def nextz():
            i = zq[0] % 4
            zq[0] += 1
            return i

        Gv = big[:, 0:8192].bitcast(F32).rearrange("p (c t) -> p c t", t=512)
        sgb = big[:, 8192:12288].rearrange("p (c t) -> p c t", t=512)
        qT = big[:, 12288:16384].rearrange("p (c t) -> p c t", t=512)
        RG = [Res("G%d" % i) for i in range(8)]; Rsgb = [Res("sgb%d" % i) for i in range(8)]; RqT = [Res("qT%d" % i) for i in range(8)]

        def col(base, n):
            return cst[:, base + n:base + n + 1]

        def vcol(base, n):
            return vecs[:, base + n:base + n + 1]

        def tile(t, T, j):
            P.defn = T
            sample = (j == 1)
            xsrc = xs if sample else xp[t * 512:(t + 1) * 512, :]
            ydst = ys if sample else yp[t * 512:(t + 1) * 512, :]
            NS = max(1, T // 128)
            RW = min(T, 128)
            last = sample or (t == NT - 1)
            P.fence(RhT, RG + Rsgb + RqT)

            if sample:
                for s in range(4):
                    b_ = s % 2
                    P.op("sp", lambda e, s=s, b_=b_: e.dma_start(out=xst[:, b_, :], in_=ck[s * 128:(s + 1) * 128, :]), writes=[Rxst[b_]], slot=Sxst[b_])
                    for half in range(2):
                        pi_ = half
                        for f4 in range(4):
                            fc = half * 4 + f4
                            P.op("pe", lambda e, b_=b_, fc=fc, f4=f4, pi_=pi_: e.transpose(pt[pi_][:, f4 * 128:(f4 + 1) * 128], xst[:, b_, fc * 128:(fc + 1) * 128], identF),
                                 reads=[Rxst[b_], Rid], writes=[Rpt[pi_]])
                        P.op("act", lambda e, half=half, s=s, pi_=pi_: e.activation(out=kTr[:, half * 4:(half + 1) * 4, s * 128:(s + 1) * 128],
                                                                                   in_=pt[pi_].rearrange("p (f t) -> p f t", t=128), func=AF.Copy),
                             writes=[Rpt[pi_]] + RkT[half * 4:(half + 1) * 4])
                    P.op("pool", lambda e, s=s: e.dma_start(out=Vr[:, s, :, 0:128], in_=cv[s * 128:(s + 1) * 128, :].rearrange("p (h d) -> p h d", d=128)),
                         writes=[RV[s]], slot=Scv[s], n=1 << 19)
                P.op("pool", lambda e: e.tensor_copy(hist.rearrange("p c j -> p j c"), cvec[:, C_CONV:C_CONV + 24].rearrange("p (j c) -> p j c", j=3)),
                     reads=[Rvec], writes=Rhist)
                P.op("pool", lambda e: e.tensor_copy(hst, cvec[:, C_LRU:C_LRU + 8]), reads=[Rvec], writes=Rhst)
            elif t == 0:
                P.op("pool", lambda e: e.memset(hist.rearrange("p c j -> p (c j)"), 0.0), writes=Rhist)
                P.op("pool", lambda e: e.memset(hst, 0.0), writes=Rhst)

            for s in range(NS):
                b_ = s % 2
                P.op("sp", lambda e, s=s, b_=b_: e.dma_start(out=xst[0:RW, b_, :], in_=xsrc[s * 128:s * 128 + RW, :]), writes=[Rxst[b_]], slot=Sxst[b_])
                for half in range(2):
                    pi_ = half
                    for f4 in range(4):
                        fc = half * 4 + f4
                        P.op("pe", lambda e, b_=b_, fc=fc, f4=f4, pi_=pi_: e.transpose(pt[pi_][:, f4 * 128:f4 * 128 + RW], xst[0:RW, b_, fc * 128:(fc + 1) * 128], identF[0:RW, 0:RW]),
                             reads=[Rxst[b_], Rid], writes=[Rpt[pi_]], n=256)
                    P.op("act", lambda e, half=half, s=s, pi_=pi_: e.activation(out=resid[:, half * 4:(half + 1) * 4, s * 128:s * 128 + RW],
                                                                               in_=pt[pi_].rearrange("p (f t) -> p f t", t=128)[:, :, 0:RW], func=AF.Copy, scale=ALPHA),
                         writes=[Rpt[pi_]] + Rres[half * 4:(half + 1) * 4])
            for fc in range(8):
                P.op("dve", lambda e, fc=fc: e.tensor_scalar(uT[:, fc, 0:T], resid[:, fc, 0:T], col(CJ(j, 0), fc), mod[:, fc, j:j + 1], ALU.mult, ALU.add),
                     reads=[Rres[fc], Rcst, Rmod], writes=[RuT[fc]])

            def proj_block(wi, jb, rhs_of_k, Rrhs):
                z = nextz()
                for k in range(8):
                    P.op("pe", lambda e, k=k, z=z: e.matmul(pzz[z][:, 0:T], wbuf[:, wi, k, jb * 128:(jb + 1) * 128], rhs_of_k(k), start=(k == 0), stop=(k == 7)),
                         reads=[Rwb[wi], Rrhs[k]], writes=[Rpzz[z]])
                return z

            uk = lambda k: uT[:, k, 0:T]
            XC, RA, IB, AM = 0, 1, 2, 3

            def chainA(fc, wi, jb):
                xb_ = fc % 2
                tp_ = fc % 2
                Rt = Rtg[tp_]
                tgp = tg[:, tp_]
                z = proj_block(wi, jb, uk, RuT)
                P.op("pool", lambda e: e.tensor_copy(xrb[:, xb_, 0:3], hist[:, fc, :]), reads=[Rhist[fc]], writes=[Rxrb[xb_]], n=8)
                P.op("act", lambda e: e.activation(out=xrb[:, xb_, 3:3 + T], in_=pzz[z][:, 0:T], func=AF.Copy), writes=[Rpzz[z], Rxrb[xb_]])
                P.op("pool", lambda e: e.tensor_copy(hist[:, fc, :], xrb[:, xb_, T:T + 3]), reads=[Rxrb[xb_]], writes=[Rhist[fc]], n=8)
                P.op("dve", lambda e: e.tensor_scalar(tgp[:, XC, 0:T], xrb[:, xb_, 0:T], vcol(V_CW, fc), vcol(V_CB, fc), ALU.mult, ALU.add),
                     reads=[Rxrb[xb_], Rvec], writes=[Rt[XC]])
                for jj in (1, 2, 3):
                    P.op("dve", lambda e, jj=jj: e.scalar_tensor_tensor(tgp[:, XC, 0:T], xrb[:, xb_, jj:jj + T], vcol(V_CW + 8 * jj, fc), tgp[:, XC, 0:T], ALU.mult, ALU.add),
                         reads=[Rxrb[xb_], Rvec], writes=[Rt[XC]])
                P.op("act", lambda e: e.activation(out=xcb[:, tp_, 0:T], in_=tgp[:, XC, 0:T], func=AF.Copy), reads=[Rt[XC]], writes=[Rxcb[tp_]])
                P.op("pe", lambda e: e.matmul(pa[0][:, 0:T], wrg[:, fc, :], xcb[:, tp_, 0:T], start=True, stop=True), reads=[Rwg, Rxcb[tp_]], writes=[Rpa[0]], delay=5000.0)
                P.op("pe", lambda e: e.matmul(pa[1][:, 0:T], wig[:, fc, :], xcb[:, tp_, 0:T], start=True, stop=True), reads=[Rwg, Rxcb[tp_]], writes=[Rpa[1]])
                P.op("act", lambda e: e.activation(out=tgp[:, RA, 0:T], in_=pa[0][:, 0:T], func=AF.Tanh, bias=col(HBR, fc), scale=0.5), reads=[Rcst], writes=[Rpa[0], Rt[RA]], tbl="tanh")
                P.op("act", lambda e: e.activation(out=tgp[:, IB, 0:T], in_=pa[1][:, 0:T], func=AF.Tanh, bias=col(HBI, fc), scale=0.5), reads=[Rcst], writes=[Rpa[1], Rt[IB]], tbl="tanh")
                P.op("act", lambda e: e.activation(out=tgp[:, RA, 0:T], in_=tgp[:, RA, 0:T], func=AF.Exp, scale=col(CL2, fc), bias=col(CL2, fc)), reads=[Rcst], writes=[Rt[RA]], tbl="exp")
                P.op("pool", lambda e: e.tensor_tensor(tgp[:, AM, 0:T], tgp[:, RA, 0:T], tgp[:, RA, 0:T], ALU.mult), reads=[Rt[RA]], writes=[Rt[AM]])
                P.op("dve", lambda e: e.scalar_tensor_tensor(tgp[:, IB, 0:T], tgp[:, IB, 0:T], 1.0, tgp[:, XC, 0:T], ALU.add, ALU.mult), reads=[Rt[XC]], writes=[Rt[IB]])

            def chainB(fc):
                tp_ = fc % 2
                Rt = Rtg[tp_]
                tgp = tg[:, tp_]
                if (not sample) and t == 0:
                    P.op("pool", lambda e: e.memset(tgp[:, AM, 0:1], 0.5), writes=[Rt[AM]], n=1)
                    P.op("pool", lambda e: e.memset(tgp[:, RA, 0:1], 0.0), writes=[Rt[RA]], n=1)
                P.op("pool", lambda e: e.tensor_tensor(tgp[:, IB, 0:T], tgp[:, IB, 0:T], tgp[:, AM, 0:T], ALU.mult), reads=[Rt[AM]], writes=[Rt[IB]])
                P.op("dve", lambda e: e.tensor_tensor_scan(Gv[:, fc, 0:T], tgp[:, RA, 0:T], tgp[:, IB, 0:T], hst[:, fc:fc + 1], ALU.mult, ALU.add),
                     reads=[Rt[RA], Rt[IB], Rhst[fc]], writes=[RG[fc]], n=2 * T)
                P.op("pool", lambda e: e.tensor_copy(hst[:, fc:fc + 1], Gv[:, fc, T - 1:T]), reads=[RG[fc]], writes=[Rhst[fc]], n=1)

            def emit_xr(g, jp):
                wi = stream(win_b[g], Rs_in[g])
                if True:
                    fcs = (g * 4 + 2 * jp, g * 4 + 2 * jp + 1)
                    for fc in fcs:
                        chainA(fc, wi, fc % 4)
                    P.op("act", lambda e: e.activation(out=tg[:, :, AM, 0:T], in_=tg[:, :, AM, 0:T], func=AF.Sqrt, bias=0.25, scale=-0.25),
                         writes=[Rtg[0][AM], Rtg[1][AM]], tbl="sqrt", n=2 * T)
                    for fc in fcs:
                        chainB(fc)
            def emit_gl(g):
                wi = stream(win_b[g], Rs_in[g])
                for jb in range(4):
                    fc = (g - 2) * 4 + jb
                    z = proj_block(wi, jb, uk, RuT)
                    sp_ = fc % 2
                    P.op("act", lambda e, z=z, sp_=sp_: e.activation(out=tgs[:, sp_, 0:T], in_=pzz[z][:, 0:T], func=AF.Gelu_apprx_tanh), writes=[Rpzz[z], Rtgs[sp_]], tbl="gelu")
                    P.op("pool", lambda e, fc=fc, sp_=sp_: e.tensor_tensor(Gv[:, fc, 0:T], Gv[:, fc, 0:T], tgs[:, sp_, 0:T], ALU.mult), reads=[Rtgs[sp_]], writes=[RG[fc]])
            def emit_ga(g):
                wi = stream(win_b[g], Rs_in[g])
                for jb in range(4):
                    fc = (g - 10) * 4 + jb
                    z = proj_block(wi, jb, uk, RuT)
                    sp_ = fc % 2
                    P.op("act", lambda e, z=z, sp_=sp_: e.activation(out=tgs[:, sp_, 0:T], in_=pzz[z][:, 0:T], func=AF.Tanh, scale=0.5), writes=[Rpzz[z], Rtgs[sp_]], tbl="tanh")
                    P.op("dve", lambda e, fc=fc, sp_=sp_: e.scalar_tensor_tensor(Gv[:, fc, 0:T], tgs[:, sp_, 0:T], 1.0, Gv[:, fc, 0:T], ALU.add, ALU.mult), reads=[Rtgs[sp_]], writes=[RG[fc]])
            def emit_gb(g):
                wi = stream(win_b[g], Rs_in[g])
                for jb in range(4):
                    fc = (g - 12) * 4 + jb
                    z = proj_block(wi, jb, uk, RuT)
                    P.op("act", lambda e, z=z, fc=fc: e.activation(out=sgb[:, fc, 0:T], in_=pzz[z][:, 0:T], func=AF.Tanh, scale=0.5), writes=[Rpzz[z], Rsgb[fc]], tbl="tanh")
            ro = 512 if sample else (t % 2) * 512
            def emit_q(g):
                wi = stream(win_b[g], Rs_in[g])
                for jb in range(4):
                    h = (g - 4) * 4 + jb
                    z = proj_block(wi, jb, uk, RuT)
                    P.op("dve", lambda e, z=z, h=h: e.tensor_scalar(qT[:, h, 0:T], pzz[z][:, 0:T], ATT_SCALE, None, ALU.mult), writes=[Rpzz[z], RqT[h]])
            def emit_k(g):
                wi = stream(win_b[g], Rs_in[g])
                for jb in range(4):
                    h = (g - 6) * 4 + jb
                    z = proj_block(wi, jb, uk, RuT)
                    P.op("dve", lambda e, z=z, h=h: e.tensor_copy(kTr[:, h, ro:ro + T], pzz[z][:, 0:T]), writes=[Rpzz[z], RkT[h]])
                if last:
                    for s in range(NS):
                        z = nextz()
                        for k in range(8):
                            P.op("pe", lambda e, k=k, z=z, s=s, wi=wi: e.matmul(pzz[z][0:RW, :], uT[:, k, s * 128:s * 128 + RW], wbuf[:, wi, k, :], start=(k == 0), stop=(k == 7)),
                                 reads=[Rwb[wi]] + RuT, writes=[Rpzz[z]])
                        ob = (s + g) % 2
                        P.op("dve", lambda e, z=z, ob=ob: e.tensor_copy(ost[0:RW, ob, 0:512], pzz[z][0:RW, :]), writes=[Rpzz[z], Rost[ob]])
                        dst = (nks if sample else nkp)[s * 128:s * 128 + RW, (g - 6) * 512:(g - 5) * 512]
                        P.op("sp", lambda e, ob=ob, dst=dst: e.dma_start(out=dst, in_=ost[0:RW, ob, 0:512]), reads=[Rost[ob]], slot=Sost[ob], n=1 << 18)
            def emit_v(g):
                wi = stream(win_b[g], Rs_in[g])
                for s in range(NS):
                    z = nextz()
                    for k in range(8):
                        P.op("pe", lambda e, k=k, z=z, s=s, wi=wi: e.matmul(pzz[z][0:RW, :], uT[:, k, s * 128:s * 128 + RW], wbuf[:, wi, k, :], start=(k == 0), stop=(k == 7)),
                             reads=[Rwb[wi]] + RuT, writes=[Rpzz[z]])
                    rb = 4 if sample else (t % 2) * 4 + s
                    P.op("dve", lambda e, z=z, rb=rb, g=g: e.tensor_copy(Vr[0:RW, rb, (g - 8) * 4:(g - 7) * 4, 0:128],
                                                                       pzz[z][0:RW, :].rearrange("p (h d) -> p h d", d=128)),
                         writes=[Rpzz[z], RV[rb]])
                    if last:
                        ob = (s + g) % 2
                        P.op("dve", lambda e, z=z, ob=ob: e.tensor_copy(ost[0:RW, ob, 0:512], pzz[z][0:RW, :]), writes=[Rpzz[z], Rost[ob]])
                        dst = (nvs if sample else nvp)[s * 128:s * 128 + RW, (g - 8) * 512:(g - 7) * 512]
                        P.op("sp", lambda e, ob=ob, dst=dst: e.dma_start(out=dst, in_=ost[0:RW, ob, 0:512]), reads=[Rost[ob]], slot=Sost[ob], n=1 << 18)
            emit_xr(0, 0); emit_q(4); emit_q(5)
            emit_xr(0, 1); emit_k(6); emit_k(7)
            emit_xr(1, 0); emit_v(8); emit_v(9)
            emit_xr(1, 1); emit_gb(12); emit_gb(13)
            emit_gl(2); emit_gl(3); emit_ga(10); emit_ga(11)
            if last:
                P.op("sp", lambda e: e.dma_start(out=(ncs if sample else ncp), in_=hist.rearrange("p c j -> p (c j)")), reads=Rhist, slot=Sout[0 if sample else 2], n=4096)
                P.op("sp", lambda e: e.dma_start(out=(nhs if sample else nhp), in_=hst), reads=Rhst, slot=Sout[1 if sample else 3], n=4096)

            NP = 1 if sample else 4
            QW = 64 if sample else 128
            it = 0
            Sset = [(pa[0], pa[1], Rpa[0], Rpa[1]), (pt[0], pt[1], Rpt[0], Rpt[1])]
            Oset = [(po, Rpo), (pz[1], Rpz[1])]
            for pi in range(NP):
                for h in range(8):
                    bb = it % 2
                    it += 1
                    sM, sC, RsM, RsC = Sset[bb]
                    oB, RoB = Oset[bb]
                    blocks = []
                    for b in range(5):
                        if sample:
                            blocks.append((b, b))
                        else:
                            cs = 8 * t + 2 * pi - 8 + 2 * b
                            if cs >= 0:
                                blocks.append((b, (cs // 2) % 8))
                    mpos = {0: 0, 3: 1, 4: 2}
                    cpos = {1: 0, 2: 1}
                    q_ap = qT[:, h, pi * 128:pi * 128 + QW]
                    for b, rb in blocks:
                        if b in mpos:
                            o_ap = sM[:, mpos[b] * 128:mpos[b] * 128 + QW]; R_ = RsM
                        else:
                            o_ap = sC[:, cpos[b] * 128:cpos[b] * 128 + QW]; R_ = RsC
                        P.op("pe", lambda e, o_ap=o_ap, h=h, rb=rb, q_ap=q_ap: e.matmul(o_ap, kTr[:, h, rb * 128:(rb + 1) * 128], q_ap, start=True, stop=True),
                             reads=[RkT[h], RqT[h]], writes=[R_], n=QW + 64)
                    mb = [b for b, _ in blocks if b in mpos]
                    cb = [b for b, _ in blocks if b in cpos]
                    m0 = mpos[mb[0]]
                    nm = len(mb)
                    P.op("act", lambda e, bb=bb, m0=m0, nm=nm, sM=sM: e.activation(
                        out=sbm[:, bb, m0:m0 + nm, 0:QW], in_=sM.rearrange("p (b q) -> p b q", q=128)[:, m0:m0 + nm, 0:QW], func=AF.Exp),
                        writes=[RsM, Rsbm[bb]], n=nm * QW, tbl="exp")
                    P.op("pool", lambda e, bb=bb, h=h, m0=m0, nm=nm: e.tensor_tensor(
                        pT[:, bb, m0:m0 + nm, 0:QW], sbm[:, bb, m0:m0 + nm, 0:QW], biasM[:, h, m0:m0 + nm, 0:QW], ALU.mult),
                        reads=[Rvec, Rsbm[bb]], writes=[RpT[bb]], n=nm * QW)
                    if cb:
                        c0_ = cpos[cb[0]]
                        ncb = len(cb)
                        P.op("act", lambda e, bb=bb, h=h, c0_=c0_, ncb=ncb, sC=sC: e.activation(
                            out=pT[:, bb, 3 + c0_:3 + c0_ + ncb, 0:QW], in_=sC.rearrange("p (b q) -> p b q", q=128)[:, c0_:c0_ + ncb, 0:QW],
                            func=AF.Exp, bias=cbias[:, h:h + 1], scale=1.0),
                            reads=[Rvec], writes=[RsC, RpT[bb]], n=ncb * QW, tbl="exp")
                    for i_, (b, rb) in enumerate(blocks):
                        pidx = mpos[b] if b in mpos else 3 + cpos[b]
                        P.op("pe", lambda e, bb=bb, pidx=pidx, rb=rb, h=h, i_=i_, nb=len(blocks), oB=oB: e.matmul(
                            oB[0:QW, 0:129], pT[:, bb, pidx, 0:QW], Vr[:, rb, h, :], start=(i_ == 0), stop=(i_ == nb - 1)),
                            reads=[RpT[bb], RV[rb]], writes=[RoB], n=129 + 64)
                    P.op("dve", lambda e, bb=bb, oB=oB: e.reciprocal(rc[0:QW, bb:bb + 1], oB[0:QW, 128:129]), writes=[RoB, Rrc[bb]], n=1)
                    P.op("dve", lambda e, bb=bb, oB=oB: e.tensor_scalar(osb[0:QW, bb, :], oB[0:QW, 0:128], rc[0:QW, bb:bb + 1], None, ALU.mult),
                         reads=[Rrc[bb]], writes=[RoB, Rosb[bb]], n=128)
                    P.op("pe", lambda e, bb=bb: e.transpose(pob[:, bb * 128:bb * 128 + QW], osb[0:QW, bb, :], identB[0:QW, 0:QW]), reads=[Rosb[bb], Rid], writes=[Rpob], n=128)
                    P.op("dve", lambda e, bb=bb, h=h, pi=pi: e.scalar_tensor_tensor(mtmp[:, bb, 0:QW], sgb[:, h, pi * 128:pi * 128 + QW], 1.0, pob[:, bb * 128:bb * 128 + QW], ALU.add, ALU.mult),
                         reads=[Rsgb[h]], writes=[Rpob, Rmt[bb]], n=QW)
                    P.op("dve", lambda e, bb=bb, h=h, pi=pi: e.tensor_tensor(uT[:, h, pi * 128:pi * 128 + QW], Gv[:, h, pi * 128:pi * 128 + QW], mtmp[:, bb, 0:QW], ALU.add),
                         reads=[Rmt[bb], RG[h]], writes=[RuT[h]], n=QW)

            def layernorm(emit_out, inplace=False):
                for n in range(8):
                    b2 = n % 2
                    P.op("dve", lambda e, n=n, b2=b2: e.tensor_copy(xb[:, b2, 0:T], resid[:, n, 0:T]), reads=[Rres[n]], writes=[Rxb[b2]])
                    P.op("act", lambda e, n=n, b2=b2: e.activation(out=sq[:, b2, 0:T], in_=resid[:, n, 0:T], func=AF.Square), reads=[Rres[n]], writes=[Rsq[b2]])
                    P.op("pe", lambda e, n=n, b2=b2: e.matmul(pa[0][:, 0:T], onesS, xb[:, b2, 0:T], start=(n == 0), stop=(n == 7)), reads=[Rid, Rxb[b2]], writes=[Rpa[0]])
                    P.op("pe", lambda e, n=n, b2=b2: e.matmul(pa[1][:, 0:T], onesS, sq[:, b2, 0:T], start=(n == 0), stop=(n == 7)), reads=[Rid, Rsq[b2]], writes=[Rpa[1]])
                P.op("act", lambda e: e.activation(out=lnv[:, 0:T], in_=pa[0][:, 0:T], func=AF.Square), writes=[Rpa[0], Rlnv])
                P.op("dve", lambda e: e.tensor_tensor(lnv[:, 0:T], pa[1][:, 0:T], lnv[:, 0:T], ALU.subtract), writes=[Rpa[1], Rlnv])
                P.op("act", lambda e: e.activation(out=lnv[:, 0:T], in_=lnv[:, 0:T], func=AF.Sqrt, bias=LN_EPS, scale=1.0), writes=[Rlnv], tbl="sqrt")
                P.op("dve", lambda e: e.reciprocal(lnv[:, 0:T], lnv[:, 0:T]), writes=[Rlnv])
                P.op("dve", lambda e: e.scalar_tensor_tensor(lnn[:, 0:T], pa[0][:, 0:T], -1.0, lnv[:, 0:T], ALU.mult, ALU.mult), reads=[Rlnv], writes=[Rpa[0], Rlnn])
                if inplace:
                    for n in range(8):
                        P.op("dve", lambda e, n=n: e.tensor_tensor(Gv[:, n, 0:T], resid[:, n, 0:T], lnv[:, 0:T], ALU.mult), reads=[Rres[n], Rlnv], writes=[RG[n], Rln2m])
                    for n in range(8):
                        P.op("pool", lambda e, n=n: e.tensor_tensor(Gv[:, n, 0:T], Gv[:, n, 0:T], lnn[:, 0:T], ALU.add), reads=[Rlnn, Rln2m], writes=[RG[n]])
                        emit_out(n, None)
                    return
                for n in range(8):
                    b2 = n % 2
                    P.op("dve", lambda e, n=n, b2=b2: e.tensor_tensor(lnt[:, b2, 0:T], resid[:, n, 0:T], lnv[:, 0:T], ALU.mult), reads=[Rres[n], Rlnv], writes=[Rlnt[b2]])
                    P.op("dve", lambda e, b2=b2: e.tensor_tensor(lnt[:, b2, 0:T], lnt[:, b2, 0:T], lnn[:, 0:T], ALU.add), reads=[Rlnn], writes=[Rlnt[b2]])
                    emit_out(n, b2)

            mk = lambda k: uT[:, k, 0:T]
            for g in range(2):
                wi = stream(wout_b[g], Rs_out[g])
                for jb in range(4):
                    n = g * 4 + jb
                    z = proj_block(wi, jb, mk, RuT)
                    P.op("dve", lambda e, z=z, n=n: e.scalar_tensor_tensor(resid[:, n, 0:T], pzz[z][:, 0:T], col(CJ(j, 1), n), resid[:, n, 0:T], ALU.mult, ALU.add),
                         reads=[RcstB], writes=[Rpzz[z], Rres[n]])

            def ln1_out(n, b2):
                P.op("act", lambda e, n=n, b2=b2: e.activation(out=u2T[:, n, 0:T], in_=lnt[:, b2, 0:T], func=AF.Identity, scale=col(CJ(j, 2), n), bias=col(CJ(j, 3), n)),
                     reads=[Rlnt[b2], RcstB], writes=[Ru2[n]])
                P.op("act", lambda e, n=n, b2=b2: e.activation(out=resid[:, n, 0:T], in_=lnt[:, b2, 0:T], func=AF.Identity, scale=col(GA, n), bias=col(CJ(j, 5), n)),
                     reads=[Rlnt[b2], Rcst, RcstB], writes=[Rres[n]])
            layernorm(ln1_out)

            P.fence(RG + Rsgb + RqT, RhT)
            u2k = lambda k: u2T[:, k, 0:T]
            for g in range(8):
                wi = stream(wup_b[g], Rs_up[g])
                if g == 0:
                    zs0 = [nextz() for _ in range(4)]
                    for k in range(8):
                        for jb in range(4):
                            P.op("pe", lambda e, k=k, jb=jb, zz=zs0[jb], wi=wi: e.matmul(pzz[zz][:, 0:T], wbuf[:, wi, k, jb * 128:(jb + 1) * 128], u2T[:, k, 0:T], start=(k == 0), stop=(k == 7)),
                                 reads=[Rwb[wi], Ru2[k]], writes=[Rpzz[zs0[jb]]])
                for jb in range(4):
                    m = g * 4 + jb
                    z = zs0[jb] if g == 0 else proj_block(wi, jb, u2k, Ru2)
                    tb = m % 2
                    P.op("act", lambda e, z=z, m=m, tb=tb: e.activation(out=tgs[:, tb, 0:T], in_=pzz[z][:, 0:T], func=AF.Relu, bias=vcol(V_BUP, m), scale=1.0),
                         reads=[Rvec], writes=[Rpzz[z], Rtgs[tb]])
                    eng = "pool" if m % 2 == 0 else "dve"
                    P.op(eng, lambda e, m=m, tb=tb: e.tensor_tensor(hT[:, m, 0:T], tgs[:, tb, 0:T], tgs[:, tb, 0:T], ALU.mult), reads=[Rtgs[tb]], writes=[RhT[m]])
            for n in range(8):
                i = wq[0] % NWB
                wq[0] += 1
                wd = wbuf[:, i].rearrange("p k c -> p (k c)").rearrange("p (k c) -> p k c", c=128)
                if id(Rs_dn[n]) not in converted:
                    converted.add(id(Rs_dn[n]))
                    for kh in range(2):
                        P.op("pool", lambda e, n=n, wd=wd, kh=kh: e.dma_start(
                            out=wd[:, kh * 16:(kh + 1) * 16, :],
                            in_=w_down[kh * 2048:(kh + 1) * 2048, n * 128:(n + 1) * 128].rearrange("(k p) c -> p k c", p=128)),
                            writes=[Rwb[i]], slot=Swbp[i], n=1 << 20)
                    P.op("sp", lambda e, n=n, wd=wd: e.dma_start(out=wdn_b[n], in_=wd), reads=[Rwb[i]], writes=[Rs_dn[n]], slot=Ss_dn[n])
                else:
                    P.op("sp", lambda e, i=i, n=n: e.dma_start(out=wbuf[:, i].rearrange("p k c -> p (k c)"), in_=wdn_b[n].rearrange("p k c -> p (k c)")),
                         reads=[Rs_dn[n]], writes=[Rwb[i]], slot=Swb[i])
                z = nextz()
                for k in range(32):
                    P.op("pe", lambda e, k=k, z=z, wd=wd: e.matmul(pzz[z][:, 0:T], wd[:, k, :], hT[:, k, 0:T], start=(k == 0), stop=(k == 31)),
                         reads=[Rwb[i], RhT[k]], writes=[Rpzz[z]])
                P.op("dve", lambda e, z=z, n=n: e.scalar_tensor_tensor(resid[:, n, 0:T], pzz[z][:, 0:T], col(CJ(j, 4), n), resid[:, n, 0:T], ALU.mult, ALU.add),
                     reads=[RcstB], writes=[Rpzz[z], Rres[n]])

            def ln2_out(n, b2):
                P.op("act", lambda e, n=n: e.activation(out=Gv[:, n, 0:T], in_=Gv[:, n, 0:T], func=AF.Identity, scale=vcol(V_L2G, n), bias=vcol(V_L2B, n)),
                     reads=[Rvec], writes=[RG[n]])
            P.fence(RhT, RG)
            layernorm(ln2_out, inplace=True)

            for s in range(NS):
                ob = s % 2
                for half in range(2):
                    pi_ = half
                    for f4 in range(4):
                        fc = half * 4 + f4
                        P.op("pe", lambda e, s=s, fc=fc, f4=f4, pi_=pi_: e.transpose(pa[pi_][0:RW, f4 * 128:(f4 + 1) * 128], Gv[:, fc, s * 128:s * 128 + RW], identF),
                             reads=[RG[fc], Rid], writes=[Rpa[pi_]], n=256)
                    eng = "act" if half == 0 else "dve"
                    if eng == "act":
                        P.op("act", lambda e, ob=ob, half=half, pi_=pi_: e.activation(out=ost[0:RW, ob, half * 512:(half + 1) * 512], in_=pa[pi_][0:RW, :], func=AF.Copy),
                             writes=[Rpa[pi_], Rost[ob]])
                    else:
                        P.op("dve", lambda e, ob=ob, half=half, pi_=pi_: e.tensor_copy(ost[0:RW, ob, half * 512:(half + 1) * 512], pa[pi_][0:RW, :]),
                             writes=[Rpa[pi_], Rost[ob]])
                P.op("sp", lambda e, ob=ob, s=s: e.dma_start(out=ydst[s * 128:s * 128 + RW, :], in_=ost[0:RW, ob, :]), reads=[Rost[ob]], slot=Sost[ob])

        for t in range(NT):
            tile(t, 512, 0)
            if t == 0:
                for r_ in Rs_in + Rs_out + Rs_up + Rs_dn:
                    r_.ro = True
        tile(None, 64, 1)

        P.final_slots = Sost + Sout
        P.emit()
        build.last = (P.est_ns, getattr(P, "n_tbl", 0), len(P.ops))
    return nc


_NC_CACHE = {}


def _pcol(v):
    v = np.asarray(v, np.float32).reshape(-1, 128)
    return np.ascontiguousarray(v.T)


def _host_layout(inputs, NT):
    f = lambda k: np.asarray(inputs[k], np.float32)
    conv_w = f("conv_w")[0]
    vecs = np.concatenate([
        _pcol(conv_w[0]), _pcol(conv_w[1]), _pcol(conv_w[2]), _pcol(conv_w[3]),
        _pcol(f("conv_b")[0]), _pcol(f("b_rg")[0].reshape(-1)), _pcol(f("b_ig")[0].reshape(-1)), _pcol(f("lru_lambda")[0]),
        _pcol(f("ln1_g")[0]), _pcol(f("ln1_b")[0]), _pcol(f("ln2_g")[0]), _pcol(f("ln2_b")[0]),
        _pcol(f("b_down")[0]), _pcol(f("b_up")[0]), _pcol(f("b_ada")[0])], axis=1)
    assert vecs.shape == (128, NV)
    table = f("rel_bias")[0]
    kr = np.arange(128)[:, None]; qc = np.arange(128)[None, :]
    qch = qc // 64
    biasM = np.empty((128, 8, 3, 128), np.float32)
    for i, b in enumerate((0, 3, 4)):
        kch = 2 * b - 8 + kr // 64
        kpos = (2 * b - 8) * 64 + kr
        rel = np.clip(qc - kpos, -128, 128) + 128
        vis = (kch <= qch) & (kch >= qch - 8)
        for h in range(8):
            biasM[:, h, i, :] = np.where(vis, table[h][rel], np.float32(NEG))
    cbias = np.ascontiguousarray(np.broadcast_to(table[:, 256][None, :], (128, 8))).astype(np.float32)
    shared = {
        "vecs": vecs, "biasM": np.ascontiguousarray(biasM.reshape(128, -1)), "cbias": cbias,
        "w_ada": np.ascontiguousarray(f("w_ada")[0]), "w_in": np.ascontiguousarray(f("w_in")[0]),
        "w_rg": np.ascontiguousarray(f("w_rg")[0]), "w_ig": np.ascontiguousarray(f("w_ig")[0]),
        "w_out": np.ascontiguousarray(f("w_out")[0]), "w_up": np.ascontiguousarray(f("w_up")[0]),
        "w_down": np.ascontiguousarray(f("w_down")[0]),
    }
    in_maps = []
    nb = f("x_prompt").shape[0]
    for b in range(nb):
        sc = f("state_conv")[0, b]
        cvec = np.concatenate([_pcol(f("c_prompt")[b]), _pcol(f("c_sample")[b]), _pcol(f("state_lru")[0, b]),
                               _pcol(sc[0]), _pcol(sc[1]), _pcol(sc[2])], axis=1)
        m = dict(shared)
        m.update({
            "xp": np.ascontiguousarray(f("x_prompt")[b, :NT * 512]), "xs": np.ascontiguousarray(f("x_sample")[b]),
            "ck": np.ascontiguousarray(f("cache_k")[0, b].reshape(512, 1024)),
            "cv": np.ascontiguousarray(f("cache_v")[0, b].reshape(512, 1024)),
            "cvec": np.ascontiguousarray(cvec),
        })
        in_maps.append(m)
    return in_maps


def _unp(a, n):
    return np.ascontiguousarray(a.T).reshape(-1)


def run(inputs, NT, cores=None):
    if NT not in _NC_CACHE:
        _NC_CACHE[NT] = build(NT)
    nc = _NC_CACHE[NT]
    in_maps = _host_layout(inputs, NT)
    if cores is not None:
        in_maps = [in_maps[c] for c in cores]
    res = run_bass_kernel_spmd(nc, in_maps, core_ids=list(range(len(in_maps))))
    R = res.results
    B = len(R)
    st = lambda k: np.stack([np.asarray(r[k], np.float32) for r in R])
    yp = st("yp"); ys = st("ys")
    nkp = st("nkp").reshape(1, B, 512, 8, 128); nvp = st("nvp").reshape(1, B, 512, 8, 128)
    nks = st("nks").reshape(1, B, 64, 8, 128); nvs = st("nvs").reshape(1, B, 64, 8, 128)

    def conv(k):
        a = st(k).reshape(B, 128, 8, 3)
        return np.ascontiguousarray(a.transpose(0, 3, 2, 1)).reshape(1, B, 3, 1024)

    def lru(k):
        a = st(k)
        return np.ascontiguousarray(a.transpose(0, 2, 1)).reshape(1, B, 1024)
    return (yp, ys, nkp, nvp, conv("ncp"), lru("nhp"), nks, nvs, conv("ncs"), lru("nhs"))


def kernel(**inputs):
    return run(inputs, 16)
```
